# Optimizing a Trainium2 kernel written in Bass

```python
import jax, jax.numpy as jnp
from jax import lax
import numpy as np

D_MODEL = 1024
BATCH = 8
SEQ = 4096
DEPTH = 2

N_META = 16
D_FF = 2816
EPS = 1e-6
POOL_WINDOWS = (2, 4, 8, 16)
POOL_GROUP = D_MODEL // 16
D_POOL = POOL_GROUP * len(POOL_WINDOWS)
HG_HEAD_K = 128
HG_HEAD_V = 128
D_HGRN = D_MODEL - D_POOL
HG_HEADS = D_HGRN // HG_HEAD_K
HG_CHUNK = 64
D_IN_EVEN = D_POOL + 4 * D_HGRN
D_CONV = D_MODEL // 2
CONV_WIDTH = 31
CONV_GROUPS = 4
D_LRU = D_MODEL // 2
LRU_HEADS = 8
LRU_HEAD = D_LRU // LRU_HEADS
LRU_CONV = 4
LRU_C = 8.0
D_IN_ODD = 2 * D_CONV + 2 * D_LRU
N_EVEN = (DEPTH + 1) // 2
N_ODD = DEPTH // 2

kernel_name = "hybrid_pool_hgrn2_conv_rglru_macaron"


def rms_norm(x, g):
    xf = x.astype(jnp.float32)
    y = xf * lax.rsqrt(jnp.mean(xf * xf, axis=-1, keepdims=True) + EPS)
    return (y * g.astype(jnp.float32)).astype(x.dtype)


def swiglu_ffn(x, wg, wu, wd):
    return (jax.nn.silu(x @ wg) * (x @ wu)) @ wd


def causal_depthwise_conv(x, w, b):
    width = w.shape[0]
    xp = jnp.pad(x, ((0, 0), (width - 1, 0), (0, 0)))
    y = lax.conv_general_dilated(xp, w.astype(x.dtype)[:, None, :], window_strides=(1,), padding='VALID',
                                 dimension_numbers=('NWC', 'WIO', 'NWC'), feature_group_count=x.shape[-1])
    return y + b.astype(x.dtype)


def multiscale_pool(u, w_grp, scale):
    bn, L, _ = u.shape
    ug = u.astype(jnp.float32).reshape(bn, L, len(POOL_WINDOWS), POOL_GROUP)
    c = jnp.cumsum(ug, axis=1)
    t = jnp.arange(1, L + 1, dtype=jnp.float32)[None, :, None]
    pooled = []
    for gi, w in enumerate(POOL_WINDOWS):
        cg = c[:, :, gi]
        lagged = jnp.pad(cg, ((0, 0), (w, 0), (0, 0)))[:, :L]
        pooled.append((cg - lagged) / jnp.minimum(t, float(w)))
    mixed = jnp.stack(pooled, axis=2) - ug
    y = jnp.einsum('blgc,gcd->blgd', mixed, w_grp.astype(jnp.float32))
    return (y.reshape(bn, L, D_POOL) * scale.astype(jnp.float32)).astype(u.dtype)


def hgrn2_mixer(q_raw, f_raw, i_raw, g_raw, lb, gnorm):
    f32 = jnp.float32
    bn, L, _ = q_raw.shape
    q = jax.nn.silu(q_raw.astype(f32))
    z = f_raw.astype(f32)
    lbf = lb.astype(f32)
    log_f = jnp.logaddexp(jnp.log(lbf), jnp.log1p(-lbf) + jax.nn.log_sigmoid(z))
    k = (1.0 - lbf) * jax.nn.sigmoid(-z)
    v = i_raw.astype(f32)
    pad = (-N_META) % HG_CHUNK
    n_chunks = (L + pad) // HG_CHUNK

    def to_chunks(a):
        a = jnp.pad(a, ((0, 0), (pad, 0), (0, 0)))
        return a.reshape(bn, n_chunks, HG_CHUNK, HG_HEADS, -1).transpose(1, 0, 3, 2, 4)

    causal = jnp.tril(jnp.ones((HG_CHUNK, HG_CHUNK), dtype=bool))[:, :, None]

    def step(S, inp):
        qc, kc, lfc, vc = inp
        b = jnp.cumsum(lfc, axis=2)
        o_inter = jnp.einsum('bhtk,bhkv->bhtv', qc * jnp.exp(b), S)
        diff = b[:, :, :, None, :] - b[:, :, None, :, :]
        decay = jnp.exp(jnp.where(causal, diff, -jnp.inf))
        A = jnp.einsum('bhtsk,bhsk->bhts', qc[:, :, :, None, :] * decay, kc)
        o = o_inter + jnp.einsum('bhts,bhsv->bhtv', A, vc)
        b_last = b[:, :, -1:, :]
        S_new = jnp.exp(b_last[:, :, 0, :])[..., None] * S + jnp.einsum('bhsk,bhsv->bhkv', kc * jnp.exp(b_last - b), vc)
        return S_new, o

    S0 = jnp.zeros((bn, HG_HEADS, HG_HEAD_K, HG_HEAD_V), f32)
    _, o = lax.scan(step, S0, (to_chunks(q), to_chunks(k), to_chunks(log_f), to_chunks(v)))
    o = o.transpose(1, 0, 3, 2, 4).reshape(bn, n_chunks * HG_CHUNK, HG_HEADS, HG_HEAD_V)[:, pad:]
    o = o * lax.rsqrt(jnp.mean(o * o, axis=-1, keepdims=True) + EPS) * gnorm.astype(f32)
    o = o * jax.nn.silu(g_raw.astype(f32)).reshape(bn, L, HG_HEADS, HG_HEAD_V)
    return o.reshape(bn, L, D_HGRN).astype(q_raw.dtype)


def conformer_conv_module(a, b, w, bias, ln_g, ln_b):
    f32 = jnp.float32
    bn, L, _ = a.shape
    u = a * jax.nn.sigmoid(b)
    u = causal_depthwise_conv(u, w, bias).astype(f32).reshape(bn, L, CONV_GROUPS, D_CONV // CONV_GROUPS)
    mu = jnp.mean(u, axis=-1, keepdims=True)
    var = jnp.mean(jnp.square(u - mu), axis=-1, keepdims=True)
    un = ((u - mu) * lax.rsqrt(var + EPS)).reshape(bn, L, D_CONV) * ln_g.astype(f32) + ln_b.astype(f32)
    return jax.nn.silu(un).astype(a.dtype)


def rglru_block(xb, gate, conv_w, conv_b, wa, ba, wx, bx, lam):
    f32 = jnp.float32
    u = causal_depthwise_conv(xb, conv_w, conv_b).astype(f32)
    bn, L, _ = u.shape
    uh = u.reshape(bn, L, LRU_HEADS, LRU_HEAD)
    r = jax.nn.sigmoid(jnp.einsum('blhi,hij->blhj', uh, wa.astype(f32)).reshape(bn, L, D_LRU) + ba.astype(f32))
    i = jax.nn.sigmoid(jnp.einsum('blhi,hij->blhj', uh, wx.astype(f32)).reshape(bn, L, D_LRU) + bx.astype(f32))
    log_a = -LRU_C * r * jax.nn.softplus(-lam.astype(f32))
    a = jnp.exp(log_a)
    mult = jnp.sqrt(-jnp.expm1(2.0 * log_a))
    reset = (jnp.arange(L) == 0)[None, :, None]
    bterm = jnp.where(reset, 1.0, mult) * (i * u)

    def combine(c1, c2):
        a1, b1 = c1
        a2, b2 = c2
        return a1 * a2, a2 * b1 + b2

    _, h = lax.associative_scan(combine, (a, bterm), axis=1)
    return (jax.nn.gelu(gate.astype(f32)) * h).astype(xb.dtype)


def setup_inputs(seed: int = 0) -> dict:
    key = jax.random.key(seed)
    ks = iter(jax.random.split(key, 48))
    f32 = jnp.float32

    def nrm(shape, scale):
        return jax.random.normal(next(ks), shape, f32) * scale

    def gain(shape):
        return 1.0 + 0.05 * jax.random.normal(next(ks), shape, f32)

    a0 = jax.random.uniform(next(ks), (N_ODD, D_LRU), f32, 0.9, 0.999)
    s = a0 ** (1.0 / LRU_C)
    lam = jnp.log(s) - jnp.log1p(-s)
    return {
        "x": nrm((BATCH, SEQ, D_MODEL), 1.0),
        "meta_tokens": nrm((N_META, D_MODEL), 1.0),
        "ffn1_norm": gain((DEPTH, D_MODEL)),
        "ffn1_wg": nrm((DEPTH, D_MODEL, D_FF), D_MODEL ** -0.5),
        "ffn1_wu": nrm((DEPTH, D_MODEL, D_FF), D_MODEL ** -0.5),
        "ffn1_wd": nrm((DEPTH, D_FF, D_MODEL), D_FF ** -0.5),
        "mix_norm": gain((DEPTH, D_MODEL)),
        "ffn2_norm": gain((DEPTH, D_MODEL)),
        "ffn2_wg": nrm((DEPTH, D_MODEL, D_FF), D_MODEL ** -0.5),
        "ffn2_wu": nrm((DEPTH, D_MODEL, D_FF), D_MODEL ** -0.5),
        "ffn2_wd": nrm((DEPTH, D_FF, D_MODEL), D_FF ** -0.5),
        "w_in_even": nrm((N_EVEN, D_MODEL, D_IN_EVEN), D_MODEL ** -0.5),
        "pool_w": nrm((N_EVEN, len(POOL_WINDOWS), POOL_GROUP, POOL_GROUP), POOL_GROUP ** -0.5),
        "pool_scale": gain((N_EVEN, D_POOL)),
        "hgrn_lb_logits": nrm((N_EVEN + 1, D_HGRN), 0.5),
        "hgrn_gnorm": gain((N_EVEN, HG_HEAD_V)),
        "w_out_even": nrm((N_EVEN, D_POOL + D_HGRN, D_MODEL), (D_POOL + D_HGRN) ** -0.5),
        "w_in_odd": nrm((N_ODD, D_MODEL, D_IN_ODD), D_MODEL ** -0.5),
        "conv_w": nrm((N_ODD, CONV_WIDTH, D_CONV), CONV_WIDTH ** -0.5),
        "conv_b": nrm((N_ODD, D_CONV), 0.02),
        "conv_ln_g": gain((N_ODD, D_CONV)),
        "conv_ln_b": nrm((N_ODD, D_CONV), 0.02),
        "lru_conv_w": nrm((N_ODD, LRU_CONV, D_LRU), LRU_CONV ** -0.5),
        "lru_conv_b": nrm((N_ODD, D_LRU), 0.02),
        "lru_wa": nrm((N_ODD, LRU_HEADS, LRU_HEAD, LRU_HEAD), LRU_HEAD ** -0.5),
        "lru_ba": nrm((N_ODD, D_LRU), 0.02),
        "lru_wx": nrm((N_ODD, LRU_HEADS, LRU_HEAD, LRU_HEAD), LRU_HEAD ** -0.5),
        "lru_bx": nrm((N_ODD, D_LRU), 0.02),
        "lru_lambda": lam,
        "w_out_odd": nrm((N_ODD, D_CONV + D_LRU, D_MODEL), (D_CONV + D_LRU) ** -0.5),
        "final_norm": gain((D_MODEL,)),
    }


def reference(x, meta_tokens, ffn1_norm, ffn1_wg, ffn1_wu, ffn1_wd, mix_norm, ffn2_norm, ffn2_wg, ffn2_wu,
              ffn2_wd, w_in_even, pool_w, pool_scale, hgrn_lb_logits, hgrn_gnorm, w_out_even, w_in_odd,
              conv_w, conv_b, conv_ln_g, conv_ln_b, lru_conv_w, lru_conv_b, lru_wa, lru_ba, lru_wx, lru_bx,
              lru_lambda, w_out_odd, final_norm):
    bn = x.shape[0]
    meta = jnp.broadcast_to(meta_tokens.astype(x.dtype)[None], (bn, N_META, D_MODEL))
    h = jnp.concatenate([meta, x], axis=1)
    lbs = jnp.cumsum(jax.nn.softmax(hgrn_lb_logits.astype(jnp.float32), axis=0), axis=0)
    for l in range(DEPTH):
        j = l // 2
        h = h + 0.5 * swiglu_ffn(rms_norm(h, ffn1_norm[l]), ffn1_wg[l], ffn1_wu[l], ffn1_wd[l])
        u = rms_norm(h, mix_norm[l])
        if l % 2 == 0:
            p = u @ w_in_even[j]
            p_pool, q_r, f_r, i_r, g_r = jnp.split(
                p, [D_POOL, D_POOL + D_HGRN, D_POOL + 2 * D_HGRN, D_POOL + 3 * D_HGRN], axis=-1)
            ya = multiscale_pool(p_pool, pool_w[j], pool_scale[j])
            yb = hgrn2_mixer(q_r, f_r, i_r, g_r, lbs[j], hgrn_gnorm[j])
            y = jnp.concatenate([ya, yb], axis=-1) @ w_out_even[j]
        else:
            p = u @ w_in_odd[j]
            c_a, c_b, d_x, d_g = jnp.split(p, [D_CONV, 2 * D_CONV, 2 * D_CONV + D_LRU], axis=-1)
            yc = conformer_conv_module(c_a, c_b, conv_w[j], conv_b[j], conv_ln_g[j], conv_ln_b[j])
            yd = rglru_block(d_x, d_g, lru_conv_w[j], lru_conv_b[j], lru_wa[j], lru_ba[j], lru_wx[j],
                             lru_bx[j], lru_lambda[j])
            y = jnp.concatenate([yc, yd], axis=-1) @ w_out_odd[j]
        h = h + y
        h = h + 0.5 * swiglu_ffn(rms_norm(h, ffn2_norm[l]), ffn2_wg[l], ffn2_wu[l], ffn2_wd[l])
    h = rms_norm(h, final_norm)
    return h[:, N_META:]
```

```python
import numpy as np
import concourse.bass as bass
import concourse.mybir as mybir
from concourse.bass_utils import run_bass_kernel_spmd

F32 = mybir.dt.float32
BF16 = mybir.dt.bfloat16
AF = mybir.ActivationFunctionType
ALU = mybir.AluOpType

D = 1024
KC = 8
SEQ = 4096
NM = 16
T = SEQ + NM
DFF = 2816
NFF = 22
EPS = 1e-6
NCORES = 8

ENGS = ("pe", "dve", "act", "pool", "sp")


class Prog:
    def __init__(self, nc):
        self.nc = nc
        self.ops = {e: [] for e in ENGS}
        self.last_write = {}
        self.reads_since = {}
        self.dma_sem_val = {}
        self.barrier_evs = []

    def op(self, eng, fn, reads=(), writes=(), dma=None, n_dma=1):
        deps = []
        for r in reads:
            ev = self.last_write.get(r)
            if ev is not None:
                deps.append((ev, "raw"))
        for w in writes:
            ev = self.last_write.get(w)
            if ev is not None:
                deps.append((ev, "waw"))
            for ev in self.reads_since.get(w, ()):
                deps.append((ev, "war"))
        for ev in self.barrier_evs:
            deps.append((ev, "bar"))
        idx = len(self.ops[eng])
        if dma is None:
            event = ("eng", eng, idx)
        else:
            self.dma_sem_val[dma] = self.dma_sem_val.get(dma, 0) + 16 * n_dma
            event = ("dma", dma, self.dma_sem_val[dma])
        fdeps = []
        for ev, kind in deps:
            if ev[0] == "eng" and ev[1] == eng and kind == "bar":
                continue
            if ev[0] == "eng" and ev[1] == eng and dma is not None:
                continue
            fdeps.append(ev)
        self.ops[eng].append(dict(fn=fn, deps=fdeps, dma=dma))
        for r in reads:
            self.reads_since.setdefault(r, []).append(event)
        for w in writes:
            self.last_write[w] = event
            self.reads_since[w] = []
        return event

    def barrier(self):
        evs = []
        for e in ENGS:
            for i in range(len(self.ops[e]) - 1, -1, -1):
                if self.ops[e][i]["dma"] is None:
                    evs.append(("eng", e, i))
                    break
        for name, v in self.dma_sem_val.items():
            evs.append(("dma", name, v))
        self.barrier_evs = evs

    def emit(self, block, sems, final_waits=()):
        marked = {e: set() for e in ENGS}
        for e in ENGS:
            waited = {}
            for o in self.ops[e]:
                best = {}
                for ev in o["deps"]:
                    key = (ev[0], ev[1])
                    val = ev[2]
                    if waited.get(key, -1) >= val:
                        continue
                    if key not in best or best[key][2] < val:
                        best[key] = ev
                o["waits"] = list(best.values())
                for ev in o["waits"]:
                    waited[(ev[0], ev[1])] = ev[2]
                    if ev[0] == "eng":
                        marked[ev[1]].add(ev[2])
        fin = list(final_waits)
        for ev in fin:
            if ev[0] == "eng":
                marked[ev[1]].add(ev[2])
        semval = {}
        for e in ENGS:
            c = 0
            for i in range(len(self.ops[e])):
                if i in marked[e]:
                    c += 1
                    semval[(e, i)] = c

        def ev_wait(h, ev):
            if ev[0] == "eng":
                h.wait_ge(sems["eng:" + ev[1]], semval[(ev[1], ev[2])])
            else:
                h.wait_ge(sems[ev[1]], ev[2])

        def run(e, h):
            for i, o in enumerate(self.ops[e]):
                for ev in o["waits"]:
                    ev_wait(h, ev)
                if o["dma"] is not None:
                    s = sems[o["dma"]]
                    o["fn"](h, lambda ins: ins.then_inc(s, 16))
                else:
                    ins = o["fn"](h)
                    if i in marked[e]:
                        ins.then_inc(sems["eng:" + e], 1)
            if e == "sp":
                for ev in fin:
                    ev_wait(h, ev)

        @block.tensor
        def _(h):
            run("pe", h)

        @block.vector
        def _(h):
            run("dve", h)

        @block.scalar
        def _(h):
            run("act", h)

        @block.gpsimd
        def _(h):
            run("pool", h)

        @block.sync
        def _(h):
            run("sp", h)


class Arena:
    def __init__(self, nc):
        self.nc = nc
        self.base = (nc.sbuf_base + 31) // 32 * 32
        self.top = nc.sbuf_top
        self.cur = self.base
        self.n = 0

    def alloc(self, name, shape, dtype):
        esz = 4 if dtype == F32 else 2
        nbytes = int(np.prod(shape[1:])) * esz
        off = self.cur
        self.cur = (off + nbytes + 31) // 32 * 32
        assert self.cur <= self.top, f"SBUF overflow allocating {name}: {self.cur} > {self.top}"
        self.n += 1
        self.last_off = off
        return self.nc.alloc_sbuf_tensor_at(f"{name}_{self.n}", list(shape), dtype, offset=off)

    def alloc_at(self, name, shape, dtype, off):
        self.n += 1
        return self.nc.alloc_sbuf_tensor_at(f"{name}_{self.n}", list(shape), dtype, offset=off)

    def mark(self):
        return self.cur

    def release(self, m):
        self.cur = m


def _fm(v):
    v = np.asarray(v, np.float32)
    lead = v.shape[:-1]
    c = v.shape[-1] // 128
    a = v.reshape(lead + (c, 128))
    a = np.moveaxis(a, -1, 0)
    return np.ascontiguousarray(a)


SMALL_LAYOUT = {}


def _pack_small(inp):
    parts = []
    off = 0

    def add(name, arr):
        nonlocal off
        a = np.ascontiguousarray(arr, np.float32).reshape(128, -1)
        SMALL_LAYOUT[name] = (off, a.shape[1])
        parts.append(a)
        off += a.shape[1]

    add("ffn1_norm", _fm(inp["ffn1_norm"]))
    add("mix_norm", _fm(inp["mix_norm"]))
    add("ffn2_norm", _fm(inp["ffn2_norm"]))
    add("final_norm", _fm(inp["final_norm"]))
    add("pool_scale", _fm(inp["pool_scale"][0]))
    add("lb_logits", _fm(inp["hgrn_lb_logits"]))
    add("gnorm", _fm(inp["hgrn_gnorm"][0]))
    add("conv_w", _fm(inp["conv_w"][0]))
    add("conv_b", _fm(inp["conv_b"][0]))
    add("conv_ln_g", _fm(inp["conv_ln_g"][0]))
    add("conv_ln_b", _fm(inp["conv_ln_b"][0]))
    add("lru_conv_w", _fm(inp["lru_conv_w"][0]))
    add("lru_conv_b", _fm(inp["lru_conv_b"][0]))
    add("lru_ba", _fm(inp["lru_ba"][0]))
    add("lru_bx", _fm(inp["lru_bx"][0]))
    add("lru_lambda", _fm(inp["lru_lambda"][0]))
    return np.ascontiguousarray(np.concatenate(parts, axis=1))


def _pack_ffa(wg, wu):
    def r(w):
        a = w.reshape(8, 128, 11, 2, 128)
        return a.transpose(2, 1, 3, 0, 4)
    a = np.stack([r(wg), r(wu)], axis=3)
    return np.ascontiguousarray(a.reshape(11, 128, 4096))


def _pack_ffb(wd):
    a = wd.reshape(22, 128, 4, 2, 128)
    a = a.transpose(2, 1, 3, 0, 4)
    return np.ascontiguousarray(a.reshape(4, 128, 5632))


def _pack_win(w):
    n = w.shape[1] // 128
    a = w.reshape(8, 128, n, 128).transpose(2, 1, 0, 3)
    return np.ascontiguousarray(a.reshape(n, 128, 1024))


def _pack_wtm(w):
    n = w.shape[1]
    a = w.reshape(8, 128, n).transpose(1, 0, 2)
    return np.ascontiguousarray(a.reshape(128, 8 * n))


def _pack_wout(w):
    a = w.reshape(8, 128, 4, 256).transpose(2, 1, 0, 3)
    return np.ascontiguousarray(a.reshape(4, 128, 2048))


BIG_SPECS = []


def _big_specs():
    specs = []
    for l in range(2):
        for f in (1, 2):
            specs.append((f"ffa{l}{f}", [11, 128, 4096], 2048))
            specs.append((f"ffb{l}{f}", [4, 128, 5632], 1408))
    specs.append(("win_e", [26, 128, 1024], 1024))
    specs.append(("wv_e", [128, 6144], 2048))
    specs.append(("wout_e", [4, 128, 2048], 2048))
    specs.append(("win_o", [16, 128, 1024], 1024))
    specs.append(("wout_o", [4, 128, 2048], 2048))
    return specs


def _host_pack(inp):
    big = {}
    for l in range(2):
        big[f"ffa{l}1"] = _pack_ffa(inp["ffn1_wg"][l], inp["ffn1_wu"][l])
        big[f"ffb{l}1"] = _pack_ffb(inp["ffn1_wd"][l])
        big[f"ffa{l}2"] = _pack_ffa(inp["ffn2_wg"][l], inp["ffn2_wu"][l])
        big[f"ffb{l}2"] = _pack_ffb(inp["ffn2_wd"][l])
    we = inp["w_in_even"][0]
    big["win_e"] = _pack_win(we)
    big["wv_e"] = _pack_wtm(we[:, 256 + 2 * 768: 256 + 3 * 768])
    big["wout_e"] = _pack_wout(inp["w_out_even"][0])
    big["win_o"] = _pack_win(inp["w_in_odd"][0])
    big["wout_o"] = _pack_wout(inp["w_out_odd"][0])
    shared = dict(big)
    shared["small"] = _pack_small(inp)
    shared["meta"] = _fm(inp["meta_tokens"]).transpose(0, 2, 1).copy()
    shared["pool_w"] = np.ascontiguousarray(inp["pool_w"][0], np.float32)
    shared["lru_wa"] = np.ascontiguousarray(inp["lru_wa"][0], np.float32)
    shared["lru_wx"] = np.ascontiguousarray(inp["lru_wx"][0], np.float32)
    return shared


def hkeys(s0, n):
    ks = []
    if s0 < NM:
        ks.append("Hm")
    lo = max(s0, NM) - NM
    hi = s0 + n - NM
    if hi > lo:
        for b in range(lo // 256, (hi - 1) // 256 + 1):
            ks.append(f"H{b}")
    return ks


class Builder:
    def __init__(self, cfg):
        self.cfg = cfg
        nc = self.nc = bass.Bass("TRN2", target_bir_lowering=False)
        self.P = Prog(nc)
        self.ar = Arena(nc)
        self.sems = {}
        self.dram = {}
        self.cast_chunks = {}
        self.sm_n = sum(v[1] for v in SMALL_LAYOUT.values())

    def sem(self, name):
        if name not in self.sems:
            self.sems[name] = self.nc.alloc_semaphore(name.replace(":", "_"))
        return self.sems[name]

    def dma(self, eng, out, in_, reads, writes, sem, **kw):
        self.sem(sem)
        return self.P.op(eng, lambda h, inc: inc(h.dma_start(out=out, in_=in_, **kw)),
                         reads=reads, writes=writes, dma=sem)

    def sm(self, name):
        off, n = SMALL_LAYOUT[name]
        return self.small[:, off:off + n]

    def declare(self):
        nc = self.nc
        dr = self.dram
        dr["xin"] = nc.dram_tensor("xin", [128, 8, SEQ], F32, kind="ExternalInput").ap()
        dr["meta"] = nc.dram_tensor("meta", [128, 8, NM], F32, kind="ExternalInput").ap()
        dr["small"] = nc.dram_tensor("small", [128, self.sm_n], F32, kind="ExternalInput").ap()
        dr["pool_w"] = nc.dram_tensor("pool_w", [4, 64, 64], F32, kind="ExternalInput").ap()
        dr["lru_wa"] = nc.dram_tensor("lru_wa", [8, 64, 64], F32, kind="ExternalInput").ap()
        dr["lru_wx"] = nc.dram_tensor("lru_wx", [8, 64, 64], F32, kind="ExternalInput").ap()
        for name, shape, _ in _big_specs():
            dr[name] = nc.dram_tensor(name, shape, F32, kind="ExternalInput").ap()
            dr[name + "_b"] = nc.dram_tensor(name + "_b", shape, BF16, kind="Internal").ap()
        dr["diag_b"] = nc.dram_tensor("diag_b", [4, 128, 31 * 128], BF16, kind="Internal").ap()
        dr["out"] = nc.dram_tensor("out", [128, 8, SEQ], F32, kind="ExternalOutput").ap()
        for e in ENGS:
            self.sem("eng:" + e)

    def cast_plan(self, name, fine_first=False):
        spec = {n: (s, b) for n, s, b in _big_specs()}[name]
        shape, b = spec
        src = self.dram[name]
        dst = self.dram[name + "_b"]
        if len(shape) == 3:
            pat = "g p (a b) -> (g p a) b"
        else:
            pat = "p (a b) -> (p a) b"
        s2 = src.rearrange(pat, b=b)
        d2 = dst.rearrange(pat, b=b)
        rows = s2.shape[0]
        r = 0
        chunks = []
        thunks = []
        while r < rows:
            step = 256 if (fine_first and r < 1024) else 512
            n = min(step, rows - r)
            key = f"{name}_b:{len(chunks)}"

            def issue(pace, r=r, n=n, key=key):
                self.dma("pool", d2[r:r + n, :], s2[r:r + n, :], reads=[name] + list(pace),
                         writes=[key, name + "_b:ser"], sem="cast_" + name)
            thunks.append(issue)
            chunks.append((r, r + n, key))
            r += n
        self.cast_chunks[name] = chunks
        return thunks

    def issue_casts(self, n, pace=()):
        for _ in range(min(n, len(self.pending_casts))):
            self.pending_casts.pop(0)(pace)

    def wkeys(self, name, lo=None, hi=None):
        ch = self.cast_chunks[name]
        if lo is None:
            return [k for _, _, k in ch]
        return [k for a, b, k in ch if a < hi and b > lo]

    def alloc_fixed(self):
        ar = self.ar
        self.H = ar.alloc("H", [128, 8, T], F32)
        self.small = ar.alloc("small", [128, self.sm_n], F32)
        self.ones_bf = ar.alloc("ones_bf", [128, 128], BF16)
        self.eps_t = ar.alloc("eps_t", [128, 1], F32)
        self.ps = [self.nc.alloc_psum_tensor(f"ps{i}", [128, 512], F32) for i in range(8)]

    def setup(self):
        P = self.P
        dr = self.dram
        self.dma("sp", self.small[:], dr["small"], reads=[], writes=["small"], sem="d_small")
        self.dma("sp", self.H[:, :, 0:NM], dr["meta"], reads=[], writes=["Hm"], sem="d_in")
        self.pending_x = list(range(8))
        self.issue_x(1)
        P.op("dve", lambda h: h.memset(self.ones_bf[:], 1.0), writes=["ones_bf"])
        P.op("dve", lambda h: h.memset(self.eps_t[:], EPS), writes=["eps_t"])

    def plan_diag(self):
        base = (self.ar.top - 3072) // 32 * 32
        stages = [self.ar.alloc_at(f"dstage{i}", [128, 6, 128], BF16, base + 1536 * i) for i in range(2)]
        cw = self.sm("conv_w")
        diag = self.dram["diag_b"]
        thunks = []
        k = 0
        for cc in range(4):
            for j0 in range(0, 31, 6):
                nj = min(6, 31 - j0)
                b = k % 2
                k += 1

                def issue(cc=cc, j0=j0, nj=nj, b=b):
                    st = stages[b]
                    for jj in range(nj):
                        j = j0 + jj
                        self.P.op("dve", lambda h, jj=jj, j=j, cc=cc, st=st: h.tensor_scalar(
                            out=st[:, jj, :], in0=self.ident_bf[:], scalar1=cw[:, j * 4 + cc:j * 4 + cc + 1], scalar2=None,
                            op0=ALU.mult), reads=["ident_bf", "small"], writes=[f"dstage{b}"])
                    self.dma("sp", diag[cc][:, j0 * 128:(j0 + nj) * 128], st[:, 0:nj, :].rearrange("p j m -> p (j m)"),
                             reads=[f"dstage{b}"], writes=["diag_b"], sem=f"d_diag{b}")
                thunks.append(issue)
        self.pending_diag = thunks
        self.diag_prebuilt = True

    def issue_diag(self, n):
        for _ in range(min(n, len(getattr(self, "pending_diag", [])))):
            self.pending_diag.pop(0)()

    def issue_x(self, n):
        for _ in range(min(n, len(self.pending_x))):
            i = self.pending_x.pop(0)
            a = NM + 512 * i
            self.dma("sp", self.H[:, :, a:a + 512], self.dram["xin"][:, :, 512 * i:512 * i + 512], reads=[],
                     writes=hkeys(a, 512), sem=f"d_in{i}")

    def wstream(self, loads, slots, slot_keys, sem_names):
        st = dict(loads=loads, issued=0, consumed=0, slots=slots, keys=slot_keys, sems=sem_names)
        return st

    def wacquire(self, st):
        k = st["consumed"]
        ns = len(st["slots"])
        while st["issued"] < min(len(st["loads"]), k + ns):
            i = st["issued"]
            s = i % ns
            dst, src, skey = st["loads"][i](st["slots"][s])
            if isinstance(skey, list):
                skeys = skey
            elif skey[:-2] in self.cast_chunks:
                skeys = self.wkeys(skey[:-2])
            else:
                skeys = [skey]
            self.dma("sp", dst, src, reads=skeys, writes=[st["keys"][s]], sem=st["sems"][s])
            st["issued"] += 1
        st["consumed"] += 1
        return k % ns

    def rmsnorm(self, s0, n, loc, xn, xn_key, sq, sq_key, rstd, rstd_key, psn, psn_key, gain):
        P = self.P
        H = self.H
        hk = hkeys(s0, n)
        P.op("act", lambda h: h.activation(out=sq[:, :, loc:loc + n], in_=H[:, :, s0:s0 + n], func=AF.Square),
             reads=hk, writes=[sq_key])

        def mm(h):
            ins = None
            for kc in range(8):
                ins = h.matmul(psn[:, 0:n], lhsT=self.ones_bf[:], rhs=sq[:, kc, loc:loc + n],
                               start=(kc == 0), stop=(kc == 7))
            return ins
        P.op("pe", mm, reads=[sq_key, "ones_bf"], writes=[psn_key])
        P.op("act", lambda h: h.activation(out=rstd[:, 0:n], in_=psn[:, 0:n], func=AF.Ln,
                                           bias=self.eps_t[:], scale=1.0 / D),
             reads=[psn_key, "eps_t"], writes=[rstd_key])
        P.op("act", lambda h: h.activation(out=rstd[:, 0:n], in_=rstd[:, 0:n], func=AF.Exp, scale=-0.5),
             reads=[rstd_key], writes=[rstd_key])
        for kc in range(8):
            P.op("dve", lambda h, kc=kc: h.scalar_tensor_tensor(
                out=xn[:, kc, loc:loc + n], in0=H[:, kc, s0:s0 + n], scalar=gain[:, kc:kc + 1],
                in1=rstd[:, 0:n], op0=ALU.mult, op1=ALU.mult),
                reads=hk + [rstd_key, "small"], writes=[xn_key])

    def ffn(self, l, f, final=False):
        P = self.P
        ar = self.ar
        H = self.H
        m = ar.mark()
        WT = 528
        xn = ar.alloc("xn", [128, 8, WT], BF16)
        aT = ar.alloc("aT", [128, NFF, WT], BF16)
        rstd = ar.alloc("rstd", [128, 512], F32)
        stt = [ar.alloc(f"st{i}", [128, 512], F32) for i in range(2)]
        wsl = [ar.alloc(f"wsl{i}", [128, 4096], BF16) for i in range(3)]
        sqv = ar.alloc("sq", [128, 8, WT], BF16)
        gain = self.sm(f"ffn{f}_norm")[:, l * 8:(l + 1) * 8]
        tiles = [[(0, 16), (16, 512)]] + [[(528 + 512 * i, 512)] for i in range(7)]
        wa = self.dram[f"ffa{l}{f}_b"]
        wb = self.dram[f"ffb{l}{f}_b"]
        wbv = [wb[dcp].rearrange("p (j x) -> p j x", j=2) for dcp in range(4)]
        loads = []
        for ti in range(len(tiles)):
            for g in range(11):
                loads.append(lambda slot, g=g: (slot[:, 0:4096], wa[g], self.wkeys(f"ffa{l}{f}", 256 * g, 256 * g + 256)))
            for dc in range(8):
                loads.append(lambda slot, dc=dc: (slot[:, 0:2816], wbv[dc // 2][:, dc % 2, :],
                                                  self.wkeys(f"ffb{l}{f}", 512 * (dc // 2), 512 * (dc // 2) + 512)))
        ws = self.wstream(loads, wsl, ["wsl0", "wsl1", "wsl2"], ["dw0", "dw1", "dw2"])
        psH = [self.ps[0], self.ps[1]]
        psU = [self.ps[2], self.ps[3]]
        psY = [self.ps[4], self.ps[5]]
        psN = self.ps[6]
        cnt = 0
        tiles = tiles[:self.cfg.get("ffn_tiles", len(tiles))]
        stage = 3

        def norm_tile(subs):
            for (s0, n) in subs:
                self.rmsnorm(s0, n, s0 - subs[0][0], xn, "xn", sqv, "sq", rstd, "rstd", psN, "ps6", gain)

        norm_tile(tiles[0])
        per_tile = -(-len(self.pending_casts) // max(1, len(tiles) - 1))
        pending_final = None
        fin_gen = None
        for ti, subs in enumerate(tiles):
            t0 = subs[0][0]
            self.issue_x(1)
            if ti >= 1:
                prev = tiles[ti - 1]
                self.issue_casts(per_tile, pace=hkeys(prev[-1][0], prev[-1][1]))
            for g in range(11):
                if g in (0, 4, 8):
                    self.issue_diag(1)
                if g == 2 and pending_final is not None:
                    fin_gen = self.final_tile_gen(pending_final, sqv, "sq", rstd, psN)
                    pending_final = None
                s = self.wacquire(ws)
                wv = wsl[s][:, 0:4096].rearrange("p (j u k m) -> p j u k m", j=2, u=2, k=8)
                for j in range(2):
                    ffc = 2 * g + j
                    for (s0, n) in subs:
                        if fin_gen is not None:
                            try:
                                next(fin_gen)
                            except StopIteration:
                                fin_gen = None
                        loc = s0 - t0
                        b = cnt % 2
                        cnt += 1

                        def mmh(h, j=j, loc=loc, n=n, b=b, wv=wv):
                            ins = None
                            for kc in range(8):
                                ins = h.matmul(psH[b][:, 0:n], lhsT=wv[:, j, 0, kc, :], rhs=xn[:, kc, loc:loc + n],
                                               start=(kc == 0), stop=(kc == 7))
                            return ins

                        def mmu(h, j=j, loc=loc, n=n, b=b, wv=wv):
                            ins = None
                            for kc in range(8):
                                ins = h.matmul(psU[b][:, 0:n], lhsT=wv[:, j, 1, kc, :], rhs=xn[:, kc, loc:loc + n],
                                               start=(kc == 0), stop=(kc == 7))
                            return ins
                        P.op("pe", mmh, reads=[f"wsl{s}", "xn"], writes=[f"ps{b}"])
                        P.op("pe", mmu, reads=[f"wsl{s}", "xn"], writes=[f"ps{2 + b}"])
                        P.op("act", lambda h, b=b, n=n: h.activation(out=stt[b][:, 0:n], in_=psH[b][:, 0:n], func=AF.Silu),
                             reads=[f"ps{b}"], writes=[f"st{b}"])
                        P.op("dve", lambda h, b=b, n=n, ffc=ffc, loc=loc: h.tensor_tensor(
                            out=aT[:, ffc, loc:loc + n], in0=stt[b][:, 0:n], in1=psU[b][:, 0:n], op=ALU.mult),
                            reads=[f"st{b}", f"ps{2 + b}"], writes=["aT"])
            if fin_gen is not None:
                for _ in fin_gen:
                    pass
                fin_gen = None
            for dc in range(8):
                if dc == 4 and ti + 1 < len(tiles):
                    norm_tile(tiles[ti + 1])
                s = self.wacquire(ws)
                wv = wsl[s][:, 0:2816].rearrange("p (j f m) -> p j f m", j=1, f=NFF)
                for dj in range(1):
                    for (s0, n) in subs:
                        if stage < 3:
                            continue
                        loc = s0 - t0
                        b = cnt % 2
                        cnt += 1

                        def mmy(h, dj=dj, loc=loc, n=n, b=b, wv=wv):
                            ins = None
                            for ffc in range(NFF):
                                ins = h.matmul(psY[b][:, 0:n], lhsT=wv[:, dj, ffc, :], rhs=aT[:, ffc, loc:loc + n],
                                               start=(ffc == 0), stop=(ffc == NFF - 1))
                            return ins
                        P.op("pe", mmy, reads=[f"wsl{s}", "aT"], writes=[f"ps{4 + b}"])
                        hk = hkeys(s0, n)
                        P.op("dve", lambda h, b=b, n=n, dc=dc, s0=s0: h.scalar_tensor_tensor(
                            out=H[:, dc, s0:s0 + n], in0=psY[b][:, 0:n], scalar=0.5, in1=H[:, dc, s0:s0 + n],
                            op0=ALU.mult, op1=ALU.add),
                            reads=[f"ps{4 + b}"] + hk, writes=hk)
            if final:
                pending_final = subs
        if pending_final is not None:
            self.final_tile(pending_final, sqv, "sq", rstd, psN)
        self.issue_x(8)
        self.issue_diag(64)
        P.barrier()
        ar.release(m)

    def setup_even(self):
        P = self.P
        ar = self.ar
        nc = self.nc
        self.ident_bf = ar.alloc("ident_bf", [128, 128], BF16)
        self.cmask = ar.alloc("cmask", [64, 6, 64], BF16)
        self.rmask = ar.alloc("rmask", [128, 256], F32)
        self.lbt = ar.alloc("lbt", [128, 3, 6], F32)
        self.pool_bd = ar.alloc("pool_bd", [128, 2, 128], BF16)
        self.pfix = ar.alloc("pfix", [128, 2, 16], F32)
        self.pinv = ar.alloc("pinv", [128, 2], F32)
        m = ar.mark()
        idx = ar.alloc("idx", [128, 6, 64], F32)
        stg = ar.alloc("stg", [128, 2, 128], F32)
        ex = ar.alloc("ex", [128, 2, 6], F32)
        tp1 = ar.alloc("tp1", [128, 16], F32)
        idf = idx[:].rearrange("p h s -> p (h s)")
        P.op("pool", lambda h: h.iota(idf[:, 0:128], [[1, 128]], base=0, channel_multiplier=-1,
                                      allow_small_or_imprecise_dtypes=True), writes=["idx"])
        P.op("dve", lambda h: h.tensor_scalar(out=self.ident_bf[:], in0=idf[:, 0:128], scalar1=0.0, scalar2=None,
                                              op0=ALU.is_equal), reads=["idx"], writes=["ident_bf"])
        P.op("pool", lambda h: h.iota(idx[0:64, :, :], [[0, 6], [1, 64]], base=0, channel_multiplier=-1,
                                      allow_small_or_imprecise_dtypes=True), reads=["ident_bf"], writes=["idx"])
        P.op("dve", lambda h: h.tensor_scalar(out=self.cmask[:], in0=idx[0:64, :, :], scalar1=0.0, scalar2=None,
                                              op0=ALU.is_ge), reads=["idx"], writes=["cmask"])
        P.op("dve", lambda h: h.memset(self.rmask[:], 1.0), writes=["rmask"])
        P.op("dve", lambda h: h.memset(self.rmask[:].rearrange("p (c s) -> p c s", s=64)[:, :, 0:1], 0.0),
             writes=["rmask"])
        lg = self.sm("lb_logits").rearrange("p (r h) -> p r h", r=2)
        P.op("act", lambda h: h.activation(out=ex[:], in_=lg, func=AF.Exp), reads=["small"], writes=["ex"])
        P.op("dve", lambda h: h.tensor_tensor(out=self.lbt[:, 1, :], in0=ex[:, 0, :], in1=ex[:, 1, :], op=ALU.add),
             reads=["ex"], writes=["lbt"])
        P.op("dve", lambda h: h.reciprocal(out=self.lbt[:, 2, :], in_=self.lbt[:, 1, :]), reads=["lbt"], writes=["lbt"])
        P.op("dve", lambda h: h.tensor_tensor(out=self.lbt[:, 0, :], in0=ex[:, 0, :], in1=self.lbt[:, 2, :], op=ALU.mult),
             reads=["lbt", "ex"], writes=["lbt"])
        P.op("dve", lambda h: h.tensor_scalar(out=self.lbt[:, 1, :], in0=self.lbt[:, 0, :], scalar1=-1.0, scalar2=1.0,
                                              op0=ALU.mult, op1=ALU.add), reads=["lbt"], writes=["lbt"])
        P.op("dve", lambda h: h.tensor_scalar(out=self.lbt[:, 2, :], in0=self.lbt[:, 0, :], scalar1=1.0, scalar2=-1.0,
                                              op0=ALU.mult, op1=ALU.add), reads=["lbt"], writes=["lbt"])
        P.op("dve", lambda h: h.memset(stg[:], 0.0), writes=["stg"])
        for g in range(4):
            c, j = divmod(g, 2)
            self.dma("sp", stg[64 * j:64 * j + 64, c, 64 * j:64 * j + 64], self.dram["pool_w"][g],
                     reads=[], writes=["stg"], sem="d_small")
        P.op("dve", lambda h: h.tensor_copy(out=self.pool_bd[:], in_=stg[:]), reads=["stg"], writes=["pool_bd"])
        P.op("pool", lambda h: h.iota(tp1[:], [[1, 16]], base=1, channel_multiplier=0,
                                      allow_small_or_imprecise_dtypes=True), writes=["tp1"])
        for g, w in enumerate((2, 4, 8, 16)):
            c, j = divmod(g, 2)
            rng = slice(64 * j, 64 * j + 64)
            P.op("dve", lambda h, c=c, rng=rng, w=w: h.tensor_scalar(out=self.pfix[rng, c, :], in0=tp1[rng, :], scalar1=float(w),
                                                                    scalar2=None, op0=ALU.min), reads=["tp1"], writes=["pfix"])
            P.op("dve", lambda h, c=c, rng=rng: h.reciprocal(out=self.pfix[rng, c, :], in_=self.pfix[rng, c, :]),
                 reads=["pfix"], writes=["pfix"])
            P.op("dve", lambda h, c=c, rng=rng, w=w: h.tensor_scalar(out=self.pfix[rng, c, :], in0=self.pfix[rng, c, :], scalar1=float(w),
                                                                    scalar2=None, op0=ALU.mult), reads=["pfix"], writes=["pfix"])
            P.op("dve", lambda h, c=c, rng=rng, w=w: h.memset(self.pinv[rng, c:c + 1], 1.0 / w), writes=["pinv"])
        P.barrier()
        ar.release(m)

    def mix_even(self):
        P = self.P
        ar = self.ar
        H = self.H
        ps = self.ps
        m = ar.mark()
        WT = 256
        xn = ar.alloc("xn", [128, 8, WT], BF16)
        wsl = [ar.alloc(f"wsl{i}", [128, 3072], BF16) for i in range(2)]
        B1 = ar.alloc("B1", [128, 6, WT], F32)
        B2 = ar.alloc("B2", [128, 6, WT], F32)
        b2_off = ar.last_off
        B3 = ar.alloc("B3", [128, 6, WT], F32)
        qb = ar.alloc("qb", [128, 6, WT], BF16)
        kb = ar.alloc("kb", [128, 6, WT], BF16)
        kdT = ar.alloc("kdT", [128, 6, WT], BF16)
        sgl = ar.alloc("sgl", [128, 6, WT], BF16)
        kd_tm = ar.alloc("kd_tm", [64, 4, 768], BF16)
        kd_off = ar.last_off
        v_tm = ar.alloc("v_tm", [64, 4, 768], BF16)
        S = ar.alloc("S", [128, 768], F32)
        S_bf = ar.alloc("S_bf", [128, 768], BF16)
        A_sb = ar.alloc("A_sb", [64, 6, 64], BF16)
        dec = ar.alloc("dec", [128, 6, 4], F32)
        ycat = ar.alloc("ycat", [128, 8, WT], BF16)
        px = ar.alloc("px", [128, 2, 16 + WT], F32)
        Sa = ar.alloc_at("SaB2", [128, 2, 16 + WT], F32, b2_off)
        Sb = ar.alloc_at("SbB2", [128, 2, 16 + WT], F32, b2_off + 2176)
        mixed = kdT[:, 0:2, :]
        rstd = B3[:, 0, :]
        gain = self.sm("mix_norm")[:, 0:8]
        gn = self.sm("gnorm")
        pscale = self.sm("pool_scale")
        lb = self.lbt[:, 0, :]
        oml = self.lbt[:, 1, :]
        noml = self.lbt[:, 2, :]
        win = self.dram["win_e_b"]
        wv = self.dram["wv_e_b"].rearrange("p (k n) -> p k n", k=8)
        wo = self.dram["wout_e_b"]
        tiles = [(0, 16)] + [(16 + 256 * i, 256) for i in range(16)]
        tiles = tiles[:self.cfg.get("mix_tiles", len(tiles))]
        def ld_fm(oc0, n=3):
            return lambda slot: (slot[:, 0:1024 * n].rearrange("p (o x) -> p o x", o=n),
                                 win[oc0:oc0 + n].rearrange("o p x -> p o x"), "win_e_b")

        def ld_v(vp):
            return lambda slot: (slot[:, 0:3072].rearrange("p (k n) -> p k n", k=8), wv[:, :, 384 * vp:384 * vp + 384], "wv_e_b")

        def ld_o(q):
            return lambda slot: (slot[:, 0:2048], wo[q], "wout_e_b")

        loads = [ld_fm(8), ld_fm(11)]
        for ti_ in range(len(tiles)):
            loads += [ld_fm(2), ld_fm(5), ld_v(0), ld_v(1), ld_fm(20), ld_fm(23), ld_fm(0, 2)]
            if ti_ + 1 < len(tiles):
                loads += [ld_o(0), ld_o(1), ld_fm(8), ld_o(2), ld_fm(11), ld_o(3)]
            else:
                loads += [ld_o(0), ld_o(1), ld_o(2), ld_o(3)]
        ws = self.wstream(loads, wsl, ["wsl0", "wsl1"], ["dw0", "dw1"])
        P.op("pool", lambda h: h.memset(S[:], 0.0), writes=["S"])
        P.op("pool", lambda h: h.memset(S_bf[:], 0.0), writes=["S_bf"])
        P.op("pool", lambda h: h.memset(px[:, :, 0:16], 0.0), writes=["px"])
        first_chunk = True
        bankrot = 0
        sqv = B1[:].rearrange("p h w -> p (h w)").bitcast(BF16)[:, 0:8 * WT].rearrange("p (k w) -> p k w", k=8)

        def bc_last(ap2, n):
            return bass.AP(ap2.tensor, ap2.offset, [list(ap2.ap[0]), list(ap2.ap[1]), [0, n]])

        def fm_piece_w(W, nchunks, dest_fn, nbanks):
            nonlocal bankrot
            s = self.wacquire(ws)
            wvw = wsl[s][:, 0:nchunks * 1024].rearrange("p (o k m) -> p o k m", o=nchunks, k=8)
            for i in range(nchunks):
                bank = bankrot % nbanks
                bankrot += 1
                pt = ps[bank]

                def mm(h, i=i, pt=pt, wvw=wvw):
                    ins = None
                    for kc in range(8):
                        ins = h.matmul(pt[:, 0:W], lhsT=wvw[:, i, kc, :], rhs=xn[:, kc, 0:W],
                                       start=(kc == 0), stop=(kc == 7))
                    return ins
                P.op("pe", mm, reads=[f"wsl{s}", "xn"], writes=[f"ps{bank}"])
                dest_fn(i, pt[:, 0:W], f"ps{bank}")

        def za(ti):
            s0, W = tiles[ti]
            C = 16 if W == 16 else 64
            nch = W // C
            self.rmsnorm(s0, W, 0, xn, "xn", sqv, "B1", rstd, "B3", ps[6], "ps6", gain)
            yield
            for pc in range(2):
                def dz(i, pap, pk, pc=pc):
                    hh = 3 * pc + i
                    P.op("act", lambda h: h.activation(out=B1[:, hh, 0:W], in_=pap, func=AF.Sigmoid),
                         reads=[pk], writes=["B1"])
                fm_piece_w(W, 3, dz, 3)
                if pc == 0:
                    yield
            for hh in range(6):
                P.op("dve", lambda h, hh=hh: h.tensor_scalar(out=B2[:, hh, 0:W], in0=B1[:, hh, 0:W], scalar1=oml[:, hh:hh + 1],
                                                            scalar2=lb[:, hh:hh + 1], op0=ALU.mult, op1=ALU.add),
                     reads=["B1", "lbt"], writes=["B2"])
                P.op("dve", lambda h, hh=hh: h.tensor_scalar(out=B1[:, hh, 0:W], in0=B1[:, hh, 0:W], scalar1=noml[:, hh:hh + 1],
                                                            scalar2=oml[:, hh:hh + 1], op0=ALU.mult, op1=ALU.add),
                     reads=["B1", "lbt"], writes=["B1"])
            yield
            P.op("act", lambda h: h.activation(out=B2[:, :, 0:W], in_=B2[:, :, 0:W], func=AF.Ln), reads=["B2"], writes=["B2"])
            for hh in range(6):
                P.op("dve", lambda h, hh=hh: h.tensor_tensor_scan(out=B3[:, hh, 0:W], data0=self.rmask[:, 0:W], data1=B2[:, hh, 0:W],
                                                                 initial=0.0, op0=ALU.mult, op1=ALU.add),
                     reads=["B2", "rmask"], writes=["B3"])
            yield
            if nch == 1:
                b_last = B3[:, :, W - 1:W]
                dec_v = dec[:, :, 0:1]
            else:
                b_last = B3[:, :, 0:W].rearrange("p h (c s) -> p h c s", s=C)[:, :, :, C - 1]
                dec_v = dec[:, :, 0:nch]
            P.op("act", lambda h: h.activation(out=dec_v, in_=b_last, func=AF.Exp), reads=["B3"], writes=["dec"])
            P.op("act", lambda h: h.activation(out=B2[:, :, 0:W], in_=B3[:, :, 0:W], func=AF.Exp), reads=["B3"], writes=["B2"])
            P.op("act", lambda h: h.activation(out=B3[:, :, 0:W], in_=B3[:, :, 0:W], func=AF.Exp, scale=-1.0),
                 reads=["B3"], writes=["B3"])
            yield

        def out_gen(s0, W):
            hk = hkeys(s0, W)
            for q in range(4):
                s = self.wacquire(ws)
                wq = wsl[s][:, 0:2048].rearrange("p (c m) -> p c m", c=8)
                for dj in range(2):
                    dc = 2 * q + dj
                    bank = 3 + (dc % 3)

                    def mm(h, dj=dj, bank=bank, wq=wq):
                        ins = None
                        for cc in range(8):
                            ins = h.matmul(ps[bank][:, 0:W], lhsT=wq[:, cc, dj * 128:(dj + 1) * 128], rhs=ycat[:, cc, 0:W],
                                           start=(cc == 0), stop=(cc == 7))
                        return ins
                    P.op("pe", mm, reads=[f"wsl{s}", "ycat"], writes=[f"ps{bank}"])
                    P.op("dve", lambda h, dc=dc, bank=bank: h.tensor_tensor(out=H[:, dc, s0:s0 + W], in0=ps[bank][:, 0:W],
                                                                            in1=H[:, dc, s0:s0 + W], op=ALU.add),
                         reads=[f"ps{bank}"] + hk, writes=hk)
                yield

        def tile_body(ti, s0, W):
            nonlocal first_chunk, bankrot
            C = 16 if W == 16 else 64
            nch = W // C
            hk = hkeys(s0, W)
            if ti == 0:
                for _ in za(0):
                    pass

            def fm_piece(nchunks, dest_fn):
                fm_piece_w(W, nchunks, dest_fn, 6)

            P.op("dve", lambda h: h.tensor_tensor(out=kb[:, :, 0:W], in0=B1[:, :, 0:W], in1=B3[:, :, 0:W], op=ALU.mult),
                 reads=["B1", "B3"], writes=["kb"])
            if nch == 1:
                kd_out = kdT[:, :, 0:W]
                kd_in = kb[:, :, 0:W]
                dec_b = bc_last(dec[:, :, 0], W)
            else:
                kd_out = kdT[:, :, 0:W].rearrange("p h (c s) -> p (h c) s", s=C)
                kd_in = kb[:, :, 0:W].rearrange("p h (c s) -> p (h c) s", s=C)
                dec_b = bc_last(dec[:].rearrange("p h c -> p (h c)"), C)
            P.op("dve", lambda h: h.tensor_tensor(out=kd_out, in0=kd_in, in1=dec_b, op=ALU.mult),
                 reads=["kb", "dec"], writes=["kdT"])
            for pc in range(2):
                def dq(i, pap, pk, pc=pc):
                    hh = 3 * pc + i
                    P.op("act", lambda h: h.activation(out=B1[:, hh, 0:W], in_=pap, func=AF.Silu),
                         reads=[pk], writes=["B1"])
                fm_piece(3, dq)
            P.op("dve", lambda h: h.tensor_tensor(out=qb[:, :, 0:W], in0=B1[:, :, 0:W], in1=B2[:, :, 0:W], op=ALU.mult),
                 reads=["B1", "B2"], writes=["qb"])
            for vp in range(2):
                s = self.wacquire(ws)
                wvv = wsl[s][:, 0:3072].rearrange("p (k n) -> p k n", k=8)
                for c in range(nch):
                    bank = 6 + (c % 2)

                    def mmv(h, c=c, bank=bank, wvv=wvv):
                        ins = None
                        for kc in range(8):
                            ins = h.matmul(ps[bank][0:C, 0:384], lhsT=xn[:, kc, c * C:(c + 1) * C], rhs=wvv[:, kc, :],
                                           start=(kc == 0), stop=(kc == 7))
                        return ins
                    P.op("pe", mmv, reads=[f"wsl{s}", "xn"], writes=[f"ps{bank}"])
                    if c % 2 == 0:
                        P.op("act", lambda h, c=c, bank=bank, vp=vp: h.activation(out=v_tm[0:C, c, 384 * vp:384 * vp + 384],
                                                                                  in_=ps[bank][0:C, 0:384], func=AF.Copy),
                             reads=[f"ps{bank}"], writes=["v_tm"])
                    else:
                        P.op("dve", lambda h, c=c, bank=bank, vp=vp: h.tensor_copy(out=v_tm[0:C, c, 384 * vp:384 * vp + 384],
                                                                                   in_=ps[bank][0:C, 0:384]),
                             reads=[f"ps{bank}"], writes=["v_tm"])
            for pc in range(2):
                def dg(i, pap, pk, pc=pc):
                    hh = 3 * pc + i
                    P.op("act", lambda h: h.activation(out=sgl[:, hh, 0:W], in_=pap, func=AF.Silu),
                         reads=[pk], writes=["sgl"])
                fm_piece(3, dg)
            def dp(i, pap, pk):
                P.op("act", lambda h: h.activation(out=px[:, i, 16:16 + W], in_=pap, func=AF.Copy),
                     reads=[pk], writes=["px"])
            fm_piece(2, dp)
            psT = ps[5][:].bitcast(BF16)
            for c in range(nch):
                def tr(h, c=c):
                    ins = None
                    for hh in range(6):
                        ins = h.transpose(out=psT[0:C, hh * 128:(hh + 1) * 128], in_=kdT[:, hh, c * C:(c + 1) * C],
                                          identity=self.ident_bf[:])
                    return ins
                P.op("pe", tr, reads=["kdT", "ident_bf"], writes=["ps5"])
                P.op("act", lambda h, c=c: h.activation(out=kd_tm[0:C, c, :], in_=psT[0:C, 0:768], func=AF.Copy),
                     reads=["ps5"], writes=["kd_tm"])
            n16 = 16 + W
            P.op("pool", lambda h: h.tensor_tensor(out=Sa[:, :, 1:n16], in0=px[:, :, 1:n16], in1=px[:, :, 0:n16 - 1], op=ALU.add),
                 reads=["px"], writes=["B2"])
            P.op("pool", lambda h: h.tensor_tensor(out=Sb[64:128, 0, 3:n16], in0=Sa[64:128, 0, 3:n16], in1=Sa[64:128, 0, 1:n16 - 2], op=ALU.add),
                 reads=["B2"], writes=["B2"])
            P.op("pool", lambda h: h.tensor_tensor(out=Sb[:, 1, 3:n16], in0=Sa[:, 1, 3:n16], in1=Sa[:, 1, 1:n16 - 2], op=ALU.add),
                 reads=["B2"], writes=["B2"])
            P.op("pool", lambda h: h.tensor_tensor(out=Sa[:, 1, 7:n16], in0=Sb[:, 1, 7:n16], in1=Sb[:, 1, 3:n16 - 4], op=ALU.add),
                 reads=["B2"], writes=["B2"])
            P.op("pool", lambda h: h.tensor_tensor(out=Sb[64:128, 1, 15:n16], in0=Sa[64:128, 1, 15:n16], in1=Sa[64:128, 1, 7:n16 - 8], op=ALU.add),
                 reads=["B2"], writes=["B2"])
            for g, (src, c, j) in enumerate(((Sa, 0, 0), (Sb, 0, 1), (Sa, 1, 0), (Sb, 1, 1))):
                rng = slice(64 * j, 64 * j + 64)
                if ti == 0:
                    P.op("dve", lambda h, src=src, c=c, rng=rng: h.tensor_tensor(out=src[rng, c, 16:16 + W], in0=src[rng, c, 16:16 + W],
                                                                               in1=self.pfix[rng, c, 0:W], op=ALU.mult),
                         reads=["B2", "B2", "pfix"], writes=["B2", "B2"])
                P.op("dve", lambda h, src=src, c=c, rng=rng: h.scalar_tensor_tensor(
                    out=mixed[rng, c, 0:W], in0=src[rng, c, 16:16 + W], scalar=self.pinv[rng, c:c + 1],
                    in1=px[rng, c, 16:16 + W], op0=ALU.mult, op1=ALU.subtract),
                    reads=["B2", "B2", "px", "pinv"], writes=["kdT"])
            P.op("pool", lambda h: h.tensor_copy(out=px[:, :, 0:16], in_=px[:, :, W:W + 16]), reads=["px"], writes=["px"])
            for c in range(2):
                P.op("pe", lambda h, c=c: h.matmul(ps[7][:, c * W:(c + 1) * W], lhsT=self.pool_bd[:, c, :], rhs=mixed[:, c, 0:W],
                                                   start=True, stop=True),
                     reads=["pool_bd", "kdT"], writes=["ps7"])
                P.op("dve", lambda h, c=c: h.tensor_scalar(out=ycat[:, c, 0:W], in0=ps[7][:, c * W:(c + 1) * W],
                                                          scalar1=pscale[:, c:c + 1], scalar2=None, op0=ALU.mult),
                     reads=["ps7", "small"], writes=["ycat"])
            for c in range(nch):
                cs = c * C
                sb = (3, 4) if c % 2 == 0 else (5, 6)

                def mma(h, cs=cs):
                    ins = None
                    for hh in range(6):
                        ins = h.matmul(ps[0][0:C, hh * C:(hh + 1) * C], lhsT=kb[:, hh, cs:cs + C], rhs=qb[:, hh, cs:cs + C],
                                       start=True, stop=True)
                    return ins
                P.op("pe", mma, reads=["kb", "qb"], writes=["ps0"])

                def mms(h, c=c, sb=sb):
                    ins = None
                    for hh in range(6):
                        bank, off = (sb[0], hh * 128) if hh < 4 else (sb[1], (hh - 4) * 128)
                        ins = h.matmul(ps[bank][:, off:off + 128], lhsT=kd_tm[0:C, c, hh * 128:(hh + 1) * 128],
                                       rhs=v_tm[0:C, c, hh * 128:(hh + 1) * 128], start=True, stop=True)
                    return ins
                P.op("pe", mms, reads=["kd_tm", "v_tm"], writes=[f"ps{sb[0]}", f"ps{sb[1]}"])
                P.op("dve", lambda h: h.tensor_tensor(out=A_sb[0:C, :, 0:C],
                                                      in0=ps[0][0:C, 0:6 * C].rearrange("p (h t) -> p h t", h=6),
                                                      in1=self.cmask[0:C, :, 0:C], op=ALU.mult),
                     reads=["ps0", "cmask"], writes=["A_sb"])
                ob = 1 + (c % 2)

                def mmo(h, cs=cs, c=c, ob=ob, fc=first_chunk):
                    ins = None
                    for hh in range(6):
                        o_ap = ps[ob][:, hh * C:(hh + 1) * C]
                        if not fc:
                            h.matmul(o_ap, lhsT=S_bf[:, hh * 128:(hh + 1) * 128], rhs=qb[:, hh, cs:cs + C],
                                     start=True, stop=False)
                        ins = h.matmul(o_ap, lhsT=v_tm[0:C, c, hh * 128:(hh + 1) * 128], rhs=A_sb[0:C, hh, 0:C],
                                       start=fc, stop=True)
                    return ins
                P.op("pe", mmo, reads=["S_bf", "qb", "v_tm", "A_sb"], writes=[f"ps{ob}"])
                P.op("act", lambda h, cs=cs, ob=ob: h.activation(out=B1[:, :, cs:cs + C],
                                                                 in_=ps[ob][:, 0:6 * C].rearrange("p (h t) -> p h t", h=6),
                                                                 func=AF.Copy),
                     reads=[f"ps{ob}"], writes=["B1"])
                for hh in range(6):
                    bank, off = (sb[0], hh * 128) if hh < 4 else (sb[1], (hh - 4) * 128)
                    P.op("dve", lambda h, hh=hh, bank=bank, off=off, c=c: h.scalar_tensor_tensor(
                        out=S[:, hh * 128:(hh + 1) * 128], in0=S[:, hh * 128:(hh + 1) * 128], scalar=dec[:, hh, c:c + 1],
                        in1=ps[bank][:, off:off + 128], op0=ALU.mult, op1=ALU.add),
                        reads=["S", "dec", f"ps{bank}"], writes=["S"])
                P.op("dve", lambda h: h.tensor_copy(out=S_bf[:], in_=S[:]), reads=["S"], writes=["S_bf"])
                first_chunk = False
            P.op("act", lambda h: h.activation(out=qb[:, :, 0:W], in_=B1[:, :, 0:W], func=AF.Square), reads=["B1"], writes=["qb"])
            for hh in range(6):
                bank, off = hh // 2, (hh % 2) * 256
                P.op("pe", lambda h, hh=hh, bank=bank, off=off: h.matmul(ps[bank][:, off:off + W], lhsT=self.ones_bf[:],
                                                                         rhs=qb[:, hh, 0:W], start=True, stop=True),
                     reads=["qb", "ones_bf"], writes=[f"ps{bank}"])
            for hp in range(3):
                msv = ps[hp][:, 0:512].rearrange("p (i w) -> p i w", i=2)[:, :, 0:W]
                P.op("act", lambda h, hp=hp, msv=msv: h.activation(out=B2[:, 2 * hp:2 * hp + 2, 0:W], in_=msv, func=AF.Ln,
                                                                   bias=self.eps_t[:], scale=1.0 / 128),
                     reads=[f"ps{hp}", "eps_t"], writes=["B2"])
            P.op("act", lambda h: h.activation(out=B2[:, :, 0:W], in_=B2[:, :, 0:W], func=AF.Exp, scale=-0.5),
                 reads=["B2"], writes=["B2"])
            P.op("dve", lambda h: h.scalar_tensor_tensor(out=B3[:, :, 0:W], in0=B1[:, :, 0:W], scalar=gn[:, 0:1],
                                                         in1=B2[:, :, 0:W], op0=ALU.mult, op1=ALU.mult),
                 reads=["B1", "B2", "small"], writes=["B3"])
            P.op("dve", lambda h: h.tensor_tensor(out=ycat[:, 2:8, 0:W], in0=B3[:, :, 0:W], in1=sgl[:, :, 0:W], op=ALU.mult),
                 reads=["B3", "sgl"], writes=["ycat"])
            g_za = za(ti + 1) if ti + 1 < len(tiles) else iter(())
            g_out = out_gen(s0, W)

            def step(g):
                try:
                    next(g)
                except StopIteration:
                    pass
            for g in (g_za, g_out, g_out, g_za, g_out, g_za, g_out, g_za, g_za, g_za, g_out):
                step(g)

        for ti, (s0, W) in enumerate(tiles):
            tile_body(ti, s0, W)
        P.barrier()
        ar.release(m)

    def mix_even_pipe(self):
        P = self.P
        ar = self.ar
        H = self.H
        ps = self.ps
        m = ar.mark()
        WT = 256
        xn = ar.alloc("xn", [128, 8, WT], BF16)
        wsl = [ar.alloc(f"wsl{i}", [128, 2048], BF16) for i in range(2)]
        wso = [ar.alloc(f"wso{i}", [128, 1024], BF16) for i in range(2)]
        B1 = ar.alloc("B1", [128, 6, WT], F32)
        b1_off = ar.last_off
        B2 = ar.alloc("B2", [128, 6, WT], F32)
        B3 = ar.alloc("B3", [128, 6, WT], F32)
        qb = ar.alloc("qb", [128, 6, WT], BF16)
        kb = ar.alloc("kb", [128, 6, WT], BF16)
        kdT = ar.alloc("kdT", [128, 6, WT], BF16)
        sgl = ar.alloc("sgl", [128, 6, WT], BF16)
        kd_tm = ar.alloc("kd_tm", [64, 4, 768], BF16)
        v_tm = ar.alloc("v_tm", [64, 4, 768], BF16)
        S = ar.alloc("S", [128, 768], F32)
        S_bf = ar.alloc("S_bf", [128, 768], BF16)
        A_sb = ar.alloc("A_sb", [64, 6, 64], BF16)
        dec2 = [ar.alloc(f"dec{i}", [128, 6, 4], F32) for i in range(2)]
        ycat = ar.alloc("ycat", [128, 8, WT], BF16)
        px = ar.alloc("px", [128, 2, 16 + WT], F32)
        Sa = ar.alloc_at("SaB1", [128, 2, 16 + WT], F32, b1_off)
        Sb = ar.alloc_at("SbB1", [128, 2, 16 + WT], F32, b1_off + 2176)
        mixed = kdT[:, 0:2, :]
        o_sb = kdT
        rstd = B3[:, 0, :]
        sqv = B1[:].rearrange("p h w -> p (h w)").bitcast(BF16)[:, 0:8 * WT].rearrange("p (k w) -> p k w", k=8)
        gain = self.sm("mix_norm")[:, 0:8]
        gn = self.sm("gnorm")
        pscale = self.sm("pool_scale")
        lb = self.lbt[:, 0, :]
        oml = self.lbt[:, 1, :]
        noml = self.lbt[:, 2, :]
        win = self.dram["win_e_b"]
        wv = self.dram["wv_e_b"].rearrange("p (k n) -> p k n", k=8)
        wo = self.dram["wout_e_b"]
        tiles = [(0, 16)] + [(16 + 256 * i, 256) for i in range(16)]
        tiles = tiles[:self.cfg.get("mix_tiles", len(tiles))]
        nt = len(tiles)
        loads = []
        for _ in tiles:
            for oc0 in (8, 10, 12, 2, 4, 6):
                loads.append(lambda slot, oc0=oc0: (slot[:, 0:2048].rearrange("p (o x) -> p o x", o=2),
                                                    win[oc0:oc0 + 2].rearrange("o p x -> p o x"), "win_e_b"))
            for vp in range(3):
                loads.append(lambda slot, vp=vp: (slot[:, 0:2048].rearrange("p (k n) -> p k n", k=8),
                                                  wv[:, :, 256 * vp:256 * vp + 256], "wv_e_b"))
            loads.append(lambda slot: (slot[:, 0:2048].rearrange("p (o x) -> p o x", o=2),
                                       win[0:2].rearrange("o p x -> p o x"), "win_e_b"))
            for oc0 in (20, 22, 24):
                loads.append(lambda slot, oc0=oc0: (slot[:, 0:2048].rearrange("p (o x) -> p o x", o=2),
                                                    win[oc0:oc0 + 2].rearrange("o p x -> p o x"), "win_e_b"))
        ws = self.wstream(loads, wsl, ["wsl0", "wsl1"], ["dw0", "dw1"])
        oloads = []
        for _ in tiles:
            for e in range(8):
                oloads.append(lambda slot, e=e: (slot[:, 0:1024].rearrange("p (c m) -> p c m", c=8),
                                                 wo[e // 2].rearrange("p (c m) -> p c m", c=8)[:, :, 128 * (e % 2):128 * (e % 2) + 128],
                                                 "wout_e_b"))
        wsO = self.wstream(oloads, wso, ["wso0", "wso1"], ["dwo0", "dwo1"])
        P.op("pool", lambda h: h.memset(S[:], 0.0), writes=["S"])
        P.op("pool", lambda h: h.memset(S_bf[:], 0.0), writes=["S_bf"])
        P.op("pool", lambda h: h.memset(px[:, :, 0:16], 0.0), writes=["px"])
        xb = [0]

        def bc_last(ap2, n):
            return bass.AP(ap2.tensor, ap2.offset, [list(ap2.ap[0]), list(ap2.ap[1]), [0, n]])

        def fm_piece(W, dest_fn):
            s = self.wacquire(ws)
            wvw = wsl[s][:, 0:2048].rearrange("p (o k m) -> p o k m", o=2, k=8)
            bank = 5 + xb[0] % 3
            xb[0] += 1
            for i in range(2):
                pap = ps[bank][:, i * 256:i * 256 + W]

                def mm(h, i=i, pap=pap, wvw=wvw):
                    ins = None
                    for kc in range(8):
                        ins = h.matmul(pap, lhsT=wvw[:, i, kc, :], rhs=xn[:, kc, 0:W], start=(kc == 0), stop=(kc == 7))
                    return ins
                P.op("pe", mm, reads=[f"wsl{s}", "xn"], writes=[f"ps{bank}"])
                dest_fn(i, pap, f"ps{bank}")

        def XA(ti):
            s0, W = tiles[ti]
            C = 16 if W == 16 else 64
            nch = W // C
            dec = dec2[ti % 2]
            dk = f"dec{ti % 2}"
            self.rmsnorm(s0, W, 0, xn, "xn", sqv, "B1", rstd, "B3", ps[7], "ps7", gain)
            yield
            for pc in range(3):
                def dz(i, pap, pk, pc=pc):
                    hh = 2 * pc + i
                    P.op("act", lambda h: h.activation(out=B1[:, hh, 0:W], in_=pap, func=AF.Sigmoid), reads=[pk], writes=["B1"])
                fm_piece(W, dz)
                yield
            for hh in range(6):
                P.op("dve", lambda h, hh=hh: h.tensor_scalar(out=B2[:, hh, 0:W], in0=B1[:, hh, 0:W], scalar1=oml[:, hh:hh + 1],
                                                            scalar2=lb[:, hh:hh + 1], op0=ALU.mult, op1=ALU.add),
                     reads=["B1", "lbt"], writes=["B2"])
                P.op("dve", lambda h, hh=hh: h.tensor_scalar(out=B1[:, hh, 0:W], in0=B1[:, hh, 0:W], scalar1=noml[:, hh:hh + 1],
                                                            scalar2=oml[:, hh:hh + 1], op0=ALU.mult, op1=ALU.add),
                     reads=["B1", "lbt"], writes=["B1"])
            yield
            P.op("act", lambda h: h.activation(out=B2[:, :, 0:W], in_=B2[:, :, 0:W], func=AF.Ln), reads=["B2"], writes=["B2"])
            for hh in range(6):
                P.op("dve", lambda h, hh=hh: h.tensor_tensor_scan(out=B3[:, hh, 0:W], data0=self.rmask[:, 0:W], data1=B2[:, hh, 0:W],
                                                                 initial=0.0, op0=ALU.mult, op1=ALU.add),
                     reads=["B2", "rmask"], writes=["B3"])
            yield
            if nch == 1:
                b_last = B3[:, :, W - 1:W]
                dec_v = dec[:, :, 0:1]
            else:
                b_last = B3[:, :, 0:W].rearrange("p h (c s) -> p h c s", s=C)[:, :, :, C - 1]
                dec_v = dec[:, :, 0:nch]
            P.op("act", lambda h: h.activation(out=dec_v, in_=b_last, func=AF.Exp), reads=["B3"], writes=[dk])
            P.op("act", lambda h: h.activation(out=B2[:, :, 0:W], in_=B3[:, :, 0:W], func=AF.Exp), reads=["B3"], writes=["B2"])
            P.op("act", lambda h: h.activation(out=B3[:, :, 0:W], in_=B3[:, :, 0:W], func=AF.Exp, scale=-1.0),
                 reads=["B3"], writes=["B3"])
            yield

        def XB(ti):
            s0, W = tiles[ti]
            C = 16 if W == 16 else 64
            nch = W // C
            dec = dec2[ti % 2]
            dk = f"dec{ti % 2}"
            P.op("dve", lambda h: h.tensor_tensor(out=kb[:, :, 0:W], in0=B1[:, :, 0:W], in1=B3[:, :, 0:W], op=ALU.mult),
                 reads=["B1", "B3"], writes=["kb"])
            for pc in range(3):
                def dq(i, pap, pk, pc=pc):
                    hh = 2 * pc + i
                    P.op("act", lambda h: h.activation(out=B1[:, hh, 0:W], in_=pap, func=AF.Silu), reads=[pk], writes=["B1"])
                fm_piece(W, dq)
                yield
            P.op("dve", lambda h: h.tensor_tensor(out=qb[:, :, 0:W], in0=B1[:, :, 0:W], in1=B2[:, :, 0:W], op=ALU.mult),
                 reads=["B1", "B2"], writes=["qb"])
            for vp in range(3):
                s = self.wacquire(ws)
                wvv = wsl[s][:, 0:2048].rearrange("p (k n) -> p k n", k=8)
                for c in range(nch):
                    bank = 5 + xb[0] % 3
                    xb[0] += 1

                    def mmv(h, c=c, bank=bank, wvv=wvv):
                        ins = None
                        for kc in range(8):
                            ins = h.matmul(ps[bank][0:C, 0:256], lhsT=xn[:, kc, c * C:(c + 1) * C], rhs=wvv[:, kc, :],
                                           start=(kc == 0), stop=(kc == 7))
                        return ins
                    P.op("pe", mmv, reads=[f"wsl{s}", "xn"], writes=[f"ps{bank}"])
                    eng = "act" if (c % 2 == 0) else "dve"
                    if eng == "act":
                        P.op("act", lambda h, c=c, bank=bank, vp=vp: h.activation(out=v_tm[0:C, c, 256 * vp:256 * vp + 256],
                                                                                  in_=ps[bank][0:C, 0:256], func=AF.Copy),
                             reads=[f"ps{bank}"], writes=["v_tm"])
                    else:
                        P.op("dve", lambda h, c=c, bank=bank, vp=vp: h.tensor_copy(out=v_tm[0:C, c, 256 * vp:256 * vp + 256],
                                                                                   in_=ps[bank][0:C, 0:256]),
                             reads=[f"ps{bank}"], writes=["v_tm"])
                yield
            if nch == 1:
                kd_out = kdT[:, :, 0:W]
                kd_in = kb[:, :, 0:W]
                dec_b = bc_last(dec[:, :, 0], W)
            else:
                kd_out = kdT[:, :, 0:W].rearrange("p h (c s) -> p (h c) s", s=C)
                kd_in = kb[:, :, 0:W].rearrange("p h (c s) -> p (h c) s", s=C)
                dec_b = bc_last(dec[:].rearrange("p h c -> p (h c)"), C)
            P.op("dve", lambda h: h.tensor_tensor(out=kd_out, in0=kd_in, in1=dec_b, op=ALU.mult),
                 reads=["kb", dk], writes=["kdT"])
            psT = ps[5][:].bitcast(BF16)
            for c in range(nch):
                def tr(h, c=c):
                    ins = None
                    for hh in range(6):
                        ins = h.transpose(out=psT[0:C, hh * 128:(hh + 1) * 128], in_=kdT[:, hh, c * C:(c + 1) * C],
                                          identity=self.ident_bf[:])
                    return ins
                P.op("pe", tr, reads=["kdT", "ident_bf"], writes=["ps5"])
                P.op("act", lambda h, c=c: h.activation(out=kd_tm[0:C, c, :], in_=psT[0:C, 0:768], func=AF.Copy),
                     reads=["ps5"], writes=["kd_tm"])
            yield
            def dp(i, pap, pk):
                P.op("act", lambda h: h.activation(out=px[:, i, 16:16 + W], in_=pap, func=AF.Copy), reads=[pk], writes=["px"])
            fm_piece(W, dp)
            n16 = 16 + W
            P.op("pool", lambda h: h.tensor_tensor(out=Sa[:, :, 1:n16], in0=px[:, :, 1:n16], in1=px[:, :, 0:n16 - 1], op=ALU.add),
                 reads=["px"], writes=["B1"])
            P.op("pool", lambda h: h.tensor_tensor(out=Sb[64:128, 0, 3:n16], in0=Sa[64:128, 0, 3:n16], in1=Sa[64:128, 0, 1:n16 - 2], op=ALU.add),
                 reads=["B1"], writes=["B1"])
            P.op("pool", lambda h: h.tensor_tensor(out=Sb[:, 1, 3:n16], in0=Sa[:, 1, 3:n16], in1=Sa[:, 1, 1:n16 - 2], op=ALU.add),
                 reads=["B1"], writes=["B1"])
            P.op("pool", lambda h: h.tensor_tensor(out=Sa[:, 1, 7:n16], in0=Sb[:, 1, 7:n16], in1=Sb[:, 1, 3:n16 - 4], op=ALU.add),
                 reads=["B1"], writes=["B1"])
            P.op("pool", lambda h: h.tensor_tensor(out=Sb[64:128, 1, 15:n16], in0=Sa[64:128, 1, 15:n16], in1=Sa[64:128, 1, 7:n16 - 8], op=ALU.add),
                 reads=["B1"], writes=["B1"])
            for g, (src, c, j) in enumerate(((Sa, 0, 0), (Sb, 0, 1), (Sa, 1, 0), (Sb, 1, 1))):
                rng = slice(64 * j, 64 * j + 64)
                if ti == 0:
                    P.op("dve", lambda h, src=src, c=c, rng=rng: h.tensor_tensor(out=src[rng, c, 16:16 + W], in0=src[rng, c, 16:16 + W],
                                                                               in1=self.pfix[rng, c, 0:W], op=ALU.mult),
                         reads=["B1", "pfix"], writes=["B1"])
                P.op("dve", lambda h, src=src, c=c, rng=rng: h.scalar_tensor_tensor(
                    out=mixed[rng, c, 0:W], in0=src[rng, c, 16:16 + W], scalar=self.pinv[rng, c:c + 1],
                    in1=px[rng, c, 16:16 + W], op0=ALU.mult, op1=ALU.subtract),
                    reads=["B1", "px", "pinv"], writes=["kdT"])
            P.op("pool", lambda h: h.tensor_copy(out=px[:, :, 0:16], in_=px[:, :, W:W + 16]), reads=["px"], writes=["px"])
            for c in range(2):
                P.op("pe", lambda h, c=c: h.matmul(ps[7][:, c * 256:c * 256 + W], lhsT=self.pool_bd[:, c, :], rhs=mixed[:, c, 0:W],
                                                   start=True, stop=True),
                     reads=["pool_bd", "kdT"], writes=["ps7"])
                P.op("dve", lambda h, c=c: h.tensor_scalar(out=ycat[:, c, 0:W], in0=ps[7][:, c * 256:c * 256 + W],
                                                          scalar1=pscale[:, c:c + 1], scalar2=None, op0=ALU.mult),
                     reads=["ps7", "small"], writes=["ycat"])
            yield
            for pc in range(3):
                def dg(i, pap, pk, pc=pc):
                    hh = 2 * pc + i
                    P.op("act", lambda h: h.activation(out=sgl[:, hh, 0:W], in_=pap, func=AF.Silu), reads=[pk], writes=["sgl"])
                fm_piece(W, dg)
                yield

        first = [True]

        def Y1(ti):
            s0, W = tiles[ti]
            C = 16 if W == 16 else 64
            nch = W // C
            dec = dec2[ti % 2]
            dk = f"dec{ti % 2}"
            for c in range(nch):
                cs = c * C

                def mma(h, cs=cs):
                    ins = None
                    for hh in range(6):
                        ins = h.matmul(ps[0][0:C, hh * C:(hh + 1) * C], lhsT=kb[:, hh, cs:cs + C], rhs=qb[:, hh, cs:cs + C],
                                       start=True, stop=True)
                    return ins
                P.op("pe", mma, reads=["kb", "qb"], writes=["ps0"])
                P.op("dve", lambda h: h.tensor_tensor(out=A_sb[0:C, :, 0:C],
                                                      in0=ps[0][0:C, 0:6 * C].rearrange("p (h t) -> p h t", h=6),
                                                      in1=self.cmask[0:C, :, 0:C], op=ALU.mult),
                     reads=["ps0", "cmask"], writes=["A_sb"])
                ob = 1 + (c % 2)
                fc = first[0]

                def mmo(h, cs=cs, c=c, ob=ob, fc=fc):
                    ins = None
                    for hh in range(6):
                        o_ap = ps[ob][:, hh * C:(hh + 1) * C]
                        if not fc:
                            h.matmul(o_ap, lhsT=S_bf[:, hh * 128:(hh + 1) * 128], rhs=qb[:, hh, cs:cs + C],
                                     start=True, stop=False)
                        ins = h.matmul(o_ap, lhsT=v_tm[0:C, c, hh * 128:(hh + 1) * 128], rhs=A_sb[0:C, hh, 0:C],
                                       start=fc, stop=True)
                    return ins
                P.op("pe", mmo, reads=["S_bf", "qb", "v_tm", "A_sb"], writes=[f"ps{ob}"])
                P.op("act", lambda h, cs=cs, ob=ob: h.activation(out=o_sb[:, :, cs:cs + C],
                                                                 in_=ps[ob][:, 0:6 * C].rearrange("p (h t) -> p h t", h=6),
                                                                 func=AF.Copy),
                     reads=[f"ps{ob}"], writes=["kdT"])

                def mms(h, c=c):
                    ins = None
                    for hh in range(6):
                        bank, off = (3, hh * 128) if hh < 4 else (4, (hh - 4) * 128)
                        ins = h.matmul(ps[bank][:, off:off + 128], lhsT=kd_tm[0:C, c, hh * 128:(hh + 1) * 128],
                                       rhs=v_tm[0:C, c, hh * 128:(hh + 1) * 128], start=True, stop=True)
                    return ins
                P.op("pe", mms, reads=["kd_tm", "v_tm"], writes=["ps3", "ps4"])
                for hh in range(6):
                    bank, off = (3, hh * 128) if hh < 4 else (4, (hh - 4) * 128)
                    P.op("dve", lambda h, hh=hh, bank=bank, off=off, c=c: h.scalar_tensor_tensor(
                        out=S[:, hh * 128:(hh + 1) * 128], in0=S[:, hh * 128:(hh + 1) * 128], scalar=dec[:, hh, c:c + 1],
                        in1=ps[bank][:, off:off + 128], op0=ALU.mult, op1=ALU.add),
                        reads=["S", dk, f"ps{bank}"], writes=["S"])
                P.op("dve", lambda h: h.tensor_copy(out=S_bf[:], in_=S[:]), reads=["S"], writes=["S_bf"])
                first[0] = False
                yield

        def Y2(ti):
            s0, W = tiles[ti]
            P.op("act", lambda h: h.activation(out=ycat[:, 2:8, 0:W], in_=o_sb[:, :, 0:W], func=AF.Square), reads=["kdT"], writes=["ycat"])
            for hp in range(3):
                for i in range(2):
                    hh = 2 * hp + i
                    P.op("pe", lambda h, hh=hh, hp=hp, i=i: h.matmul(ps[hp][:, i * 256:i * 256 + W], lhsT=self.ones_bf[:],
                                                                     rhs=ycat[:, 2 + hh, 0:W], start=True, stop=True),
                         reads=["ycat", "ones_bf"], writes=[f"ps{hp}"])
                rs = ps[hp][:, 0:512].rearrange("p (i w) -> p i w", i=2)[:, :, 0:W]
                P.op("act", lambda h, rs=rs: h.activation(out=rs, in_=rs, func=AF.Ln, bias=self.eps_t[:], scale=1.0 / 128),
                     reads=[f"ps{hp}", "eps_t"], writes=[f"ps{hp}"])
                P.op("act", lambda h, rs=rs: h.activation(out=rs, in_=rs, func=AF.Exp, scale=-0.5), reads=[f"ps{hp}"], writes=[f"ps{hp}"])
                P.op("dve", lambda h, rs=rs, hp=hp: h.scalar_tensor_tensor(out=o_sb[:, 2 * hp:2 * hp + 2, 0:W], in0=o_sb[:, 2 * hp:2 * hp + 2, 0:W],
                                                                          scalar=gn[:, 0:1], in1=rs, op0=ALU.mult, op1=ALU.mult),
                     reads=["kdT", f"ps{hp}", "small"], writes=["kdT"])
            P.op("dve", lambda h: h.tensor_tensor(out=ycat[:, 2:8, 0:W], in0=o_sb[:, :, 0:W], in1=sgl[:, :, 0:W], op=ALU.mult),
                 reads=["kdT", "sgl"], writes=["ycat"])
            yield
            hk = hkeys(s0, W)
            for dc in range(8):
                s = self.wacquire(wsO)
                wq = wso[s][:, 0:1024].rearrange("p (c m) -> p c m", c=8)
                bank = 3 + (dc % 2)

                def mm(h, bank=bank, wq=wq):
                    ins = None
                    for cc in range(8):
                        ins = h.matmul(ps[bank][:, 0:W], lhsT=wq[:, cc, :], rhs=ycat[:, cc, 0:W], start=(cc == 0), stop=(cc == 7))
                    return ins
                P.op("pe", mm, reads=[f"wso{s}", "ycat"], writes=[f"ps{bank}"])
                P.op("dve", lambda h, dc=dc, bank=bank: h.tensor_tensor(out=H[:, dc, s0:s0 + W], in0=ps[bank][:, 0:W],
                                                                        in1=H[:, dc, s0:s0 + W], op=ALU.add),
                     reads=[f"ps{bank}"] + hk, writes=hk)
                if dc % 2 == 1:
                    yield

        for _ in XA(0):
            pass
        for _ in XB(0):
            pass
        for ti in range(nt):
            self.interleave(Y1(ti), XA(ti + 1) if ti + 1 < nt else None)
            self.interleave(Y2(ti), XB(ti + 1) if ti + 1 < nt else None)
        P.barrier()
        ar.release(m)


    def out_proj(self, ws, wsl, ycat, s0, W):
        P = self.P
        H = self.H
        ps = self.ps
        hk = hkeys(s0, W)
        for q in range(4):
            s = self.wacquire(ws)
            wq = wsl[s][:, 0:2048].rearrange("p (c m) -> p c m", c=8)
            for dj in range(2):
                dc = 2 * q + dj
                bank = 3 + (dc % 3)

                def mm(h, dj=dj, bank=bank, wq=wq):
                    ins = None
                    for cc in range(8):
                        ins = h.matmul(ps[bank][:, 0:W], lhsT=wq[:, cc, dj * 128:(dj + 1) * 128], rhs=ycat[:, cc, 0:W],
                                       start=(cc == 0), stop=(cc == 7))
                    return ins
                P.op("pe", mm, reads=[f"wsl{s}", "ycat"], writes=[f"ps{bank}"])
                P.op("dve", lambda h, dc=dc, bank=bank: h.tensor_tensor(out=H[:, dc, s0:s0 + W], in0=ps[bank][:, 0:W],
                                                                        in1=H[:, dc, s0:s0 + W], op=ALU.add),
                     reads=[f"ps{bank}"] + hk, writes=hk)


    def setup_odd(self):
        P = self.P
        ar = self.ar
        self.onesf = ar.alloc("onesf", [128, 128], F32)
        self.one_t = ar.alloc("one_t", [128, 1], F32)
        self.wa_bd = ar.alloc("wa_bd", [128, 4, 128], BF16)
        self.wx_bd = ar.alloc("wx_bd", [128, 4, 128], BF16)
        self.clru = ar.alloc("clru", [128, 2, 4], F32)
        m = ar.mark()
        stg = ar.alloc("stg2", [128, 4, 128], F32)
        P.op("dve", lambda h: h.memset(self.onesf[:], 1.0 / 128), writes=["onesf"])
        P.op("dve", lambda h: h.memset(self.one_t[:], 1.0), writes=["one_t"])
        for name, dst in (("lru_wa", self.wa_bd), ("lru_wx", self.wx_bd)):
            P.op("dve", lambda h: h.memset(stg[:], 0.0), writes=["stg2"])
            for hd in range(8):
                cc, j = divmod(hd, 2)
                self.dma("sp", stg[64 * j:64 * j + 64, cc, 64 * j:64 * j + 64], self.dram[name][hd],
                         reads=[], writes=["stg2"], sem="d_small")
            P.op("dve", lambda h, dst=dst: h.tensor_copy(out=dst[:], in_=stg[:]), reads=["stg2"], writes=[name + "_bd"])
        lam = self.sm("lru_lambda")
        P.op("act", lambda h: h.activation(out=self.clru[:, 0, :], in_=lam, func=AF.Exp, scale=-1.0),
             reads=["small"], writes=["clru"])
        P.op("act", lambda h: h.activation(out=self.clru[:, 0, :], in_=self.clru[:, 0, :], func=AF.Ln, bias=self.one_t[:]),
             reads=["clru", "one_t"], writes=["clru"])
        P.op("dve", lambda h: h.tensor_scalar(out=self.clru[:, 1, :], in0=self.clru[:, 0, :], scalar1=-16.0, scalar2=None,
                                              op0=ALU.mult), reads=["clru"], writes=["clru"])
        P.op("dve", lambda h: h.tensor_scalar(out=self.clru[:, 0, :], in0=self.clru[:, 0, :], scalar1=-8.0, scalar2=None,
                                              op0=ALU.mult), reads=["clru"], writes=["clru"])
        P.barrier()
        ar.release(m)

    def mix_odd_v1(self):
        P = self.P
        ar = self.ar
        H = self.H
        ps = self.ps
        m = ar.mark()
        WT = 256
        xn = ar.alloc("xn", [128, 8, WT], BF16)
        wsl = [ar.alloc(f"wsl{i}", [128, 3072], BF16) for i in range(2)]
        dsl = [ar.alloc(f"dsl{i}", [128, 31, 128], BF16) for i in range(2)]
        ubuf = [ar.alloc(f"ubuf{i}", [128, 4, 30 + WT], BF16) for i in range(2)]
        xl = [ar.alloc(f"xl{i}", [128, 4, 3 + WT], F32) for i in range(2)]
        ulb = ar.alloc("ulb", [128, 4, WT], BF16)
        Ta = ar.alloc("Ta", [128, 4, WT], F32)
        Tb = ar.alloc("Tb", [128, 4, WT], F32)
        Tc = ar.alloc("Tc", [128, 4, WT], F32)
        Td = ar.alloc("Td", [128, 4, WT], F32)
        Te = ar.alloc("Te", [128, 4, WT], F32)
        ycat = ar.alloc("ycat", [128, 8, WT], BF16)
        hcar = ar.alloc("hcar", [128, 4], F32)
        rstd = Te[:, 0, :]
        sqv = Ta[:].rearrange("p h w -> p (h w)").bitcast(BF16)[:, 0:8 * WT].rearrange("p (k w) -> p k w", k=8)
        gain = self.sm("mix_norm")[:, 8:16]
        cw = self.sm("conv_w")
        cb = self.sm("conv_b")
        lng = self.sm("conv_ln_g")
        lnb = self.sm("conv_ln_b")
        lw = self.sm("lru_conv_w")
        lbias = self.sm("lru_conv_b")
        ba = self.sm("lru_ba")
        bx = self.sm("lru_bx")
        c1 = self.clru[:, 0, :]
        c2 = self.clru[:, 1, :]
        win = self.dram["win_o_b"]
        wo = self.dram["wout_o_b"]
        diag = self.dram["diag_b"]
        self.sem("d_diag")
        for cc in range(4):
            for j in range(31):
                P.op("dve", lambda h, cc=cc, j=j: h.tensor_scalar(out=dsl[0][:, j, :], in0=self.ident_bf[:],
                                                                 scalar1=cw[:, j * 4 + cc:j * 4 + cc + 1], scalar2=None, op0=ALU.mult),
                     reads=["ident_bf", "small"], writes=["dsl0"])
            self.dma("sp", diag[cc], dsl[0][:].rearrange("p j m -> p (j m)"), reads=["dsl0"], writes=["diag_b"], sem="d_diag")
        tiles = [(0, 16)] + [(16 + 256 * i, 256) for i in range(16)]
        tiles = tiles[:self.cfg.get("mix_tiles", len(tiles))]
        loads = []
        for _ in tiles:
            for oc0 in (4, 6, 0, 2, 8, 10, 12, 14):
                loads.append(lambda slot, oc0=oc0: (slot[:, 0:2048].rearrange("p (o x) -> p o x", o=2),
                                                    win[oc0:oc0 + 2].rearrange("o p x -> p o x"), "win_o_b"))
            for q in range(4):
                loads.append(lambda slot, q=q: (slot[:, 0:2048], wo[q], "wout_o_b"))
        ws = self.wstream(loads, wsl, ["wsl0", "wsl1"], ["dw0", "dw1"])
        dloads = []
        for _ in tiles:
            for cc in range(4):
                dloads.append(lambda slot, cc=cc: (slot[:].rearrange("p j m -> p (j m)"), diag[cc], "diag_b"))
        dstream = self.wstream(dloads, dsl, ["dsl0", "dsl1"], ["dd0", "dd1"])
        P.op("pool", lambda h: h.memset(ubuf[0][:, :, 0:30], 0.0), writes=["ubuf0"])
        P.op("pool", lambda h: h.memset(xl[0][:, :, 0:3], 0.0), writes=["xl0"])
        P.op("pool", lambda h: h.memset(hcar[:], 0.0), writes=["hcar"])
        bankrot = 0

        def tile_body(ti, s0, W):
            nonlocal bankrot
            ub, ubn = ubuf[ti % 2], ubuf[(ti + 1) % 2]
            ubk, ubnk = f"ubuf{ti % 2}", f"ubuf{(ti + 1) % 2}"
            xc, xcn = xl[ti % 2], xl[(ti + 1) % 2]
            xck, xcnk = f"xl{ti % 2}", f"xl{(ti + 1) % 2}"
            self.rmsnorm(s0, W, 0, xn, "xn", sqv, "Ta", rstd, "Te", ps[6], "ps6", gain)

            def fm_piece(dest_fn):
                nonlocal bankrot
                s = self.wacquire(ws)
                wvw = wsl[s][:, 0:2048].rearrange("p (o k m) -> p o k m", o=2, k=8)
                for i in range(2):
                    bank = bankrot % 4
                    bankrot += 1
                    pt = ps[bank]

                    def mm(h, i=i, pt=pt, wvw=wvw):
                        ins = None
                        for kc in range(8):
                            ins = h.matmul(pt[:, 0:W], lhsT=wvw[:, i, kc, :], rhs=xn[:, kc, 0:W],
                                           start=(kc == 0), stop=(kc == 7))
                        return ins
                    P.op("pe", mm, reads=[f"wsl{s}", "xn"], writes=[f"ps{bank}"])
                    dest_fn(i, pt[:, 0:W], f"ps{bank}")

            for pc in range(2):
                def db(i, pap, pk, pc=pc):
                    cc = 2 * pc + i
                    P.op("act", lambda h: h.activation(out=Ta[:, cc, 0:W], in_=pap, func=AF.Sigmoid), reads=[pk], writes=["Ta"])
                fm_piece(db)
            for pc in range(2):
                def da(i, pap, pk, pc=pc):
                    cc = 2 * pc + i
                    P.op("dve", lambda h: h.tensor_tensor(out=ub[:, cc, 30:30 + W], in0=Ta[:, cc, 0:W], in1=pap, op=ALU.mult),
                         reads=[pk, "Ta"], writes=[ubk])
                fm_piece(da)
            P.op("pool", lambda h: h.tensor_copy(out=ubn[:, :, 0:30], in_=ub[:, :, W:W + 30]), reads=[ubk], writes=[ubnk])
            for pc in range(2):
                def dx(i, pap, pk, pc=pc):
                    cc = 2 * pc + i
                    P.op("act", lambda h: h.activation(out=xc[:, cc, 3:3 + W], in_=pap, func=AF.Copy), reads=[pk], writes=[xck])
                fm_piece(dx)
            P.op("pool", lambda h: h.tensor_copy(out=xcn[:, :, 0:3], in_=xc[:, :, W:W + 3]), reads=[xck], writes=[xcnk])
            for pc in range(2):
                def dgf(i, pap, pk, pc=pc):
                    cc = 2 * pc + i
                    P.op("act", lambda h: h.activation(out=Tc[:, cc, 0:W], in_=pap, func=AF.Gelu), reads=[pk], writes=["Tc"])
                fm_piece(dgf)
            for cc in range(4):
                ds = self.wacquire(dstream)
                bank = 4 + (cc % 2)

                def mmc(h, cc=cc, ds=ds, bank=bank):
                    ins = None
                    for j in range(31):
                        ins = h.matmul(ps[bank][:, 0:W], lhsT=dsl[ds][:, j, :], rhs=ub[:, cc, j:j + W],
                                       start=(j == 0), stop=(j == 30))
                    return ins
                P.op("pe", mmc, reads=[f"dsl{ds}", ubk], writes=[f"ps{bank}"])
                P.op("act", lambda h, cc=cc, bank=bank: h.activation(out=Tb[:, cc, 0:W], in_=ps[bank][:, 0:W], func=AF.Identity,
                                                                     bias=cb[:, cc:cc + 1]),
                     reads=[f"ps{bank}", "small"], writes=["Tb"])
            for cc in range(4):
                bank = 6 + (cc % 2)
                P.op("pe", lambda h, cc=cc, bank=bank: h.matmul(ps[bank][:, 0:W], lhsT=self.onesf[:], rhs=Tb[:, cc, 0:W],
                                                                start=True, stop=True),
                     reads=["onesf", "Tb"], writes=[f"ps{bank}"])
                P.op("dve", lambda h, cc=cc, bank=bank: h.tensor_tensor(out=Tb[:, cc, 0:W], in0=Tb[:, cc, 0:W], in1=ps[bank][:, 0:W],
                                                                        op=ALU.subtract),
                     reads=["Tb", f"ps{bank}"], writes=["Tb"])
                P.op("act", lambda h, cc=cc: h.activation(out=Ta[:, cc, 0:W], in_=Tb[:, cc, 0:W], func=AF.Square),
                     reads=["Tb"], writes=["Ta"])
                P.op("pe", lambda h, cc=cc, bank=bank: h.matmul(ps[bank][:, 0:W], lhsT=self.onesf[:], rhs=Ta[:, cc, 0:W],
                                                                start=True, stop=True),
                     reads=["onesf", "Ta"], writes=[f"ps{bank}"])
                P.op("act", lambda h, cc=cc, bank=bank: h.activation(out=Ta[:, cc, 0:W], in_=ps[bank][:, 0:W], func=AF.Ln,
                                                                     bias=self.eps_t[:]),
                     reads=[f"ps{bank}", "eps_t"], writes=["Ta"])
            P.op("act", lambda h: h.activation(out=Ta[:, :, 0:W], in_=Ta[:, :, 0:W], func=AF.Exp, scale=-0.5), reads=["Ta"], writes=["Ta"])
            P.op("dve", lambda h: h.tensor_tensor(out=Tb[:, :, 0:W], in0=Tb[:, :, 0:W], in1=Ta[:, :, 0:W], op=ALU.mult),
                 reads=["Ta", "Tb"], writes=["Tb"])
            for cc in range(4):
                P.op("dve", lambda h, cc=cc: h.tensor_scalar(out=Tb[:, cc, 0:W], in0=Tb[:, cc, 0:W], scalar1=lng[:, cc:cc + 1],
                                                            scalar2=lnb[:, cc:cc + 1], op0=ALU.mult, op1=ALU.add),
                     reads=["Tb", "small"], writes=["Tb"])
            P.op("act", lambda h: h.activation(out=ycat[:, 0:4, 0:W], in_=Tb[:, :, 0:W], func=AF.Silu), reads=["Tb"], writes=["ycat"])
            for cc in range(4):
                P.op("dve", lambda h, cc=cc: h.tensor_scalar(out=Td[:, cc, 0:W], in0=xc[:, cc, 3:3 + W], scalar1=lw[:, 12 + cc:13 + cc],
                                                            scalar2=lbias[:, cc:cc + 1], op0=ALU.mult, op1=ALU.add),
                     reads=[xck, "small"], writes=["Td"])
                for j in range(3):
                    P.op("dve", lambda h, cc=cc, j=j: h.scalar_tensor_tensor(out=Td[:, cc, 0:W], in0=xc[:, cc, j:j + W],
                                                                            scalar=lw[:, 4 * j + cc:4 * j + cc + 1], in1=Td[:, cc, 0:W],
                                                                            op0=ALU.mult, op1=ALU.add),
                         reads=[xck, "small", "Td"], writes=["Td"])
            P.op("pool", lambda h: h.tensor_copy(out=ulb[:, :, 0:W], in_=Td[:, :, 0:W]), reads=["Td"], writes=["ulb"])
            for cc in range(4):
                bank = cc % 2
                P.op("pe", lambda h, cc=cc, bank=bank: h.matmul(ps[bank][:, 0:W], lhsT=self.wa_bd[:, cc, :], rhs=ulb[:, cc, 0:W],
                                                                start=True, stop=True),
                     reads=["lru_wa_bd", "ulb"], writes=[f"ps{bank}"])
                P.op("act", lambda h, cc=cc, bank=bank: h.activation(out=Ta[:, cc, 0:W], in_=ps[bank][:, 0:W], func=AF.Sigmoid,
                                                                     bias=ba[:, cc:cc + 1]),
                     reads=[f"ps{bank}", "small"], writes=["Ta"])
                P.op("pe", lambda h, cc=cc, bank=bank: h.matmul(ps[2 + bank][:, 0:W], lhsT=self.wx_bd[:, cc, :], rhs=ulb[:, cc, 0:W],
                                                                start=True, stop=True),
                     reads=["lru_wx_bd", "ulb"], writes=[f"ps{2 + bank}"])
                P.op("act", lambda h, cc=cc, bank=bank: h.activation(out=Tb[:, cc, 0:W], in_=ps[2 + bank][:, 0:W], func=AF.Sigmoid,
                                                                     bias=bx[:, cc:cc + 1]),
                     reads=[f"ps{2 + bank}", "small"], writes=["Tb"])
            for cc in range(4):
                P.op("act", lambda h, cc=cc: h.activation(out=Te[:, cc, 0:W], in_=Ta[:, cc, 0:W], func=AF.Exp, scale=c1[:, cc:cc + 1]),
                     reads=["Ta", "clru"], writes=["Te"])
                P.op("act", lambda h, cc=cc: h.activation(out=Ta[:, cc, 0:W], in_=Ta[:, cc, 0:W], func=AF.Exp, scale=c2[:, cc:cc + 1]),
                     reads=["Ta", "clru"], writes=["Ta"])
            P.op("act", lambda h: h.activation(out=Ta[:, :, 0:W], in_=Ta[:, :, 0:W], func=AF.Ln, bias=self.one_t[:], scale=-1.0),
                 reads=["Ta", "one_t"], writes=["Ta"])
            P.op("act", lambda h: h.activation(out=Ta[:, :, 0:W], in_=Ta[:, :, 0:W], func=AF.Exp, scale=0.5), reads=["Ta"], writes=["Ta"])
            if ti == 0:
                P.op("dve", lambda h: h.memset(Ta[:, :, 0:1], 1.0), writes=["Ta"])
            P.op("dve", lambda h: h.tensor_tensor(out=Tb[:, :, 0:W], in0=Tb[:, :, 0:W], in1=Td[:, :, 0:W], op=ALU.mult),
                 reads=["Tb", "Td"], writes=["Tb"])
            P.op("dve", lambda h: h.tensor_tensor(out=Tb[:, :, 0:W], in0=Tb[:, :, 0:W], in1=Ta[:, :, 0:W], op=ALU.mult),
                 reads=["Tb", "Ta"], writes=["Tb"])
            for cc in range(4):
                P.op("dve", lambda h, cc=cc: h.tensor_tensor_scan(out=Td[:, cc, 0:W], data0=Te[:, cc, 0:W], data1=Tb[:, cc, 0:W],
                                                                 initial=hcar[:, cc:cc + 1], op0=ALU.mult, op1=ALU.add),
                     reads=["Te", "Tb", "hcar"], writes=["Td"])
            P.op("dve", lambda h: h.tensor_copy(out=hcar[:], in_=Td[:, :, W - 1]), reads=["Td"], writes=["hcar"])
            P.op("dve", lambda h: h.tensor_tensor(out=ycat[:, 4:8, 0:W], in0=Tc[:, :, 0:W], in1=Td[:, :, 0:W], op=ALU.mult),
                 reads=["Tc", "Td"], writes=["ycat"])
            self.out_proj(ws, wsl, ycat, s0, W)

        for ti, (s0, W) in enumerate(tiles):
            tile_body(ti, s0, W)
        P.barrier()
        ar.release(m)

    @staticmethod
    def interleave(ga, gb):
        done_a = done_b = False
        while not (done_a and done_b):
            if not done_a:
                try:
                    next(ga)
                except StopIteration:
                    done_a = True
            if not done_b and gb is not None:
                try:
                    next(gb)
                except StopIteration:
                    done_b = True
            if gb is None:
                done_b = True

    def mix_odd(self):
        P = self.P
        ar = self.ar
        H = self.H
        ps = self.ps
        m = ar.mark()
        WT = 256
        xn = ar.alloc("xn", [128, 8, WT], BF16)
        wsl = [ar.alloc(f"wsl{i}", [128, 2048], BF16) for i in range(2)]
        wso = [ar.alloc(f"wso{i}", [128, 1024], BF16) for i in range(2)]
        dsl = [ar.alloc(f"dsl{i}", [128, 31, 128], BF16) for i in range(2)]
        ubuf = [ar.alloc(f"ubuf{i}", [128, 4, 30 + WT], BF16) for i in range(2)]
        xl = [ar.alloc(f"xl{i}", [128, 4, 3 + WT], F32) for i in range(2)]
        xt = ar.alloc("xt", [128, 2, WT], F32)
        rstdx = xt[:, 0, :]
        gel = [ar.alloc(f"gel{i}", [128, 4, WT], BF16) for i in range(2)]
        ulb = ar.alloc("ulb", [128, 4, WT], BF16)
        Ta = ar.alloc("Ta", [128, 4, WT], F32)
        Tb = ar.alloc("Tb", [128, 4, WT], F32)
        Td = ar.alloc("Td", [128, 4, WT], F32)
        Te = ar.alloc("Te", [128, 4, WT], F32)
        ycat = ar.alloc("ycat", [128, 8, WT], BF16)
        hcar = ar.alloc("hcar", [128, 4], F32)
        gain = self.sm("mix_norm")[:, 8:16]
        cw = self.sm("conv_w")
        cb = self.sm("conv_b")
        lng = self.sm("conv_ln_g")
        lnb = self.sm("conv_ln_b")
        lw = self.sm("lru_conv_w")
        lbias = self.sm("lru_conv_b")
        ba = self.sm("lru_ba")
        bx = self.sm("lru_bx")
        c1 = self.clru[:, 0, :]
        c2 = self.clru[:, 1, :]
        win = self.dram["win_o_b"]
        wo = self.dram["wout_o_b"]
        diag = self.dram["diag_b"]
        self.sem("d_diag")
        for cc in range(4 if not getattr(self, "diag_prebuilt", False) else 0):
            for j in range(31):
                P.op("dve", lambda h, cc=cc, j=j: h.tensor_scalar(out=dsl[0][:, j, :], in0=self.ident_bf[:],
                                                                 scalar1=cw[:, j * 4 + cc:j * 4 + cc + 1], scalar2=None, op0=ALU.mult),
                     reads=["ident_bf", "small"], writes=["dsl0"])
            self.dma("sp", diag[cc], dsl[0][:].rearrange("p j m -> p (j m)"), reads=["dsl0"], writes=["diag_b"], sem="d_diag")
        tiles = [(0, 16)] + [(16 + 256 * i, 256) for i in range(16)]
        tiles = tiles[:self.cfg.get("mix_tiles", len(tiles))]
        loads = []
        for _ in tiles:
            for oc0 in (8, 10, 4, 0, 6, 2, 12, 14):
                loads.append(lambda slot, oc0=oc0: (slot[:, 0:2048].rearrange("p (o x) -> p o x", o=2),
                                                    win[oc0:oc0 + 2].rearrange("o p x -> p o x"), "win_o_b"))
        ws = self.wstream(loads, wsl, ["wsl0", "wsl1"], ["dw0", "dw1"])
        oloads = []
        for _ in tiles:
            for e in range(8):
                oloads.append(lambda slot, e=e: (slot[:, 0:1024].rearrange("p (c m) -> p c m", c=8),
                                                 wo[e // 2].rearrange("p (c m) -> p c m", c=8)[:, :, 128 * (e % 2):128 * (e % 2) + 128],
                                                 "wout_o_b"))
        wsO = self.wstream(oloads, wso, ["wso0", "wso1"], ["dwo0", "dwo1"])
        dloads = []
        for _ in tiles:
            for cc in range(4):
                dloads.append(lambda slot, cc=cc: (slot[:].rearrange("p j m -> p (j m)"), diag[cc], "diag_b"))
        dstream = self.wstream(dloads, dsl, ["dsl0", "dsl1"], ["dd0", "dd1"])
        P.op("pool", lambda h: h.memset(ubuf[0][:, :, 0:30], 0.0), writes=["ubuf0"])
        P.op("pool", lambda h: h.memset(xl[0][:, :, 0:3], 0.0), writes=["xl0"])
        P.op("pool", lambda h: h.memset(hcar[:], 0.0), writes=["hcar"])
        ISQ2 = 0.7071067811865476
        xbank = [0]

        def X(ti):
            s0, W = tiles[ti]
            ub, ubn = ubuf[ti % 2], ubuf[(ti + 1) % 2]
            ubk, ubnk = f"ubuf{ti % 2}", f"ubuf{(ti + 1) % 2}"
            xc, xcn = xl[ti % 2], xl[(ti + 1) % 2]
            xck, xcnk = f"xl{ti % 2}", f"xl{(ti + 1) % 2}"
            gl, glk = gel[ti % 2], f"gel{ti % 2}"
            self.rmsnorm(s0, W, 0, xn, "xn", ycat, "ycat", rstdx, "xt", ps[1], "ps1", gain)
            yield

            def fm_piece(dest_fn):
                s = self.wacquire(ws)
                wvw = wsl[s][:, 0:2048].rearrange("p (o k m) -> p o k m", o=2, k=8)
                bank = xbank[0] % 2
                xbank[0] += 1
                for i in range(2):
                    pap = ps[bank][:, i * 256:i * 256 + W]

                    def mm(h, i=i, pap=pap, wvw=wvw):
                        ins = None
                        for kc in range(8):
                            ins = h.matmul(pap, lhsT=wvw[:, i, kc, :], rhs=xn[:, kc, 0:W], start=(kc == 0), stop=(kc == 7))
                        return ins
                    P.op("pe", mm, reads=[f"wsl{s}", "xn"], writes=[f"ps{bank}"])
                    dest_fn(i, pap, f"ps{bank}")

            for pc in range(2):
                def dx(i, pap, pk, pc=pc):
                    cc = 2 * pc + i
                    P.op("act", lambda h: h.activation(out=xc[:, cc, 3:3 + W], in_=pap, func=AF.Copy), reads=[pk], writes=[xck])
                fm_piece(dx)
                yield
            P.op("pool", lambda h: h.tensor_copy(out=xcn[:, :, 0:3], in_=xc[:, :, W:W + 3]), reads=[xck], writes=[xcnk])
            for pc in range(2):
                def db(i, pap, pk):
                    P.op("act", lambda h: h.activation(out=xt[:, i, 0:W], in_=pap, func=AF.Sigmoid), reads=[pk], writes=["xt"])
                fm_piece(db)
                yield

                def da(i, pap, pk, pc=pc):
                    cc = 2 * pc + i
                    P.op("dve", lambda h: h.tensor_tensor(out=ub[:, cc, 30:30 + W], in0=xt[:, i, 0:W], in1=pap, op=ALU.mult),
                         reads=[pk, "xt"], writes=[ubk])
                fm_piece(da)
                yield
            P.op("pool", lambda h: h.tensor_copy(out=ubn[:, :, 0:30], in_=ub[:, :, W:W + 30]), reads=[ubk], writes=[ubnk])
            for pc in range(2):
                def dgf(i, pap, pk, pc=pc):
                    cc = 2 * pc + i
                    P.op("act", lambda h: h.activation(out=gl[:, cc, 0:W], in_=pap, func=AF.Gelu), reads=[pk], writes=[glk])
                fm_piece(dgf)
                yield

        def Y(ti):
            s0, W = tiles[ti]
            ub, ubk = ubuf[ti % 2], f"ubuf{ti % 2}"
            xc, xck = xl[ti % 2], f"xl{ti % 2}"
            gl, glk = gel[ti % 2], f"gel{ti % 2}"
            for cc in range(4):
                ds = self.wacquire(dstream)
                bank = 2 + (cc % 2)

                def mmc(h, cc=cc, ds=ds, bank=bank):
                    ins = None
                    for j in range(31):
                        ins = h.matmul(ps[bank][:, 0:W], lhsT=dsl[ds][:, j, :], rhs=ub[:, cc, j:j + W],
                                       start=(j == 0), stop=(j == 30))
                    return ins
                P.op("pe", mmc, reads=[f"dsl{ds}", ubk], writes=[f"ps{bank}"])
                P.op("act", lambda h, cc=cc, bank=bank: h.activation(out=Tb[:, cc, 0:W], in_=ps[bank][:, 0:W], func=AF.Identity,
                                                                     bias=cb[:, cc:cc + 1]),
                     reads=[f"ps{bank}", "small"], writes=["Tb"])
                P.op("dve", lambda h, cc=cc: h.tensor_scalar(out=Td[:, cc, 0:W], in0=xc[:, cc, 3:3 + W], scalar1=lw[:, 12 + cc:13 + cc],
                                                            scalar2=lbias[:, cc:cc + 1], op0=ALU.mult, op1=ALU.add),
                     reads=[xck, "small"], writes=["Td"])
                for j in range(3):
                    P.op("dve", lambda h, cc=cc, j=j: h.scalar_tensor_tensor(out=Td[:, cc, 0:W], in0=xc[:, cc, j:j + W],
                                                                            scalar=lw[:, 4 * j + cc:4 * j + cc + 1], in1=Td[:, cc, 0:W],
                                                                            op0=ALU.mult, op1=ALU.add),
                         reads=[xck, "small", "Td"], writes=["Td"])
                if cc % 2 == 1:
                    yield
            P.op("act", lambda h: h.activation(out=ulb[:, :, 0:W], in_=Td[:, :, 0:W], func=AF.Copy), reads=["Td"], writes=["ulb"])
            lnb_ = [4, 5, 2, 3]
            for cc in range(4):
                bank = lnb_[cc]
                P.op("pe", lambda h, cc=cc, bank=bank: h.matmul(ps[bank][:, 0:W], lhsT=self.onesf[:], rhs=Tb[:, cc, 0:W],
                                                                start=True, stop=True),
                     reads=["onesf", "Tb"], writes=[f"ps{bank}"])
            for cc in range(4):
                bank = lnb_[cc]
                P.op("dve", lambda h, cc=cc, bank=bank: h.tensor_tensor(out=Tb[:, cc, 0:W], in0=Tb[:, cc, 0:W], in1=ps[bank][:, 0:W],
                                                                        op=ALU.subtract),
                     reads=["Tb", f"ps{bank}"], writes=[f"Tbc{cc}"])
            yield
            for cc in range(4):
                P.op("act", lambda h, cc=cc: h.activation(out=Ta[:, cc, 0:W], in_=Tb[:, cc, 0:W], func=AF.Square),
                     reads=[f"Tbc{cc}"], writes=[f"Tac{cc}"] + (["Ta"] if cc == 0 else []))
            for cc in range(4):
                bank = lnb_[cc]
                P.op("pe", lambda h, cc=cc, bank=bank: h.matmul(ps[bank][:, 0:W], lhsT=self.onesf[:], rhs=Ta[:, cc, 0:W],
                                                                start=True, stop=True),
                     reads=["onesf", f"Tac{cc}"], writes=[f"ps{bank}"])
            for cc in range(4):
                bank = lnb_[cc]
                P.op("act", lambda h, cc=cc, bank=bank: h.activation(out=Ta[:, cc, 0:W], in_=ps[bank][:, 0:W], func=AF.Ln,
                                                                     bias=self.eps_t[:]),
                     reads=[f"ps{bank}", "eps_t", f"Tac{cc}"], writes=["Ta"])
            P.op("act", lambda h: h.activation(out=Ta[:, :, 0:W], in_=Ta[:, :, 0:W], func=AF.Exp, scale=-0.5), reads=["Ta"], writes=["Ta"])
            yield
            P.op("dve", lambda h: h.tensor_tensor(out=Tb[:, :, 0:W], in0=Tb[:, :, 0:W], in1=Ta[:, :, 0:W], op=ALU.mult),
                 reads=["Ta", "Tb", "Tbc0", "Tbc1", "Tbc2", "Tbc3"], writes=["Tb", "Tbc0", "Tbc1", "Tbc2", "Tbc3"])
            for cc in range(4):
                P.op("dve", lambda h, cc=cc: h.tensor_scalar(out=Tb[:, cc, 0:W], in0=Tb[:, cc, 0:W], scalar1=lng[:, cc:cc + 1],
                                                            scalar2=lnb[:, cc:cc + 1], op0=ALU.mult, op1=ALU.add),
                     reads=["Tb", "small"], writes=["Tb"])
            P.op("act", lambda h: h.activation(out=Ta[:, :, 0:W], in_=Tb[:, :, 0:W], func=AF.Sigmoid), reads=["Tb"], writes=["Ta"])
            P.op("dve", lambda h: h.tensor_tensor(out=ycat[:, 0:4, 0:W], in0=Tb[:, :, 0:W], in1=Ta[:, :, 0:W], op=ALU.mult),
                 reads=["Ta", "Tb"], writes=["ycat"])
            yield
            for cc in range(4):
                off = (cc % 2) * 256
                P.op("pe", lambda h, cc=cc, off=off: h.matmul(ps[4][:, off:off + W], lhsT=self.wa_bd[:, cc, :], rhs=ulb[:, cc, 0:W],
                                                              start=True, stop=True),
                     reads=["lru_wa_bd", "ulb"], writes=["ps4"])
                P.op("act", lambda h, cc=cc, off=off: h.activation(out=Ta[:, cc, 0:W], in_=ps[4][:, off:off + W], func=AF.Sigmoid,
                                                                   bias=ba[:, cc:cc + 1]),
                     reads=["ps4", "small"], writes=["Ta"])
                P.op("pe", lambda h, cc=cc, off=off: h.matmul(ps[5][:, off:off + W], lhsT=self.wx_bd[:, cc, :], rhs=ulb[:, cc, 0:W],
                                                              start=True, stop=True),
                     reads=["lru_wx_bd", "ulb"], writes=["ps5"])
                P.op("act", lambda h, cc=cc, off=off: h.activation(out=Tb[:, cc, 0:W], in_=ps[5][:, off:off + W], func=AF.Sigmoid,
                                                                   bias=bx[:, cc:cc + 1]),
                     reads=["ps5", "small"], writes=["Tb"])
            yield
            for cc in range(4):
                P.op("act", lambda h, cc=cc: h.activation(out=Te[:, cc, 0:W], in_=Ta[:, cc, 0:W], func=AF.Exp, scale=c1[:, cc:cc + 1]),
                     reads=["Ta", "clru"], writes=["Te"])
                P.op("act", lambda h, cc=cc: h.activation(out=Ta[:, cc, 0:W], in_=Ta[:, cc, 0:W], func=AF.Exp, scale=c2[:, cc:cc + 1]),
                     reads=["Ta", "clru"], writes=["Ta"])
            P.op("act", lambda h: h.activation(out=Ta[:, :, 0:W], in_=Ta[:, :, 0:W], func=AF.Ln, bias=self.one_t[:], scale=-1.0),
                 reads=["Ta", "one_t"], writes=["Ta"])
            P.op("act", lambda h: h.activation(out=Ta[:, :, 0:W], in_=Ta[:, :, 0:W], func=AF.Exp, scale=0.5), reads=["Ta"], writes=["Ta"])
            if ti == 0:
                P.op("dve", lambda h: h.memset(Ta[:, :, 0:1], 1.0), writes=["Ta"])
            P.op("dve", lambda h: h.tensor_tensor(out=Tb[:, :, 0:W], in0=Tb[:, :, 0:W], in1=Td[:, :, 0:W], op=ALU.mult),
                 reads=["Tb", "Td"], writes=["Tb"])
            P.op("dve", lambda h: h.tensor_tensor(out=Tb[:, :, 0:W], in0=Tb[:, :, 0:W], in1=Ta[:, :, 0:W], op=ALU.mult),
                 reads=["Tb", "Ta"], writes=["Tb"])
            yield
            for cc in range(4):
                P.op("dve", lambda h, cc=cc: h.tensor_tensor_scan(out=Td[:, cc, 0:W], data0=Te[:, cc, 0:W], data1=Tb[:, cc, 0:W],
                                                                 initial=hcar[:, cc:cc + 1], op0=ALU.mult, op1=ALU.add),
                     reads=["Te", "Tb", "hcar"], writes=["Td"])
            P.op("dve", lambda h: h.tensor_copy(out=hcar[:], in_=Td[:, :, W - 1]), reads=["Td"], writes=["hcar"])
            P.op("dve", lambda h: h.tensor_tensor(out=ycat[:, 4:8, 0:W], in0=gl[:, :, 0:W], in1=Td[:, :, 0:W], op=ALU.mult),
                 reads=[glk, "Td"], writes=["ycat"])
            yield
            hk = hkeys(s0, W)
            for dc in range(8):
                s = self.wacquire(wsO)
                wq = wso[s][:, 0:1024].rearrange("p (c m) -> p c m", c=8)
                bank = 6 + (dc % 2)

                def mm(h, bank=bank, wq=wq):
                    ins = None
                    for cc in range(8):
                        ins = h.matmul(ps[bank][:, 0:W], lhsT=wq[:, cc, :], rhs=ycat[:, cc, 0:W], start=(cc == 0), stop=(cc == 7))
                    return ins
                P.op("pe", mm, reads=[f"wso{s}", "ycat"], writes=[f"ps{bank}"])
                P.op("dve", lambda h, dc=dc, bank=bank: h.tensor_tensor(out=H[:, dc, s0:s0 + W], in0=ps[bank][:, 0:W],
                                                                        in1=H[:, dc, s0:s0 + W], op=ALU.add),
                     reads=[f"ps{bank}"] + hk, writes=hk)
                if dc % 2 == 1:
                    yield

        for _ in X(0):
            pass
        for ti in range(len(tiles)):
            self.interleave(Y(ti), X(ti + 1) if ti + 1 < len(tiles) else None)
        P.barrier()
        ar.release(m)


    def final_tile(self, subs, sq, sq_key, rstd, psn):
        for _ in self.final_tile_gen(subs, sq, sq_key, rstd, psn):
            pass

    def final_tile_gen(self, subs, sq, sq_key, rstd, psn):
        P = self.P
        H = self.H
        gain = self.sm("final_norm")
        for (s0, n) in subs:
            hk = hkeys(s0, n)
            P.op("pool", lambda h, s0=s0, n=n: h.tensor_tensor(out=sq[:, :, 0:n], in0=H[:, :, s0:s0 + n], in1=H[:, :, s0:s0 + n], op=ALU.mult),
                 reads=hk, writes=[sq_key])

            def mm(h, n=n):
                ins = None
                for kc in range(8):
                    ins = h.matmul(psn[:, 0:n], lhsT=self.ones_bf[:], rhs=sq[:, kc, 0:n],
                                   start=(kc == 0), stop=(kc == 7))
                return ins
            P.op("pe", mm, reads=[sq_key, "ones_bf"], writes=["ps6"])
            P.op("act", lambda h, n=n: h.activation(out=rstd[:, 0:n], in_=psn[:, 0:n], func=AF.Ln,
                                                    bias=self.eps_t[:], scale=1.0 / D),
                 reads=["ps6", "eps_t"], writes=["rstd"])
            P.op("act", lambda h, n=n: h.activation(out=rstd[:, 0:n], in_=rstd[:, 0:n], func=AF.Exp, scale=-0.5),
                 reads=["rstd"], writes=["rstd"])
            yield
            for kc in range(8):
                P.op("dve", lambda h, kc=kc, s0=s0, n=n: h.scalar_tensor_tensor(
                    out=H[:, kc, s0:s0 + n], in0=H[:, kc, s0:s0 + n], scalar=gain[:, kc:kc + 1],
                    in1=rstd[:, 0:n], op0=ALU.mult, op1=ALU.mult),
                    reads=hk + ["rstd", "small"], writes=hk)
                yield
            if s0 >= NM:
                self.out_events.append(self.dma("sp", self.dram["out"][:, :, s0 - NM:s0 - NM + n], H[:, :, s0:s0 + n],
                                                reads=hk, writes=["out"], sem="d_out"))
            yield

    def dump_H(self):
        self.issue_x(8)
        for c in range(8):
            self.out_events.append(self.dma("sp", self.dram["out"][:, c, :], self.H[:, c, NM:T],
                                            reads=[f"H{i}" for i in range(16)], writes=["out"], sem="d_out"))

    def build(self):
        cfg = self.cfg
        self.declare()
        self.alloc_fixed()
        self.out_events = []
        phases = cfg["phases"]
        order = []
        for ph in phases:
            if ph.startswith("ffn"):
                order += [f"ffa{ph[3]}{ph[4]}", f"ffb{ph[3]}{ph[4]}"]
            elif ph == "mix0":
                order += ["win_e", "wv_e", "wout_e"]
            elif ph == "mix1":
                order += ["win_o", "wout_o"]
        def packs(ph):
            if ph.startswith("ffn"):
                return [f"ffa{ph[3]}{ph[4]}", f"ffb{ph[3]}{ph[4]}"]
            return ["win_e", "wv_e", "wout_e"] if ph == "mix0" else ["win_o", "wout_o"]

        plans = {}
        for i, ph in enumerate(phases):
            for name in packs(ph):
                plans[name] = self.cast_plan(name, fine_first=(i == 0))
        self.pending_casts = []
        for name in packs(phases[0]) if phases else []:
            self.pending_casts += plans.pop(name)
        self.issue_casts(len(self.pending_casts))
        self.setup()
        mixer_setup_done = False
        for i, ph in enumerate(phases):
            last = (i == len(phases) - 1) and cfg.get("final", True)
            if not mixer_setup_done and (i >= 1 or not ph.startswith("ffn")):
                if "mix0" in phases or "mix1" in phases:
                    self.setup_even()
                if "mix1" in phases:
                    self.setup_odd()
                mixer_setup_done = True
            for name in packs(ph):
                if name in plans:
                    self.pending_casts += plans.pop(name)
            self.issue_casts(len(self.pending_casts))
            if ph.startswith("ffn") and mixer_setup_done and i + 1 < len(phases) and phases[i + 1] == "mix1":
                self.plan_diag()
            if ph.startswith("ffn"):
                for j in range(i + 1, len(phases)):
                    for name in packs(phases[j]):
                        if name in plans:
                            self.pending_casts += plans.pop(name)
                    if phases[j].startswith("ffn"):
                        break
                self.ffn(int(ph[3]), int(ph[4]), final=last)
                self.issue_casts(len(self.pending_casts))
            elif ph == "mix0":
                self.issue_x(8)
                self.mix_even()
            elif ph == "mix1":
                self.issue_x(8)
                self.mix_odd()
        if not cfg.get("final", True):
            self.dump_H()
        block = self.nc.Block()
        with block as blk:
            self.P.emit(blk, self.sems, final_waits=[self.out_events[-1]])
        return self.nc


FULL_PHASES = ["ffn01", "mix0", "ffn02", "ffn11", "mix1", "ffn12"]


def run(inputs, cfg):
    shared = _host_pack(inputs)
    x = np.asarray(inputs["x"], np.float32)
    bld = Builder(cfg)
    nc = bld.build()
    in_maps = []
    ncores = cfg.get("ncores", NCORES)
    for b in range(ncores):
        m = dict(shared)
        m["xin"] = np.ascontiguousarray(x[b].T.reshape(8, 128, SEQ).transpose(1, 0, 2))
        in_maps.append(m)
    if cfg.get("trace"):
        res = run_bass_kernel_spmd(nc, in_maps, core_ids=list(range(ncores)), trace=True)
        print("EXEC_TIME_NS", res.exec_time_ns)
    else:
        res = run_bass_kernel_spmd(nc, in_maps, core_ids=list(range(ncores)))
    outs = []
    for b in range(ncores):
        o = res.results[b]["out"]
        outs.append(o.transpose(2, 1, 0).reshape(SEQ, D))
    return np.ascontiguousarray(np.stack(outs, axis=0), dtype=np.float32)


def kernel(**inputs):
    return run(inputs, dict(phases=FULL_PHASES, final=True))
```

```python
import numpy as np
import concourse.bass as bass
import concourse.mybir as mybir
from concourse.bass_utils import run_bass_kernel_spmd

F32 = mybir.dt.float32
BF16 = mybir.dt.bfloat16
AF = mybir.ActivationFunctionType
ALU = mybir.AluOpType

D = 1024
KC = 8
SEQ = 4096
NM = 16
T = SEQ + NM
DFF = 2816
NFF = 22
EPS = 1e-6
NCORES = 8

ENGS = ("pe", "dve", "act", "pool", "sp")


class Prog:
    def __init__(self, nc):
        self.nc = nc
        self.ops = {e: [] for e in ENGS}
        self.last_write = {}
        self.reads_since = {}
        self.dma_sem_val = {}
        self.barrier_evs = []

    def op(self, eng, fn, reads=(), writes=(), dma=None, n_dma=1):
        deps = []
        for r in reads:
            ev = self.last_write.get(r)
            if ev is not None:
                deps.append((ev, "raw"))
        for w in writes:
            ev = self.last_write.get(w)
            if ev is not None:
                deps.append((ev, "waw"))
            for ev in self.reads_since.get(w, ()):
                deps.append((ev, "war"))
        for ev in self.barrier_evs:
            deps.append((ev, "bar"))
        idx = len(self.ops[eng])
        if dma is None:
            event = ("eng", eng, idx)
        else:
            self.dma_sem_val[dma] = self.dma_sem_val.get(dma, 0) + 16 * n_dma
            event = ("dma", dma, self.dma_sem_val[dma])
        fdeps = []
        for ev, kind in deps:
            if ev[0] == "eng" and ev[1] == eng and kind == "bar":
                continue
            if ev[0] == "eng" and ev[1] == eng and dma is not None:
                continue
            fdeps.append(ev)
        self.ops[eng].append(dict(fn=fn, deps=fdeps, dma=dma))
        for r in reads:
            self.reads_since.setdefault(r, []).append(event)
        for w in writes:
            self.last_write[w] = event
            self.reads_since[w] = []
        return event

    def barrier(self):
        evs = []
        for e in ENGS:
            for i in range(len(self.ops[e]) - 1, -1, -1):
                if self.ops[e][i]["dma"] is None:
                    evs.append(("eng", e, i))
                    break
        for name, v in self.dma_sem_val.items():
            evs.append(("dma", name, v))
        self.barrier_evs = evs

    def emit(self, block, sems, final_waits=()):
        marked = {e: set() for e in ENGS}
        for e in ENGS:
            waited = {}
            for o in self.ops[e]:
                best = {}
                for ev in o["deps"]:
                    key = (ev[0], ev[1])
                    val = ev[2]
                    if waited.get(key, -1) >= val:
                        continue
                    if key not in best or best[key][2] < val:
                        best[key] = ev
                o["waits"] = list(best.values())
                for ev in o["waits"]:
                    waited[(ev[0], ev[1])] = ev[2]
                    if ev[0] == "eng":
                        marked[ev[1]].add(ev[2])
        fin = list(final_waits)
        for ev in fin:
            if ev[0] == "eng":
                marked[ev[1]].add(ev[2])
        semval = {}
        for e in ENGS:
            c = 0
            for i in range(len(self.ops[e])):
                if i in marked[e]:
                    c += 1
                    semval[(e, i)] = c

        def ev_wait(h, ev):
            if ev[0] == "eng":
                h.wait_ge(sems["eng:" + ev[1]], semval[(ev[1], ev[2])])
            else:
                h.wait_ge(sems[ev[1]], ev[2])

        def run(e, h):
            for i, o in enumerate(self.ops[e]):
                for ev in o["waits"]:
                    ev_wait(h, ev)
                if o["dma"] is not None:
                    s = sems[o["dma"]]
                    o["fn"](h, lambda ins: ins.then_inc(s, 16))
                else:
                    ins = o["fn"](h)
                    if i in marked[e]:
                        ins.then_inc(sems["eng:" + e], 1)
            if e == "sp":
                for ev in fin:
                    ev_wait(h, ev)

        @block.tensor
        def _(h):
            run("pe", h)

        @block.vector
        def _(h):
            run("dve", h)

        @block.scalar
        def _(h):
            run("act", h)

        @block.gpsimd
        def _(h):
            run("pool", h)

        @block.sync
        def _(h):
            run("sp", h)


class Arena:
    def __init__(self, nc):
        self.nc = nc
        self.base = (nc.sbuf_base + 31) // 32 * 32
        self.top = nc.sbuf_top
        self.cur = self.base
        self.n = 0

    def alloc(self, name, shape, dtype):
        esz = 4 if dtype == F32 else 2
        nbytes = int(np.prod(shape[1:])) * esz
        off = self.cur
        self.cur = (off + nbytes + 31) // 32 * 32
        assert self.cur <= self.top, f"SBUF overflow allocating {name}: {self.cur} > {self.top}"
        self.n += 1
        self.last_off = off
        return self.nc.alloc_sbuf_tensor_at(f"{name}_{self.n}", list(shape), dtype, offset=off)

    def alloc_at(self, name, shape, dtype, off):
        self.n += 1
        return self.nc.alloc_sbuf_tensor_at(f"{name}_{self.n}", list(shape), dtype, offset=off)

    def mark(self):
        return self.cur

    def release(self, m):
        self.cur = m


def _fm(v):
    v = np.asarray(v, np.float32)
    lead = v.shape[:-1]
    c = v.shape[-1] // 128
    a = v.reshape(lead + (c, 128))
    a = np.moveaxis(a, -1, 0)
    return np.ascontiguousarray(a)


SMALL_LAYOUT = {}


def _pack_small(inp):
    parts = []
    off = 0

    def add(name, arr):
        nonlocal off
        a = np.ascontiguousarray(arr, np.float32).reshape(128, -1)
        SMALL_LAYOUT[name] = (off, a.shape[1])
        parts.append(a)
        off += a.shape[1]

    add("ffn1_norm", _fm(inp["ffn1_norm"]))
    add("mix_norm", _fm(inp["mix_norm"]))
    add("ffn2_norm", _fm(inp["ffn2_norm"]))
    add("final_norm", _fm(inp["final_norm"]))
    add("pool_scale", _fm(inp["pool_scale"][0]))
    add("lb_logits", _fm(inp["hgrn_lb_logits"]))
    add("gnorm", _fm(inp["hgrn_gnorm"][0]))
    add("conv_w", _fm(inp["conv_w"][0]))
    add("conv_b", _fm(inp["conv_b"][0]))
    add("conv_ln_g", _fm(inp["conv_ln_g"][0]))
    add("conv_ln_b", _fm(inp["conv_ln_b"][0]))
    add("lru_conv_w", _fm(inp["lru_conv_w"][0]))
    add("lru_conv_b", _fm(inp["lru_conv_b"][0]))
    add("lru_ba", _fm(inp["lru_ba"][0]))
    add("lru_bx", _fm(inp["lru_bx"][0]))
    add("lru_lambda", _fm(inp["lru_lambda"][0]))
    return np.ascontiguousarray(np.concatenate(parts, axis=1))


def _pack_ffa(wg, wu):
    def r(w):
        a = w.reshape(8, 128, 11, 2, 128)
        return a.transpose(2, 1, 3, 0, 4)
    a = np.stack([r(wg), r(wu)], axis=3)
    return np.ascontiguousarray(a.reshape(11, 128, 4096))


def _pack_ffb(wd):
    a = wd.reshape(22, 128, 4, 2, 128)
    a = a.transpose(2, 1, 3, 0, 4)
    return np.ascontiguousarray(a.reshape(4, 128, 5632))


def _pack_win(w):
    n = w.shape[1] // 128
    a = w.reshape(8, 128, n, 128).transpose(2, 1, 0, 3)
    return np.ascontiguousarray(a.reshape(n, 128, 1024))


def _pack_wtm(w):
    n = w.shape[1]
    a = w.reshape(8, 128, n).transpose(1, 0, 2)
    return np.ascontiguousarray(a.reshape(128, 8 * n))


def _pack_wout(w):
    a = w.reshape(8, 128, 4, 256).transpose(2, 1, 0, 3)
    return np.ascontiguousarray(a.reshape(4, 128, 2048))


BIG_SPECS = []


def _big_specs():
    specs = []
    for l in range(2):
        for f in (1, 2):
            specs.append((f"ffa{l}{f}", [11, 128, 4096], 2048))
            specs.append((f"ffb{l}{f}", [4, 128, 5632], 1408))
    specs.append(("win_e", [26, 128, 1024], 1024))
    specs.append(("wv_e", [128, 6144], 2048))
    specs.append(("wout_e", [4, 128, 2048], 2048))
    specs.append(("win_o", [16, 128, 1024], 1024))
    specs.append(("wout_o", [4, 128, 2048], 2048))
    return specs


def _host_pack(inp):
    big = {}
    for l in range(2):
        big[f"ffa{l}1"] = _pack_ffa(inp["ffn1_wg"][l], inp["ffn1_wu"][l])
        big[f"ffb{l}1"] = _pack_ffb(inp["ffn1_wd"][l])
        big[f"ffa{l}2"] = _pack_ffa(inp["ffn2_wg"][l], inp["ffn2_wu"][l])
        big[f"ffb{l}2"] = _pack_ffb(inp["ffn2_wd"][l])
    we = inp["w_in_even"][0]
    big["win_e"] = _pack_win(we)
    big["wv_e"] = _pack_wtm(we[:, 256 + 2 * 768: 256 + 3 * 768])
    big["wout_e"] = _pack_wout(inp["w_out_even"][0])
    big["win_o"] = _pack_win(inp["w_in_odd"][0])
    big["wout_o"] = _pack_wout(inp["w_out_odd"][0])
    shared = dict(big)
    shared["small"] = _pack_small(inp)
    shared["meta"] = _fm(inp["meta_tokens"]).transpose(0, 2, 1).copy()
    shared["pool_w"] = np.ascontiguousarray(inp["pool_w"][0], np.float32)
    shared["lru_wa"] = np.ascontiguousarray(inp["lru_wa"][0], np.float32)
    shared["lru_wx"] = np.ascontiguousarray(inp["lru_wx"][0], np.float32)
    return shared


def hkeys(s0, n):
    ks = []
    if s0 < NM:
        ks.append("Hm")
    lo = max(s0, NM) - NM
    hi = s0 + n - NM
    if hi > lo:
        for b in range(lo // 256, (hi - 1) // 256 + 1):
            ks.append(f"H{b}")
    return ks


class Builder:
    def __init__(self, cfg):
        self.cfg = cfg
        nc = self.nc = bass.Bass("TRN2", target_bir_lowering=False)
        self.P = Prog(nc)
        self.ar = Arena(nc)
        self.sems = {}
        self.dram = {}
        self.cast_chunks = {}
        self.sm_n = sum(v[1] for v in SMALL_LAYOUT.values())

    def sem(self, name):
        if name not in self.sems:
            self.sems[name] = self.nc.alloc_semaphore(name.replace(":", "_"))
        return self.sems[name]

    def dma(self, eng, out, in_, reads, writes, sem, **kw):
        self.sem(sem)
        return self.P.op(eng, lambda h, inc: inc(h.dma_start(out=out, in_=in_, **kw)),
                         reads=reads, writes=writes, dma=sem)

    def sm(self, name):
        off, n = SMALL_LAYOUT[name]
        return self.small[:, off:off + n]

    def declare(self):
        nc = self.nc
        dr = self.dram
        dr["xin"] = nc.dram_tensor("xin", [128, 8, SEQ], F32, kind="ExternalInput").ap()
        dr["meta"] = nc.dram_tensor("meta", [128, 8, NM], F32, kind="ExternalInput").ap()
        dr["small"] = nc.dram_tensor("small", [128, self.sm_n], F32, kind="ExternalInput").ap()
        dr["pool_w"] = nc.dram_tensor("pool_w", [4, 64, 64], F32, kind="ExternalInput").ap()
        dr["lru_wa"] = nc.dram_tensor("lru_wa", [8, 64, 64], F32, kind="ExternalInput").ap()
        dr["lru_wx"] = nc.dram_tensor("lru_wx", [8, 64, 64], F32, kind="ExternalInput").ap()
        for name, shape, _ in _big_specs():
            dr[name] = nc.dram_tensor(name, shape, F32, kind="ExternalInput").ap()
            dr[name + "_b"] = nc.dram_tensor(name + "_b", shape, BF16, kind="Internal").ap()
        dr["diag_b"] = nc.dram_tensor("diag_b", [4, 128, 31 * 128], BF16, kind="Internal").ap()
        dr["out"] = nc.dram_tensor("out", [128, 8, SEQ], F32, kind="ExternalOutput").ap()
        for e in ENGS:
            self.sem("eng:" + e)

    def cast_plan(self, name, fine_first=False):
        spec = {n: (s, b) for n, s, b in _big_specs()}[name]
        shape, b = spec
        src = self.dram[name]
        dst = self.dram[name + "_b"]
        if len(shape) == 3:
            pat = "g p (a b) -> (g p a) b"
        else:
            pat = "p (a b) -> (p a) b"
        s2 = src.rearrange(pat, b=b)
        d2 = dst.rearrange(pat, b=b)
        rows = s2.shape[0]
        r = 0
        chunks = []
        thunks = []
        while r < rows:
            step = 256
            n = min(step, rows - r)
            key = f"{name}_b:{len(chunks)}"

            def issue(pace, r=r, n=n, key=key):
                self.dma("pool", d2[r:r + n, :], s2[r:r + n, :], reads=[name] + list(pace),
                         writes=[key, name + "_b:ser"], sem="cast_" + name)
            thunks.append(issue)
            chunks.append((r, r + n, key))
            r += n
        self.cast_chunks[name] = chunks
        return thunks

    def issue_casts(self, n, pace=()):
        for _ in range(min(n, len(self.pending_casts))):
            self.pending_casts.pop(0)(pace)

    def wkeys(self, name, lo=None, hi=None):
        ch = self.cast_chunks[name]
        if lo is None:
            return [k for _, _, k in ch]
        return [k for a, b, k in ch if a < hi and b > lo]

    def alloc_fixed(self):
        ar = self.ar
        self.H = ar.alloc("H", [128, 8, T], F32)
        self.small = ar.alloc("small", [128, self.sm_n], F32)
        self.ones_bf = ar.alloc("ones_bf", [128, 128], BF16)
        self.eps_t = ar.alloc("eps_t", [128, 1], F32)
        self.ps = [self.nc.alloc_psum_tensor(f"ps{i}", [128, 512], F32) for i in range(8)]

    def setup(self):
        P = self.P
        dr = self.dram
        self.dma("sp", self.small[:], dr["small"], reads=[], writes=["small"], sem="d_small")
        self.dma("sp", self.H[:, :, 0:NM], dr["meta"], reads=[], writes=["Hm"], sem="d_in")
        self.pending_x = list(range(8))
        self.issue_x(1)
        P.op("dve", lambda h: h.memset(self.ones_bf[:], 1.0), writes=["ones_bf"])
        P.op("dve", lambda h: h.memset(self.eps_t[:], EPS), writes=["eps_t"])

    def plan_diag(self):
        base = (self.ar.top - 3072) // 32 * 32
        stages = [self.ar.alloc_at(f"dstage{i}", [128, 6, 128], BF16, base + 1536 * i) for i in range(2)]
        cw = self.sm("conv_w")
        diag = self.dram["diag_b"]
        thunks = []
        k = 0
        for cc in range(4):
            for j0 in range(0, 31, 6):
                nj = min(6, 31 - j0)
                b = k % 2
                k += 1

                def issue(cc=cc, j0=j0, nj=nj, b=b):
                    st = stages[b]
                    for jj in range(nj):
                        j = j0 + jj
                        self.P.op("dve", lambda h, jj=jj, j=j, cc=cc, st=st: h.tensor_scalar(
                            out=st[:, jj, :], in0=self.ident_bf[:], scalar1=cw[:, j * 4 + cc:j * 4 + cc + 1], scalar2=None,
                            op0=ALU.mult), reads=["ident_bf", "small"], writes=[f"dstage{b}"])
                    self.dma("sp", diag[cc][:, j0 * 128:(j0 + nj) * 128], st[:, 0:nj, :].rearrange("p j m -> p (j m)"),
                             reads=[f"dstage{b}"], writes=["diag_b"], sem=f"d_diag{b}")
                thunks.append(issue)
        self.pending_diag = thunks
        self.diag_prebuilt = True

    def issue_diag(self, n):
        for _ in range(min(n, len(getattr(self, "pending_diag", [])))):
            self.pending_diag.pop(0)()

    def issue_x(self, n):
        for _ in range(min(n, len(self.pending_x))):
            i = self.pending_x.pop(0)
            a = NM + 512 * i
            self.dma("sp", self.H[:, :, a:a + 512], self.dram["xin"][:, :, 512 * i:512 * i + 512], reads=[],
                     writes=hkeys(a, 512), sem=f"d_in{i}")

    def wstream(self, loads, slots, slot_keys, sem_names):
        st = dict(loads=loads, issued=0, consumed=0, slots=slots, keys=slot_keys, sems=sem_names)
        return st

    def wacquire(self, st):
        k = st["consumed"]
        ns = len(st["slots"])
        while st["issued"] < min(len(st["loads"]), k + ns):
            i = st["issued"]
            s = i % ns
            dst, src, skey = st["loads"][i](st["slots"][s])
            if isinstance(skey, list):
                skeys = skey
            elif skey[:-2] in self.cast_chunks:
                skeys = self.wkeys(skey[:-2])
            else:
                skeys = [skey]
            self.dma("sp", dst, src, reads=skeys, writes=[st["keys"][s]], sem=st["sems"][s])
            st["issued"] += 1
        st["consumed"] += 1
        return k % ns

    def rmsnorm(self, s0, n, loc, xn, xn_key, sq, sq_key, rstd, rstd_key, psn, psn_key, gain):
        P = self.P
        H = self.H
        hk = hkeys(s0, n)
        P.op("act", lambda h: h.activation(out=sq[:, :, loc:loc + n], in_=H[:, :, s0:s0 + n], func=AF.Square),
             reads=hk, writes=[sq_key])

        def mm(h):
            ins = None
            for kc in range(8):
                ins = h.matmul(psn[:, 0:n], lhsT=self.ones_bf[:], rhs=sq[:, kc, loc:loc + n],
                               start=(kc == 0), stop=(kc == 7))
            return ins
        P.op("pe", mm, reads=[sq_key, "ones_bf"], writes=[psn_key])
        P.op("act", lambda h: h.activation(out=rstd[:, 0:n], in_=psn[:, 0:n], func=AF.Ln,
                                           bias=self.eps_t[:], scale=1.0 / D),
             reads=[psn_key, "eps_t"], writes=[rstd_key])
        P.op("act", lambda h: h.activation(out=rstd[:, 0:n], in_=rstd[:, 0:n], func=AF.Exp, scale=-0.5),
             reads=[rstd_key], writes=[rstd_key])
        for kc in range(8):
            P.op("dve", lambda h, kc=kc: h.scalar_tensor_tensor(
                out=xn[:, kc, loc:loc + n], in0=H[:, kc, s0:s0 + n], scalar=gain[:, kc:kc + 1],
                in1=rstd[:, 0:n], op0=ALU.mult, op1=ALU.mult),
                reads=hk + [rstd_key, "small"], writes=[xn_key])

    def ffn(self, l, f, final=False):
        P = self.P
        ar = self.ar
        H = self.H
        m = ar.mark()
        WT = 528
        xn = ar.alloc("xn", [128, 8, WT], BF16)
        aT = ar.alloc("aT", [128, NFF, WT], BF16)
        rstd = ar.alloc("rstd", [128, 512], F32)
        stt = [ar.alloc(f"st{i}", [128, 512], F32) for i in range(2)]
        wsl = [ar.alloc(f"wsl{i}", [128, 4096], BF16) for i in range(3)]
        sqv = ar.alloc("sq", [128, 8, WT], BF16)
        gain = self.sm(f"ffn{f}_norm")[:, l * 8:(l + 1) * 8]
        tiles = [[(0, 16), (16, 512)]] + [[(528 + 512 * i, 512)] for i in range(7)]
        wa = self.dram[f"ffa{l}{f}_b"]
        wb = self.dram[f"ffb{l}{f}_b"]
        wbv = [wb[dcp].rearrange("p (j x) -> p j x", j=2) for dcp in range(4)]
        loads = []
        for ti in range(len(tiles)):
            for g in range(11):
                loads.append(lambda slot, g=g: (slot[:, 0:4096], wa[g], self.wkeys(f"ffa{l}{f}", 256 * g, 256 * g + 256)))
            for dc in range(8):
                loads.append(lambda slot, dc=dc: (slot[:, 0:2816], wbv[dc // 2][:, dc % 2, :],
                                                  self.wkeys(f"ffb{l}{f}", 512 * (dc // 2), 512 * (dc // 2) + 512)))
        ws = self.wstream(loads, wsl, ["wsl0", "wsl1", "wsl2"], ["dw0", "dw1", "dw2"])
        psH = [self.ps[0], self.ps[1]]
        psU = [self.ps[2], self.ps[3]]
        psY = [self.ps[4], self.ps[5]]
        psN = self.ps[6]
        cnt = 0
        tiles = tiles[:self.cfg.get("ffn_tiles", len(tiles))]
        stage = 3

        def norm_tile(subs):
            for (s0, n) in subs:
                self.rmsnorm(s0, n, s0 - subs[0][0], xn, "xn", sqv, "sq", rstd, "rstd", psN, "ps6", gain)

        norm_tile(tiles[0])
        per_tile = -(-len(self.pending_casts) // max(1, len(tiles) - 1))
        pending_final = None
        fin_gen = None
        for ti, subs in enumerate(tiles):
            t0 = subs[0][0]
            self.issue_x(1)
            if ti >= 1:
                prev = tiles[ti - 1]
                self.issue_casts(per_tile, pace=hkeys(prev[-1][0], prev[-1][1]))
            for g in range(11):
                if g in (0, 4, 8):
                    self.issue_diag(1)
                if g == 2 and pending_final is not None:
                    fin_gen = self.final_tile_gen(pending_final, sqv, "sq", rstd, psN)
                    pending_final = None
                s = self.wacquire(ws)
                wv = wsl[s][:, 0:4096].rearrange("p (j u k m) -> p j u k m", j=2, u=2, k=8)
                for j in range(2):
                    ffc = 2 * g + j
                    for (s0, n) in subs:
                        if fin_gen is not None:
                            try:
                                next(fin_gen)
                            except StopIteration:
                                fin_gen = None
                        loc = s0 - t0
                        b = cnt % 2
                        cnt += 1

                        def mmh(h, j=j, loc=loc, n=n, b=b, wv=wv):
                            ins = None
                            for kc in range(8):
                                ins = h.matmul(psH[b][:, 0:n], lhsT=wv[:, j, 0, kc, :], rhs=xn[:, kc, loc:loc + n],
                                               start=(kc == 0), stop=(kc == 7))
                            return ins

                        def mmu(h, j=j, loc=loc, n=n, b=b, wv=wv):
                            ins = None
                            for kc in range(8):
                                ins = h.matmul(psU[b][:, 0:n], lhsT=wv[:, j, 1, kc, :], rhs=xn[:, kc, loc:loc + n],
                                               start=(kc == 0), stop=(kc == 7))
                            return ins
                        P.op("pe", mmh, reads=[f"wsl{s}", "xn"], writes=[f"ps{b}"])
                        P.op("pe", mmu, reads=[f"wsl{s}", "xn"], writes=[f"ps{2 + b}"])
                        P.op("act", lambda h, b=b, n=n: h.activation(out=stt[b][:, 0:n], in_=psH[b][:, 0:n], func=AF.Silu),
                             reads=[f"ps{b}"], writes=[f"st{b}"])
                        P.op("dve", lambda h, b=b, n=n, ffc=ffc, loc=loc: h.tensor_tensor(
                            out=aT[:, ffc, loc:loc + n], in0=stt[b][:, 0:n], in1=psU[b][:, 0:n], op=ALU.mult),
                            reads=[f"st{b}", f"ps{2 + b}"], writes=["aT"])
            if fin_gen is not None:
                for _ in fin_gen:
                    pass
                fin_gen = None
            for dc in range(8):
                if dc == 4 and ti + 1 < len(tiles):
                    norm_tile(tiles[ti + 1])
                s = self.wacquire(ws)
                wv = wsl[s][:, 0:2816].rearrange("p (j f m) -> p j f m", j=1, f=NFF)
                for dj in range(1):
                    for (s0, n) in subs:
                        if stage < 3:
                            continue
                        loc = s0 - t0
                        b = cnt % 2
                        cnt += 1

                        def mmy(h, dj=dj, loc=loc, n=n, b=b, wv=wv):
                            ins = None
                            for ffc in range(NFF):
                                ins = h.matmul(psY[b][:, 0:n], lhsT=wv[:, dj, ffc, :], rhs=aT[:, ffc, loc:loc + n],
                                               start=(ffc == 0), stop=(ffc == NFF - 1))
                            return ins
                        P.op("pe", mmy, reads=[f"wsl{s}", "aT"], writes=[f"ps{4 + b}"])
                        hk = hkeys(s0, n)
                        P.op("dve", lambda h, b=b, n=n, dc=dc, s0=s0: h.scalar_tensor_tensor(
                            out=H[:, dc, s0:s0 + n], in0=psY[b][:, 0:n], scalar=0.5, in1=H[:, dc, s0:s0 + n],
                            op0=ALU.mult, op1=ALU.add),
                            reads=[f"ps{4 + b}"] + hk, writes=hk)
            if final:
                pending_final = subs
        if pending_final is not None:
            self.final_tile(pending_final, sqv, "sq", rstd, psN)
        self.issue_x(8)
        self.issue_diag(64)
        P.barrier()
        ar.release(m)

    def setup_even(self):
        P = self.P
        ar = self.ar
        nc = self.nc
        self.ident_bf = ar.alloc("ident_bf", [128, 128], BF16)
        self.cmask = ar.alloc("cmask", [64, 6, 64], BF16)
        self.rmask = ar.alloc("rmask", [128, 256], F32)
        self.lbt = ar.alloc("lbt", [128, 3, 6], F32)
        self.pool_bd = ar.alloc("pool_bd", [128, 2, 128], BF16)
        self.pfix = ar.alloc("pfix", [128, 2, 16], F32)
        self.pinv = ar.alloc("pinv", [128, 2], F32)
        m = ar.mark()
        idx = ar.alloc("idx", [128, 6, 64], F32)
        stg = ar.alloc("stg", [128, 2, 128], F32)
        ex = ar.alloc("ex", [128, 2, 6], F32)
        tp1 = ar.alloc("tp1", [128, 16], F32)
        idf = idx[:].rearrange("p h s -> p (h s)")
        P.op("pool", lambda h: h.iota(idf[:, 0:128], [[1, 128]], base=0, channel_multiplier=-1,
                                      allow_small_or_imprecise_dtypes=True), writes=["idx"])
        P.op("dve", lambda h: h.tensor_scalar(out=self.ident_bf[:], in0=idf[:, 0:128], scalar1=0.0, scalar2=None,
                                              op0=ALU.is_equal), reads=["idx"], writes=["ident_bf"])
        P.op("pool", lambda h: h.iota(idx[0:64, :, :], [[0, 6], [1, 64]], base=0, channel_multiplier=-1,
                                      allow_small_or_imprecise_dtypes=True), reads=["ident_bf"], writes=["idx"])
        P.op("dve", lambda h: h.tensor_scalar(out=self.cmask[:], in0=idx[0:64, :, :], scalar1=0.0, scalar2=None,
                                              op0=ALU.is_ge), reads=["idx"], writes=["cmask"])
        P.op("dve", lambda h: h.memset(self.rmask[:], 1.0), writes=["rmask"])
        P.op("dve", lambda h: h.memset(self.rmask[:].rearrange("p (c s) -> p c s", s=64)[:, :, 0:1], 0.0),
             writes=["rmask"])
        lg = self.sm("lb_logits").rearrange("p (r h) -> p r h", r=2)
        P.op("act", lambda h: h.activation(out=ex[:], in_=lg, func=AF.Exp), reads=["small"], writes=["ex"])
        P.op("dve", lambda h: h.tensor_tensor(out=self.lbt[:, 1, :], in0=ex[:, 0, :], in1=ex[:, 1, :], op=ALU.add),
             reads=["ex"], writes=["lbt"])
        P.op("dve", lambda h: h.reciprocal(out=self.lbt[:, 2, :], in_=self.lbt[:, 1, :]), reads=["lbt"], writes=["lbt"])
        P.op("dve", lambda h: h.tensor_tensor(out=self.lbt[:, 0, :], in0=ex[:, 0, :], in1=self.lbt[:, 2, :], op=ALU.mult),
             reads=["lbt", "ex"], writes=["lbt"])
        P.op("dve", lambda h: h.tensor_scalar(out=self.lbt[:, 1, :], in0=self.lbt[:, 0, :], scalar1=-1.0, scalar2=1.0,
                                              op0=ALU.mult, op1=ALU.add), reads=["lbt"], writes=["lbt"])
        P.op("dve", lambda h: h.tensor_scalar(out=self.lbt[:, 2, :], in0=self.lbt[:, 0, :], scalar1=1.0, scalar2=-1.0,
                                              op0=ALU.mult, op1=ALU.add), reads=["lbt"], writes=["lbt"])
        P.op("dve", lambda h: h.memset(stg[:], 0.0), writes=["stg"])
        for g in range(4):
            c, j = divmod(g, 2)
            self.dma("sp", stg[64 * j:64 * j + 64, c, 64 * j:64 * j + 64], self.dram["pool_w"][g],
                     reads=[], writes=["stg"], sem="d_small")
        P.op("dve", lambda h: h.tensor_copy(out=self.pool_bd[:], in_=stg[:]), reads=["stg"], writes=["pool_bd"])
        P.op("pool", lambda h: h.iota(tp1[:], [[1, 16]], base=1, channel_multiplier=0,
                                      allow_small_or_imprecise_dtypes=True), writes=["tp1"])
        for g, w in enumerate((2, 4, 8, 16)):
            c, j = divmod(g, 2)
            rng = slice(64 * j, 64 * j + 64)
            P.op("dve", lambda h, c=c, rng=rng, w=w: h.tensor_scalar(out=self.pfix[rng, c, :], in0=tp1[rng, :], scalar1=float(w),
                                                                    scalar2=None, op0=ALU.min), reads=["tp1"], writes=["pfix"])
            P.op("dve", lambda h, c=c, rng=rng: h.reciprocal(out=self.pfix[rng, c, :], in_=self.pfix[rng, c, :]),
                 reads=["pfix"], writes=["pfix"])
            P.op("dve", lambda h, c=c, rng=rng, w=w: h.tensor_scalar(out=self.pfix[rng, c, :], in0=self.pfix[rng, c, :], scalar1=float(w),
                                                                    scalar2=None, op0=ALU.mult), reads=["pfix"], writes=["pfix"])
            P.op("dve", lambda h, c=c, rng=rng, w=w: h.memset(self.pinv[rng, c:c + 1], 1.0 / w), writes=["pinv"])
        P.barrier()
        ar.release(m)

    def mix_even(self):
        P = self.P
        ar = self.ar
        H = self.H
        ps = self.ps
        m = ar.mark()
        WT = 256
        xn = ar.alloc("xn", [128, 8, WT], BF16)
        wsl = [ar.alloc(f"wsl{i}", [128, 3072], BF16) for i in range(2)]
        B1 = ar.alloc("B1", [128, 6, WT], F32)
        B2 = ar.alloc("B2", [128, 6, WT], F32)
        b2_off = ar.last_off
        B3 = ar.alloc("B3", [128, 6, WT], F32)
        qb = ar.alloc("qb", [128, 6, WT], BF16)
        kb = ar.alloc("kb", [128, 6, WT], BF16)
        kdT = ar.alloc("kdT", [128, 6, WT], BF16)
        sgl = ar.alloc("sgl", [128, 6, WT], BF16)
        kd_tm = ar.alloc("kd_tm", [64, 4, 768], BF16)
        kd_off = ar.last_off
        v_tm = ar.alloc("v_tm", [64, 4, 768], BF16)
        S = ar.alloc("S", [128, 768], F32)
        S_bf = ar.alloc("S_bf", [128, 768], BF16)
        A_sb = ar.alloc("A_sb", [64, 6, 64], BF16)
        dec = ar.alloc("dec", [128, 6, 4], F32)
        ycat = ar.alloc("ycat", [128, 8, WT], BF16)
        px = ar.alloc("px", [128, 2, 16 + WT], F32)
        Sa = ar.alloc_at("SaB2", [128, 2, 16 + WT], F32, b2_off)
        Sb = ar.alloc_at("SbB2", [128, 2, 16 + WT], F32, b2_off + 2176)
        mixed = kdT[:, 0:2, :]
        rstd = B3[:, 0, :]
        gain = self.sm("mix_norm")[:, 0:8]
        gn = self.sm("gnorm")
        pscale = self.sm("pool_scale")
        lb = self.lbt[:, 0, :]
        oml = self.lbt[:, 1, :]
        noml = self.lbt[:, 2, :]
        win = self.dram["win_e_b"]
        wv = self.dram["wv_e_b"].rearrange("p (k n) -> p k n", k=8)
        wo = self.dram["wout_e_b"]
        tiles = [(0, 16)] + [(16 + 256 * i, 256) for i in range(16)]
        tiles = tiles[:self.cfg.get("mix_tiles", len(tiles))]
        def ld_fm(oc0, n=3):
            return lambda slot: (slot[:, 0:1024 * n].rearrange("p (o x) -> p o x", o=n),
                                 win[oc0:oc0 + n].rearrange("o p x -> p o x"), "win_e_b")

        def ld_v(vp):
            return lambda slot: (slot[:, 0:3072].rearrange("p (k n) -> p k n", k=8), wv[:, :, 384 * vp:384 * vp + 384], "wv_e_b")

        def ld_o(q):
            return lambda slot: (slot[:, 0:2048], wo[q], "wout_e_b")

        loads = [ld_fm(8), ld_fm(11)]
        for ti_ in range(len(tiles)):
            loads += [ld_fm(2), ld_fm(5), ld_v(0), ld_v(1), ld_fm(20), ld_fm(23), ld_fm(0, 2)]
            if ti_ + 1 < len(tiles):
                loads += [ld_o(0), ld_o(1), ld_fm(8), ld_o(2), ld_fm(11), ld_o(3)]
            else:
                loads += [ld_o(0), ld_o(1), ld_o(2), ld_o(3)]
        ws = self.wstream(loads, wsl, ["wsl0", "wsl1"], ["dw0", "dw1"])
        P.op("pool", lambda h: h.memset(S[:], 0.0), writes=["S"])
        P.op("pool", lambda h: h.memset(S_bf[:], 0.0), writes=["S_bf"])
        P.op("pool", lambda h: h.memset(px[:, :, 0:16], 0.0), writes=["px"])
        first_chunk = True
        bankrot = 0
        sqv = B1[:].rearrange("p h w -> p (h w)").bitcast(BF16)[:, 0:8 * WT].rearrange("p (k w) -> p k w", k=8)

        def bc_last(ap2, n):
            return bass.AP(ap2.tensor, ap2.offset, [list(ap2.ap[0]), list(ap2.ap[1]), [0, n]])

        def fm_piece_w(W, nchunks, dest_fn, nbanks):
            nonlocal bankrot
            s = self.wacquire(ws)
            wvw = wsl[s][:, 0:nchunks * 1024].rearrange("p (o k m) -> p o k m", o=nchunks, k=8)
            for i in range(nchunks):
                bank = bankrot % nbanks
                bankrot += 1
                pt = ps[bank]

                def mm(h, i=i, pt=pt, wvw=wvw):
                    ins = None
                    for kc in range(8):
                        ins = h.matmul(pt[:, 0:W], lhsT=wvw[:, i, kc, :], rhs=xn[:, kc, 0:W],
                                       start=(kc == 0), stop=(kc == 7))
                    return ins
                P.op("pe", mm, reads=[f"wsl{s}", "xn"], writes=[f"ps{bank}"])
                dest_fn(i, pt[:, 0:W], f"ps{bank}")

        def za(ti):
            s0, W = tiles[ti]
            C = 16 if W == 16 else 64
            nch = W // C
            self.rmsnorm(s0, W, 0, xn, "xn", sqv, "B1", rstd, "B3", ps[6], "ps6", gain)
            yield
            for pc in range(2):
                def dz(i, pap, pk, pc=pc):
                    hh = 3 * pc + i
                    P.op("act", lambda h: h.activation(out=B1[:, hh, 0:W], in_=pap, func=AF.Sigmoid),
                         reads=[pk], writes=["B1"])
                fm_piece_w(W, 3, dz, 3)
                if pc == 0:
                    yield
            for hh in range(6):
                P.op("dve", lambda h, hh=hh: h.tensor_scalar(out=B2[:, hh, 0:W], in0=B1[:, hh, 0:W], scalar1=oml[:, hh:hh + 1],
                                                            scalar2=lb[:, hh:hh + 1], op0=ALU.mult, op1=ALU.add),
                     reads=["B1", "lbt"], writes=["B2"])
                P.op("dve", lambda h, hh=hh: h.tensor_scalar(out=B1[:, hh, 0:W], in0=B1[:, hh, 0:W], scalar1=noml[:, hh:hh + 1],
                                                            scalar2=oml[:, hh:hh + 1], op0=ALU.mult, op1=ALU.add),
                     reads=["B1", "lbt"], writes=["B1"])
            yield
            P.op("act", lambda h: h.activation(out=B2[:, :, 0:W], in_=B2[:, :, 0:W], func=AF.Ln), reads=["B2"], writes=["B2"])
            for hh in range(6):
                P.op("dve", lambda h, hh=hh: h.tensor_tensor_scan(out=B3[:, hh, 0:W], data0=self.rmask[:, 0:W], data1=B2[:, hh, 0:W],
                                                                 initial=0.0, op0=ALU.mult, op1=ALU.add),
                     reads=["B2", "rmask"], writes=["B3"])
            yield
            if nch == 1:
                b_last = B3[:, :, W - 1:W]
                dec_v = dec[:, :, 0:1]
            else:
                b_last = B3[:, :, 0:W].rearrange("p h (c s) -> p h c s", s=C)[:, :, :, C - 1]
                dec_v = dec[:, :, 0:nch]
            P.op("act", lambda h: h.activation(out=dec_v, in_=b_last, func=AF.Exp), reads=["B3"], writes=["dec"])
            P.op("act", lambda h: h.activation(out=B2[:, :, 0:W], in_=B3[:, :, 0:W], func=AF.Exp), reads=["B3"], writes=["B2"])
            P.op("act", lambda h: h.activation(out=B3[:, :, 0:W], in_=B3[:, :, 0:W], func=AF.Exp, scale=-1.0),
                 reads=["B3"], writes=["B3"])
            yield

        def out_gen(s0, W):
            hk = hkeys(s0, W)
            for q in range(4):
                s = self.wacquire(ws)
                wq = wsl[s][:, 0:2048].rearrange("p (c m) -> p c m", c=8)
                for dj in range(2):
                    dc = 2 * q + dj
                    bank = 3 + (dc % 3)

                    def mm(h, dj=dj, bank=bank, wq=wq):
                        ins = None
                        for cc in range(8):
                            ins = h.matmul(ps[bank][:, 0:W], lhsT=wq[:, cc, dj * 128:(dj + 1) * 128], rhs=ycat[:, cc, 0:W],
                                           start=(cc == 0), stop=(cc == 7))
                        return ins
                    P.op("pe", mm, reads=[f"wsl{s}", "ycat"], writes=[f"ps{bank}"])
                    P.op("dve", lambda h, dc=dc, bank=bank: h.tensor_tensor(out=H[:, dc, s0:s0 + W], in0=ps[bank][:, 0:W],
                                                                            in1=H[:, dc, s0:s0 + W], op=ALU.add),
                         reads=[f"ps{bank}"] + hk, writes=hk)
                yield

        def tile_body(ti, s0, W):
            nonlocal first_chunk, bankrot
            C = 16 if W == 16 else 64
            nch = W // C
            hk = hkeys(s0, W)
            if ti == 0:
                for _ in za(0):
                    pass

            def fm_piece(nchunks, dest_fn):
                fm_piece_w(W, nchunks, dest_fn, 6)

            P.op("dve", lambda h: h.tensor_tensor(out=kb[:, :, 0:W], in0=B1[:, :, 0:W], in1=B3[:, :, 0:W], op=ALU.mult),
                 reads=["B1", "B3"], writes=["kb"])
            if nch == 1:
                kd_out = kdT[:, :, 0:W]
                kd_in = kb[:, :, 0:W]
                dec_b = bc_last(dec[:, :, 0], W)
            else:
                kd_out = kdT[:, :, 0:W].rearrange("p h (c s) -> p (h c) s", s=C)
                kd_in = kb[:, :, 0:W].rearrange("p h (c s) -> p (h c) s", s=C)
                dec_b = bc_last(dec[:].rearrange("p h c -> p (h c)"), C)
            P.op("dve", lambda h: h.tensor_tensor(out=kd_out, in0=kd_in, in1=dec_b, op=ALU.mult),
                 reads=["kb", "dec"], writes=["kdT"])
            for pc in range(2):
                def dq(i, pap, pk, pc=pc):
                    hh = 3 * pc + i
                    P.op("act", lambda h: h.activation(out=B1[:, hh, 0:W], in_=pap, func=AF.Silu),
                         reads=[pk], writes=["B1"])
                fm_piece(3, dq)
            P.op("dve", lambda h: h.tensor_tensor(out=qb[:, :, 0:W], in0=B1[:, :, 0:W], in1=B2[:, :, 0:W], op=ALU.mult),
                 reads=["B1", "B2"], writes=["qb"])
            for vp in range(2):
                s = self.wacquire(ws)
                wvv = wsl[s][:, 0:3072].rearrange("p (k n) -> p k n", k=8)
                for c in range(nch):
                    bank = 6 + (c % 2)

                    def mmv(h, c=c, bank=bank, wvv=wvv):
                        ins = None
                        for kc in range(8):
                            ins = h.matmul(ps[bank][0:C, 0:384], lhsT=xn[:, kc, c * C:(c + 1) * C], rhs=wvv[:, kc, :],
                                           start=(kc == 0), stop=(kc == 7))
                        return ins
                    P.op("pe", mmv, reads=[f"wsl{s}", "xn"], writes=[f"ps{bank}"])
                    if c % 2 == 0:
                        P.op("act", lambda h, c=c, bank=bank, vp=vp: h.activation(out=v_tm[0:C, c, 384 * vp:384 * vp + 384],
                                                                                  in_=ps[bank][0:C, 0:384], func=AF.Copy),
                             reads=[f"ps{bank}"], writes=["v_tm"])
                    else:
                        P.op("dve", lambda h, c=c, bank=bank, vp=vp: h.tensor_copy(out=v_tm[0:C, c, 384 * vp:384 * vp + 384],
                                                                                   in_=ps[bank][0:C, 0:384]),
                             reads=[f"ps{bank}"], writes=["v_tm"])
            for pc in range(2):
                def dg(i, pap, pk, pc=pc):
                    hh = 3 * pc + i
                    P.op("act", lambda h: h.activation(out=sgl[:, hh, 0:W], in_=pap, func=AF.Silu),
                         reads=[pk], writes=["sgl"])
                fm_piece(3, dg)
            def dp(i, pap, pk):
                P.op("act", lambda h: h.activation(out=px[:, i, 16:16 + W], in_=pap, func=AF.Copy),
                     reads=[pk], writes=["px"])
            fm_piece(2, dp)
            psT = ps[5][:].bitcast(BF16)
            for c in range(nch):
                def tr(h, c=c):
                    ins = None
                    for hh in range(6):
                        ins = h.transpose(out=psT[0:C, hh * 128:(hh + 1) * 128], in_=kdT[:, hh, c * C:(c + 1) * C],
                                          identity=self.ident_bf[:])
                    return ins
                P.op("pe", tr, reads=["kdT", "ident_bf"], writes=["ps5"])
                P.op("act", lambda h, c=c: h.activation(out=kd_tm[0:C, c, :], in_=psT[0:C, 0:768], func=AF.Copy),
                     reads=["ps5"], writes=["kd_tm"])
            n16 = 16 + W
            P.op("pool", lambda h: h.tensor_tensor(out=Sa[:, :, 1:n16], in0=px[:, :, 1:n16], in1=px[:, :, 0:n16 - 1], op=ALU.add),
                 reads=["px"], writes=["B2"])
            P.op("pool", lambda h: h.tensor_tensor(out=Sb[64:128, 0, 3:n16], in0=Sa[64:128, 0, 3:n16], in1=Sa[64:128, 0, 1:n16 - 2], op=ALU.add),
                 reads=["B2"], writes=["B2"])
            P.op("pool", lambda h: h.tensor_tensor(out=Sb[:, 1, 3:n16], in0=Sa[:, 1, 3:n16], in1=Sa[:, 1, 1:n16 - 2], op=ALU.add),
                 reads=["B2"], writes=["B2"])
            P.op("pool", lambda h: h.tensor_tensor(out=Sa[:, 1, 7:n16], in0=Sb[:, 1, 7:n16], in1=Sb[:, 1, 3:n16 - 4], op=ALU.add),
                 reads=["B2"], writes=["B2"])
            P.op("pool", lambda h: h.tensor_tensor(out=Sb[64:128, 1, 15:n16], in0=Sa[64:128, 1, 15:n16], in1=Sa[64:128, 1, 7:n16 - 8], op=ALU.add),
                 reads=["B2"], writes=["B2"])
            for g, (src, c, j) in enumerate(((Sa, 0, 0), (Sb, 0, 1), (Sa, 1, 0), (Sb, 1, 1))):
                rng = slice(64 * j, 64 * j + 64)
                if ti == 0:
                    P.op("dve", lambda h, src=src, c=c, rng=rng: h.tensor_tensor(out=src[rng, c, 16:16 + W], in0=src[rng, c, 16:16 + W],
                                                                               in1=self.pfix[rng, c, 0:W], op=ALU.mult),
                         reads=["B2", "B2", "pfix"], writes=["B2", "B2"])
                P.op("dve", lambda h, src=src, c=c, rng=rng: h.scalar_tensor_tensor(
                    out=mixed[rng, c, 0:W], in0=src[rng, c, 16:16 + W], scalar=self.pinv[rng, c:c + 1],
                    in1=px[rng, c, 16:16 + W], op0=ALU.mult, op1=ALU.subtract),
                    reads=["B2", "B2", "px", "pinv"], writes=["kdT"])
            P.op("pool", lambda h: h.tensor_copy(out=px[:, :, 0:16], in_=px[:, :, W:W + 16]), reads=["px"], writes=["px"])
            for c in range(2):
                P.op("pe", lambda h, c=c: h.matmul(ps[7][:, c * W:(c + 1) * W], lhsT=self.pool_bd[:, c, :], rhs=mixed[:, c, 0:W],
                                                   start=True, stop=True),
                     reads=["pool_bd", "kdT"], writes=["ps7"])
                P.op("dve", lambda h, c=c: h.tensor_scalar(out=ycat[:, c, 0:W], in0=ps[7][:, c * W:(c + 1) * W],
                                                          scalar1=pscale[:, c:c + 1], scalar2=None, op0=ALU.mult),
                     reads=["ps7", "small"], writes=["ycat"])
            for c in range(nch):
                cs = c * C
                sb = (3, 4) if c % 2 == 0 else (5, 6)

                def mma(h, cs=cs):
                    ins = None
                    for hh in range(6):
                        ins = h.matmul(ps[0][0:C, hh * C:(hh + 1) * C], lhsT=kb[:, hh, cs:cs + C], rhs=qb[:, hh, cs:cs + C],
                                       start=True, stop=True)
                    return ins
                P.op("pe", mma, reads=["kb", "qb"], writes=["ps0"])

                def mms(h, c=c, sb=sb):
                    ins = None
                    for hh in range(6):
                        bank, off = (sb[0], hh * 128) if hh < 4 else (sb[1], (hh - 4) * 128)
                        ins = h.matmul(ps[bank][:, off:off + 128], lhsT=kd_tm[0:C, c, hh * 128:(hh + 1) * 128],
                                       rhs=v_tm[0:C, c, hh * 128:(hh + 1) * 128], start=True, stop=True)
                    return ins
                P.op("pe", mms, reads=["kd_tm", "v_tm"], writes=[f"ps{sb[0]}", f"ps{sb[1]}"])
                P.op("dve", lambda h: h.tensor_tensor(out=A_sb[0:C, :, 0:C],
                                                      in0=ps[0][0:C, 0:6 * C].rearrange("p (h t) -> p h t", h=6),
                                                      in1=self.cmask[0:C, :, 0:C], op=ALU.mult),
                     reads=["ps0", "cmask"], writes=["A_sb"])
                ob = 1 + (c % 2)

                def mmo(h, cs=cs, c=c, ob=ob, fc=first_chunk):
                    ins = None
                    for hh in range(6):
                        o_ap = ps[ob][:, hh * C:(hh + 1) * C]
                        if not fc:
                            h.matmul(o_ap, lhsT=S_bf[:, hh * 128:(hh + 1) * 128], rhs=qb[:, hh, cs:cs + C],
                                     start=True, stop=False)
                        ins = h.matmul(o_ap, lhsT=v_tm[0:C, c, hh * 128:(hh + 1) * 128], rhs=A_sb[0:C, hh, 0:C],
                                       start=fc, stop=True)
                    return ins
                P.op("pe", mmo, reads=["S_bf", "qb", "v_tm", "A_sb"], writes=[f"ps{ob}"])
                P.op("act", lambda h, cs=cs, ob=ob: h.activation(out=B1[:, :, cs:cs + C],
                                                                 in_=ps[ob][:, 0:6 * C].rearrange("p (h t) -> p h t", h=6),
                                                                 func=AF.Copy),
                     reads=[f"ps{ob}"], writes=["B1"])
                for hh in range(6):
                    bank, off = (sb[0], hh * 128) if hh < 4 else (sb[1], (hh - 4) * 128)
                    P.op("dve", lambda h, hh=hh, bank=bank, off=off, c=c: h.scalar_tensor_tensor(
                        out=S[:, hh * 128:(hh + 1) * 128], in0=S[:, hh * 128:(hh + 1) * 128], scalar=dec[:, hh, c:c + 1],
                        in1=ps[bank][:, off:off + 128], op0=ALU.mult, op1=ALU.add),
                        reads=["S", "dec", f"ps{bank}"], writes=["S"])
                P.op("dve", lambda h: h.tensor_copy(out=S_bf[:], in_=S[:]), reads=["S"], writes=["S_bf"])
                first_chunk = False
            P.op("act", lambda h: h.activation(out=qb[:, :, 0:W], in_=B1[:, :, 0:W], func=AF.Square), reads=["B1"], writes=["qb"])
            for hh in range(6):
                bank = hh % 3
                P.op("pe", lambda h, hh=hh, bank=bank: h.matmul(ps[bank][:, 0:W], lhsT=self.ones_bf[:], rhs=qb[:, hh, 0:W],
                                                                start=True, stop=True),
                     reads=["qb", "ones_bf"], writes=[f"ps{bank}"])
                P.op("act", lambda h, hh=hh, bank=bank: h.activation(out=B2[:, hh, 0:W], in_=ps[bank][:, 0:W], func=AF.Ln,
                                                                     bias=self.eps_t[:], scale=1.0 / 128),
                     reads=[f"ps{bank}", "eps_t"], writes=["B2"])
            P.op("act", lambda h: h.activation(out=B2[:, :, 0:W], in_=B2[:, :, 0:W], func=AF.Exp, scale=-0.5),
                 reads=["B2"], writes=["B2"])
            P.op("dve", lambda h: h.scalar_tensor_tensor(out=B3[:, :, 0:W], in0=B1[:, :, 0:W], scalar=gn[:, 0:1],
                                                         in1=B2[:, :, 0:W], op0=ALU.mult, op1=ALU.mult),
                 reads=["B1", "B2", "small"], writes=["B3"])
            P.op("dve", lambda h: h.tensor_tensor(out=ycat[:, 2:8, 0:W], in0=B3[:, :, 0:W], in1=sgl[:, :, 0:W], op=ALU.mult),
                 reads=["B3", "sgl"], writes=["ycat"])
            g_za = za(ti + 1) if ti + 1 < len(tiles) else iter(())
            g_out = out_gen(s0, W)

            def step(g):
                try:
                    next(g)
                except StopIteration:
                    pass
            for g in (g_za, g_out, g_out, g_za, g_out, g_za, g_out, g_za, g_za, g_za, g_out):
                step(g)

        for ti, (s0, W) in enumerate(tiles):
            tile_body(ti, s0, W)
        P.barrier()
        ar.release(m)

    def mix_even_pipe(self):
        P = self.P
        ar = self.ar
        H = self.H
        ps = self.ps
        m = ar.mark()
        WT = 256
        xn = ar.alloc("xn", [128, 8, WT], BF16)
        wsl = [ar.alloc(f"wsl{i}", [128, 2048], BF16) for i in range(2)]
        wso = [ar.alloc(f"wso{i}", [128, 1024], BF16) for i in range(2)]
        B1 = ar.alloc("B1", [128, 6, WT], F32)
        b1_off = ar.last_off
        B2 = ar.alloc("B2", [128, 6, WT], F32)
        B3 = ar.alloc("B3", [128, 6, WT], F32)
        qb = ar.alloc("qb", [128, 6, WT], BF16)
        kb = ar.alloc("kb", [128, 6, WT], BF16)
        kdT = ar.alloc("kdT", [128, 6, WT], BF16)
        sgl = ar.alloc("sgl", [128, 6, WT], BF16)
        kd_tm = ar.alloc("kd_tm", [64, 4, 768], BF16)
        v_tm = ar.alloc("v_tm", [64, 4, 768], BF16)
        S = ar.alloc("S", [128, 768], F32)
        S_bf = ar.alloc("S_bf", [128, 768], BF16)
        A_sb = ar.alloc("A_sb", [64, 6, 64], BF16)
        dec2 = [ar.alloc(f"dec{i}", [128, 6, 4], F32) for i in range(2)]
        ycat = ar.alloc("ycat", [128, 8, WT], BF16)
        px = ar.alloc("px", [128, 2, 16 + WT], F32)
        Sa = ar.alloc_at("SaB1", [128, 2, 16 + WT], F32, b1_off)
        Sb = ar.alloc_at("SbB1", [128, 2, 16 + WT], F32, b1_off + 2176)
        mixed = kdT[:, 0:2, :]
        o_sb = kdT
        rstd = B3[:, 0, :]
        sqv = B1[:].rearrange("p h w -> p (h w)").bitcast(BF16)[:, 0:8 * WT].rearrange("p (k w) -> p k w", k=8)
        gain = self.sm("mix_norm")[:, 0:8]
        gn = self.sm("gnorm")
        pscale = self.sm("pool_scale")
        lb = self.lbt[:, 0, :]
        oml = self.lbt[:, 1, :]
        noml = self.lbt[:, 2, :]
        win = self.dram["win_e_b"]
        wv = self.dram["wv_e_b"].rearrange("p (k n) -> p k n", k=8)
        wo = self.dram["wout_e_b"]
        tiles = [(0, 16)] + [(16 + 256 * i, 256) for i in range(16)]
        tiles = tiles[:self.cfg.get("mix_tiles", len(tiles))]
        nt = len(tiles)
        loads = []
        for _ in tiles:
            for oc0 in (8, 10, 12, 2, 4, 6):
                loads.append(lambda slot, oc0=oc0: (slot[:, 0:2048].rearrange("p (o x) -> p o x", o=2),
                                                    win[oc0:oc0 + 2].rearrange("o p x -> p o x"), "win_e_b"))
            for vp in range(3):
                loads.append(lambda slot, vp=vp: (slot[:, 0:2048].rearrange("p (k n) -> p k n", k=8),
                                                  wv[:, :, 256 * vp:256 * vp + 256], "wv_e_b"))
            loads.append(lambda slot: (slot[:, 0:2048].rearrange("p (o x) -> p o x", o=2),
                                       win[0:2].rearrange("o p x -> p o x"), "win_e_b"))
            for oc0 in (20, 22, 24):
                loads.append(lambda slot, oc0=oc0: (slot[:, 0:2048].rearrange("p (o x) -> p o x", o=2),
                                                    win[oc0:oc0 + 2].rearrange("o p x -> p o x"), "win_e_b"))
        ws = self.wstream(loads, wsl, ["wsl0", "wsl1"], ["dw0", "dw1"])
        oloads = []
        for _ in tiles:
            for e in range(8):
                oloads.append(lambda slot, e=e: (slot[:, 0:1024].rearrange("p (c m) -> p c m", c=8),
                                                 wo[e // 2].rearrange("p (c m) -> p c m", c=8)[:, :, 128 * (e % 2):128 * (e % 2) + 128],
                                                 "wout_e_b"))
        wsO = self.wstream(oloads, wso, ["wso0", "wso1"], ["dwo0", "dwo1"])
        P.op("pool", lambda h: h.memset(S[:], 0.0), writes=["S"])
        P.op("pool", lambda h: h.memset(S_bf[:], 0.0), writes=["S_bf"])
        P.op("pool", lambda h: h.memset(px[:, :, 0:16], 0.0), writes=["px"])
        xb = [0]

        def bc_last(ap2, n):
            return bass.AP(ap2.tensor, ap2.offset, [list(ap2.ap[0]), list(ap2.ap[1]), [0, n]])

        def fm_piece(W, dest_fn):
            s = self.wacquire(ws)
            wvw = wsl[s][:, 0:2048].rearrange("p (o k m) -> p o k m", o=2, k=8)
            bank = 5 + xb[0] % 3
            xb[0] += 1
            for i in range(2):
                pap = ps[bank][:, i * 256:i * 256 + W]

                def mm(h, i=i, pap=pap, wvw=wvw):
                    ins = None
                    for kc in range(8):
                        ins = h.matmul(pap, lhsT=wvw[:, i, kc, :], rhs=xn[:, kc, 0:W], start=(kc == 0), stop=(kc == 7))
                    return ins
                P.op("pe", mm, reads=[f"wsl{s}", "xn"], writes=[f"ps{bank}"])
                dest_fn(i, pap, f"ps{bank}")

        def XA(ti):
            s0, W = tiles[ti]
            C = 16 if W == 16 else 64
            nch = W // C
            dec = dec2[ti % 2]
            dk = f"dec{ti % 2}"
            self.rmsnorm(s0, W, 0, xn, "xn", sqv, "B1", rstd, "B3", ps[7], "ps7", gain)
            yield
            for pc in range(3):
                def dz(i, pap, pk, pc=pc):
                    hh = 2 * pc + i
                    P.op("act", lambda h: h.activation(out=B1[:, hh, 0:W], in_=pap, func=AF.Sigmoid), reads=[pk], writes=["B1"])
                fm_piece(W, dz)
                yield
            for hh in range(6):
                P.op("dve", lambda h, hh=hh: h.tensor_scalar(out=B2[:, hh, 0:W], in0=B1[:, hh, 0:W], scalar1=oml[:, hh:hh + 1],
                                                            scalar2=lb[:, hh:hh + 1], op0=ALU.mult, op1=ALU.add),
                     reads=["B1", "lbt"], writes=["B2"])
                P.op("dve", lambda h, hh=hh: h.tensor_scalar(out=B1[:, hh, 0:W], in0=B1[:, hh, 0:W], scalar1=noml[:, hh:hh + 1],
                                                            scalar2=oml[:, hh:hh + 1], op0=ALU.mult, op1=ALU.add),
                     reads=["B1", "lbt"], writes=["B1"])
            yield
            P.op("act", lambda h: h.activation(out=B2[:, :, 0:W], in_=B2[:, :, 0:W], func=AF.Ln), reads=["B2"], writes=["B2"])
            for hh in range(6):
                P.op("dve", lambda h, hh=hh: h.tensor_tensor_scan(out=B3[:, hh, 0:W], data0=self.rmask[:, 0:W], data1=B2[:, hh, 0:W],
                                                                 initial=0.0, op0=ALU.mult, op1=ALU.add),
                     reads=["B2", "rmask"], writes=["B3"])
            yield
            if nch == 1:
                b_last = B3[:, :, W - 1:W]
                dec_v = dec[:, :, 0:1]
            else:
                b_last = B3[:, :, 0:W].rearrange("p h (c s) -> p h c s", s=C)[:, :, :, C - 1]
                dec_v = dec[:, :, 0:nch]
            P.op("act", lambda h: h.activation(out=dec_v, in_=b_last, func=AF.Exp), reads=["B3"], writes=[dk])
            P.op("act", lambda h: h.activation(out=B2[:, :, 0:W], in_=B3[:, :, 0:W], func=AF.Exp), reads=["B3"], writes=["B2"])
            P.op("act", lambda h: h.activation(out=B3[:, :, 0:W], in_=B3[:, :, 0:W], func=AF.Exp, scale=-1.0),
                 reads=["B3"], writes=["B3"])
            yield

        def XB(ti):
            s0, W = tiles[ti]
            C = 16 if W == 16 else 64
            nch = W // C
            dec = dec2[ti % 2]
            dk = f"dec{ti % 2}"
            P.op("dve", lambda h: h.tensor_tensor(out=kb[:, :, 0:W], in0=B1[:, :, 0:W], in1=B3[:, :, 0:W], op=ALU.mult),
                 reads=["B1", "B3"], writes=["kb"])
            for pc in range(3):
                def dq(i, pap, pk, pc=pc):
                    hh = 2 * pc + i
                    P.op("act", lambda h: h.activation(out=B1[:, hh, 0:W], in_=pap, func=AF.Silu), reads=[pk], writes=["B1"])
                fm_piece(W, dq)
                yield
            P.op("dve", lambda h: h.tensor_tensor(out=qb[:, :, 0:W], in0=B1[:, :, 0:W], in1=B2[:, :, 0:W], op=ALU.mult),
                 reads=["B1", "B2"], writes=["qb"])
            for vp in range(3):
                s = self.wacquire(ws)
                wvv = wsl[s][:, 0:2048].rearrange("p (k n) -> p k n", k=8)
                for c in range(nch):
                    bank = 5 + xb[0] % 3
                    xb[0] += 1

                    def mmv(h, c=c, bank=bank, wvv=wvv):
                        ins = None
                        for kc in range(8):
                            ins = h.matmul(ps[bank][0:C, 0:256], lhsT=xn[:, kc, c * C:(c + 1) * C], rhs=wvv[:, kc, :],
                                           start=(kc == 0), stop=(kc == 7))
                        return ins
                    P.op("pe", mmv, reads=[f"wsl{s}", "xn"], writes=[f"ps{bank}"])
                    eng = "act" if (c % 2 == 0) else "dve"
                    if eng == "act":
                        P.op("act", lambda h, c=c, bank=bank, vp=vp: h.activation(out=v_tm[0:C, c, 256 * vp:256 * vp + 256],
                                                                                  in_=ps[bank][0:C, 0:256], func=AF.Copy),
                             reads=[f"ps{bank}"], writes=["v_tm"])
                    else:
                        P.op("dve", lambda h, c=c, bank=bank, vp=vp: h.tensor_copy(out=v_tm[0:C, c, 256 * vp:256 * vp + 256],
                                                                                   in_=ps[bank][0:C, 0:256]),
                             reads=[f"ps{bank}"], writes=["v_tm"])
                yield
            if nch == 1:
                kd_out = kdT[:, :, 0:W]
                kd_in = kb[:, :, 0:W]
                dec_b = bc_last(dec[:, :, 0], W)
            else:
                kd_out = kdT[:, :, 0:W].rearrange("p h (c s) -> p (h c) s", s=C)
                kd_in = kb[:, :, 0:W].rearrange("p h (c s) -> p (h c) s", s=C)
                dec_b = bc_last(dec[:].rearrange("p h c -> p (h c)"), C)
            P.op("dve", lambda h: h.tensor_tensor(out=kd_out, in0=kd_in, in1=dec_b, op=ALU.mult),
                 reads=["kb", dk], writes=["kdT"])
            psT = ps[5][:].bitcast(BF16)
            for c in range(nch):
                def tr(h, c=c):
                    ins = None
                    for hh in range(6):
                        ins = h.transpose(out=psT[0:C, hh * 128:(hh + 1) * 128], in_=kdT[:, hh, c * C:(c + 1) * C],
                                          identity=self.ident_bf[:])
                    return ins
                P.op("pe", tr, reads=["kdT", "ident_bf"], writes=["ps5"])
                P.op("act", lambda h, c=c: h.activation(out=kd_tm[0:C, c, :], in_=psT[0:C, 0:768], func=AF.Copy),
                     reads=["ps5"], writes=["kd_tm"])
            yield
            def dp(i, pap, pk):
                P.op("act", lambda h: h.activation(out=px[:, i, 16:16 + W], in_=pap, func=AF.Copy), reads=[pk], writes=["px"])
            fm_piece(W, dp)
            n16 = 16 + W
            P.op("pool", lambda h: h.tensor_tensor(out=Sa[:, :, 1:n16], in0=px[:, :, 1:n16], in1=px[:, :, 0:n16 - 1], op=ALU.add),
                 reads=["px"], writes=["B1"])
            P.op("pool", lambda h: h.tensor_tensor(out=Sb[64:128, 0, 3:n16], in0=Sa[64:128, 0, 3:n16], in1=Sa[64:128, 0, 1:n16 - 2], op=ALU.add),
                 reads=["B1"], writes=["B1"])
            P.op("pool", lambda h: h.tensor_tensor(out=Sb[:, 1, 3:n16], in0=Sa[:, 1, 3:n16], in1=Sa[:, 1, 1:n16 - 2], op=ALU.add),
                 reads=["B1"], writes=["B1"])
            P.op("pool", lambda h: h.tensor_tensor(out=Sa[:, 1, 7:n16], in0=Sb[:, 1, 7:n16], in1=Sb[:, 1, 3:n16 - 4], op=ALU.add),
                 reads=["B1"], writes=["B1"])
            P.op("pool", lambda h: h.tensor_tensor(out=Sb[64:128, 1, 15:n16], in0=Sa[64:128, 1, 15:n16], in1=Sa[64:128, 1, 7:n16 - 8], op=ALU.add),
                 reads=["B1"], writes=["B1"])
            for g, (src, c, j) in enumerate(((Sa, 0, 0), (Sb, 0, 1), (Sa, 1, 0), (Sb, 1, 1))):
                rng = slice(64 * j, 64 * j + 64)
                if ti == 0:
                    P.op("dve", lambda h, src=src, c=c, rng=rng: h.tensor_tensor(out=src[rng, c, 16:16 + W], in0=src[rng, c, 16:16 + W],
                                                                               in1=self.pfix[rng, c, 0:W], op=ALU.mult),
                         reads=["B1", "pfix"], writes=["B1"])
                P.op("dve", lambda h, src=src, c=c, rng=rng: h.scalar_tensor_tensor(
                    out=mixed[rng, c, 0:W], in0=src[rng, c, 16:16 + W], scalar=self.pinv[rng, c:c + 1],
                    in1=px[rng, c, 16:16 + W], op0=ALU.mult, op1=ALU.subtract),
                    reads=["B1", "px", "pinv"], writes=["kdT"])
            P.op("pool", lambda h: h.tensor_copy(out=px[:, :, 0:16], in_=px[:, :, W:W + 16]), reads=["px"], writes=["px"])
            for c in range(2):
                P.op("pe", lambda h, c=c: h.matmul(ps[7][:, c * 256:c * 256 + W], lhsT=self.pool_bd[:, c, :], rhs=mixed[:, c, 0:W],
                                                   start=True, stop=True),
                     reads=["pool_bd", "kdT"], writes=["ps7"])
                P.op("dve", lambda h, c=c: h.tensor_scalar(out=ycat[:, c, 0:W], in0=ps[7][:, c * 256:c * 256 + W],
                                                          scalar1=pscale[:, c:c + 1], scalar2=None, op0=ALU.mult),
                     reads=["ps7", "small"], writes=["ycat"])
            yield
            for pc in range(3):
                def dg(i, pap, pk, pc=pc):
                    hh = 2 * pc + i
                    P.op("act", lambda h: h.activation(out=sgl[:, hh, 0:W], in_=pap, func=AF.Silu), reads=[pk], writes=["sgl"])
                fm_piece(W, dg)
                yield

        first = [True]

        def Y1(ti):
            s0, W = tiles[ti]
            C = 16 if W == 16 else 64
            nch = W // C
            dec = dec2[ti % 2]
            dk = f"dec{ti % 2}"
            for c in range(nch):
                cs = c * C

                def mma(h, cs=cs):
                    ins = None
                    for hh in range(6):
                        ins = h.matmul(ps[0][0:C, hh * C:(hh + 1) * C], lhsT=kb[:, hh, cs:cs + C], rhs=qb[:, hh, cs:cs + C],
                                       start=True, stop=True)
                    return ins
                P.op("pe", mma, reads=["kb", "qb"], writes=["ps0"])
                P.op("dve", lambda h: h.tensor_tensor(out=A_sb[0:C, :, 0:C],
                                                      in0=ps[0][0:C, 0:6 * C].rearrange("p (h t) -> p h t", h=6),
                                                      in1=self.cmask[0:C, :, 0:C], op=ALU.mult),
                     reads=["ps0", "cmask"], writes=["A_sb"])
                ob = 1 + (c % 2)
                fc = first[0]

                def mmo(h, cs=cs, c=c, ob=ob, fc=fc):
                    ins = None
                    for hh in range(6):
                        o_ap = ps[ob][:, hh * C:(hh + 1) * C]
                        if not fc:
                            h.matmul(o_ap, lhsT=S_bf[:, hh * 128:(hh + 1) * 128], rhs=qb[:, hh, cs:cs + C],
                                     start=True, stop=False)
                        ins = h.matmul(o_ap, lhsT=v_tm[0:C, c, hh * 128:(hh + 1) * 128], rhs=A_sb[0:C, hh, 0:C],
                                       start=fc, stop=True)
                    return ins
                P.op("pe", mmo, reads=["S_bf", "qb", "v_tm", "A_sb"], writes=[f"ps{ob}"])
                P.op("act", lambda h, cs=cs, ob=ob: h.activation(out=o_sb[:, :, cs:cs + C],
                                                                 in_=ps[ob][:, 0:6 * C].rearrange("p (h t) -> p h t", h=6),
                                                                 func=AF.Copy),
                     reads=[f"ps{ob}"], writes=["kdT"])

                def mms(h, c=c):
                    ins = None
                    for hh in range(6):
                        bank, off = (3, hh * 128) if hh < 4 else (4, (hh - 4) * 128)
                        ins = h.matmul(ps[bank][:, off:off + 128], lhsT=kd_tm[0:C, c, hh * 128:(hh + 1) * 128],
                                       rhs=v_tm[0:C, c, hh * 128:(hh + 1) * 128], start=True, stop=True)
                    return ins
                P.op("pe", mms, reads=["kd_tm", "v_tm"], writes=["ps3", "ps4"])
                for hh in range(6):
                    bank, off = (3, hh * 128) if hh < 4 else (4, (hh - 4) * 128)
                    P.op("dve", lambda h, hh=hh, bank=bank, off=off, c=c: h.scalar_tensor_tensor(
                        out=S[:, hh * 128:(hh + 1) * 128], in0=S[:, hh * 128:(hh + 1) * 128], scalar=dec[:, hh, c:c + 1],
                        in1=ps[bank][:, off:off + 128], op0=ALU.mult, op1=ALU.add),
                        reads=["S", dk, f"ps{bank}"], writes=["S"])
                P.op("dve", lambda h: h.tensor_copy(out=S_bf[:], in_=S[:]), reads=["S"], writes=["S_bf"])
                first[0] = False
                yield

        def Y2(ti):
            s0, W = tiles[ti]
            P.op("act", lambda h: h.activation(out=ycat[:, 2:8, 0:W], in_=o_sb[:, :, 0:W], func=AF.Square), reads=["kdT"], writes=["ycat"])
            for hp in range(3):
                for i in range(2):
                    hh = 2 * hp + i
                    P.op("pe", lambda h, hh=hh, hp=hp, i=i: h.matmul(ps[hp][:, i * 256:i * 256 + W], lhsT=self.ones_bf[:],
                                                                     rhs=ycat[:, 2 + hh, 0:W], start=True, stop=True),
                         reads=["ycat", "ones_bf"], writes=[f"ps{hp}"])
                rs = ps[hp][:, 0:512].rearrange("p (i w) -> p i w", i=2)[:, :, 0:W]
                P.op("act", lambda h, rs=rs: h.activation(out=rs, in_=rs, func=AF.Ln, bias=self.eps_t[:], scale=1.0 / 128),
                     reads=[f"ps{hp}", "eps_t"], writes=[f"ps{hp}"])
                P.op("act", lambda h, rs=rs: h.activation(out=rs, in_=rs, func=AF.Exp, scale=-0.5), reads=[f"ps{hp}"], writes=[f"ps{hp}"])
                P.op("dve", lambda h, rs=rs, hp=hp: h.scalar_tensor_tensor(out=o_sb[:, 2 * hp:2 * hp + 2, 0:W], in0=o_sb[:, 2 * hp:2 * hp + 2, 0:W],
                                                                          scalar=gn[:, 0:1], in1=rs, op0=ALU.mult, op1=ALU.mult),
                     reads=["kdT", f"ps{hp}", "small"], writes=["kdT"])
            P.op("dve", lambda h: h.tensor_tensor(out=ycat[:, 2:8, 0:W], in0=o_sb[:, :, 0:W], in1=sgl[:, :, 0:W], op=ALU.mult),
                 reads=["kdT", "sgl"], writes=["ycat"])
            yield
            hk = hkeys(s0, W)
            for dc in range(8):
                s = self.wacquire(wsO)
                wq = wso[s][:, 0:1024].rearrange("p (c m) -> p c m", c=8)
                bank = 3 + (dc % 2)

                def mm(h, bank=bank, wq=wq):
                    ins = None
                    for cc in range(8):
                        ins = h.matmul(ps[bank][:, 0:W], lhsT=wq[:, cc, :], rhs=ycat[:, cc, 0:W], start=(cc == 0), stop=(cc == 7))
                    return ins
                P.op("pe", mm, reads=[f"wso{s}", "ycat"], writes=[f"ps{bank}"])
                P.op("dve", lambda h, dc=dc, bank=bank: h.tensor_tensor(out=H[:, dc, s0:s0 + W], in0=ps[bank][:, 0:W],
                                                                        in1=H[:, dc, s0:s0 + W], op=ALU.add),
                     reads=[f"ps{bank}"] + hk, writes=hk)
                if dc % 2 == 1:
                    yield

        for _ in XA(0):
            pass
        for _ in XB(0):
            pass
        for ti in range(nt):
            self.interleave(Y1(ti), XA(ti + 1) if ti + 1 < nt else None)
            self.interleave(Y2(ti), XB(ti + 1) if ti + 1 < nt else None)
        P.barrier()
        ar.release(m)


    def out_proj(self, ws, wsl, ycat, s0, W):
        P = self.P
        H = self.H
        ps = self.ps
        hk = hkeys(s0, W)
        for q in range(4):
            s = self.wacquire(ws)
            wq = wsl[s][:, 0:2048].rearrange("p (c m) -> p c m", c=8)
            for dj in range(2):
                dc = 2 * q + dj
                bank = 3 + (dc % 3)

                def mm(h, dj=dj, bank=bank, wq=wq):
                    ins = None
                    for cc in range(8):
                        ins = h.matmul(ps[bank][:, 0:W], lhsT=wq[:, cc, dj * 128:(dj + 1) * 128], rhs=ycat[:, cc, 0:W],
                                       start=(cc == 0), stop=(cc == 7))
                    return ins
                P.op("pe", mm, reads=[f"wsl{s}", "ycat"], writes=[f"ps{bank}"])
                P.op("dve", lambda h, dc=dc, bank=bank: h.tensor_tensor(out=H[:, dc, s0:s0 + W], in0=ps[bank][:, 0:W],
                                                                        in1=H[:, dc, s0:s0 + W], op=ALU.add),
                     reads=[f"ps{bank}"] + hk, writes=hk)


    def setup_odd(self):
        P = self.P
        ar = self.ar
        self.onesf = ar.alloc("onesf", [128, 128], F32)
        self.one_t = ar.alloc("one_t", [128, 1], F32)
        self.wa_bd = ar.alloc("wa_bd", [128, 4, 128], BF16)
        self.wx_bd = ar.alloc("wx_bd", [128, 4, 128], BF16)
        self.clru = ar.alloc("clru", [128, 2, 4], F32)
        m = ar.mark()
        stg = ar.alloc("stg2", [128, 4, 128], F32)
        P.op("dve", lambda h: h.memset(self.onesf[:], 1.0 / 128), writes=["onesf"])
        P.op("dve", lambda h: h.memset(self.one_t[:], 1.0), writes=["one_t"])
        for name, dst in (("lru_wa", self.wa_bd), ("lru_wx", self.wx_bd)):
            P.op("dve", lambda h: h.memset(stg[:], 0.0), writes=["stg2"])
            for hd in range(8):
                cc, j = divmod(hd, 2)
                self.dma("sp", stg[64 * j:64 * j + 64, cc, 64 * j:64 * j + 64], self.dram[name][hd],
                         reads=[], writes=["stg2"], sem="d_small")
            P.op("dve", lambda h, dst=dst: h.tensor_copy(out=dst[:], in_=stg[:]), reads=["stg2"], writes=[name + "_bd"])
        lam = self.sm("lru_lambda")
        P.op("act", lambda h: h.activation(out=self.clru[:, 0, :], in_=lam, func=AF.Exp, scale=-1.0),
             reads=["small"], writes=["clru"])
        P.op("act", lambda h: h.activation(out=self.clru[:, 0, :], in_=self.clru[:, 0, :], func=AF.Ln, bias=self.one_t[:]),
             reads=["clru", "one_t"], writes=["clru"])
        P.op("dve", lambda h: h.tensor_scalar(out=self.clru[:, 1, :], in0=self.clru[:, 0, :], scalar1=-16.0, scalar2=None,
                                              op0=ALU.mult), reads=["clru"], writes=["clru"])
        P.op("dve", lambda h: h.tensor_scalar(out=self.clru[:, 0, :], in0=self.clru[:, 0, :], scalar1=-8.0, scalar2=None,
                                              op0=ALU.mult), reads=["clru"], writes=["clru"])
        P.barrier()
        ar.release(m)

    def mix_odd_v1(self):
        P = self.P
        ar = self.ar
        H = self.H
        ps = self.ps
        m = ar.mark()
        WT = 256
        xn = ar.alloc("xn", [128, 8, WT], BF16)
        wsl = [ar.alloc(f"wsl{i}", [128, 3072], BF16) for i in range(2)]
        dsl = [ar.alloc(f"dsl{i}", [128, 31, 128], BF16) for i in range(2)]
        ubuf = [ar.alloc(f"ubuf{i}", [128, 4, 30 + WT], BF16) for i in range(2)]
        xl = [ar.alloc(f"xl{i}", [128, 4, 3 + WT], F32) for i in range(2)]
        ulb = ar.alloc("ulb", [128, 4, WT], BF16)
        Ta = ar.alloc("Ta", [128, 4, WT], F32)
        Tb = ar.alloc("Tb", [128, 4, WT], F32)
        Tc = ar.alloc("Tc", [128, 4, WT], F32)
        Td = ar.alloc("Td", [128, 4, WT], F32)
        Te = ar.alloc("Te", [128, 4, WT], F32)
        ycat = ar.alloc("ycat", [128, 8, WT], BF16)
        hcar = ar.alloc("hcar", [128, 4], F32)
        rstd = Te[:, 0, :]
        sqv = Ta[:].rearrange("p h w -> p (h w)").bitcast(BF16)[:, 0:8 * WT].rearrange("p (k w) -> p k w", k=8)
        gain = self.sm("mix_norm")[:, 8:16]
        cw = self.sm("conv_w")
        cb = self.sm("conv_b")
        lng = self.sm("conv_ln_g")
        lnb = self.sm("conv_ln_b")
        lw = self.sm("lru_conv_w")
        lbias = self.sm("lru_conv_b")
        ba = self.sm("lru_ba")
        bx = self.sm("lru_bx")
        c1 = self.clru[:, 0, :]
        c2 = self.clru[:, 1, :]
        win = self.dram["win_o_b"]
        wo = self.dram["wout_o_b"]
        diag = self.dram["diag_b"]
        self.sem("d_diag")
        for cc in range(4):
            for j in range(31):
                P.op("dve", lambda h, cc=cc, j=j: h.tensor_scalar(out=dsl[0][:, j, :], in0=self.ident_bf[:],
                                                                 scalar1=cw[:, j * 4 + cc:j * 4 + cc + 1], scalar2=None, op0=ALU.mult),
                     reads=["ident_bf", "small"], writes=["dsl0"])
            self.dma("sp", diag[cc], dsl[0][:].rearrange("p j m -> p (j m)"), reads=["dsl0"], writes=["diag_b"], sem="d_diag")
        tiles = [(0, 16)] + [(16 + 256 * i, 256) for i in range(16)]
        tiles = tiles[:self.cfg.get("mix_tiles", len(tiles))]
        loads = []
        for _ in tiles:
            for oc0 in (4, 6, 0, 2, 8, 10, 12, 14):
                loads.append(lambda slot, oc0=oc0: (slot[:, 0:2048].rearrange("p (o x) -> p o x", o=2),
                                                    win[oc0:oc0 + 2].rearrange("o p x -> p o x"), "win_o_b"))
            for q in range(4):
                loads.append(lambda slot, q=q: (slot[:, 0:2048], wo[q], "wout_o_b"))
        ws = self.wstream(loads, wsl, ["wsl0", "wsl1"], ["dw0", "dw1"])
        dloads = []
        for _ in tiles:
            for cc in range(4):
                dloads.append(lambda slot, cc=cc: (slot[:].rearrange("p j m -> p (j m)"), diag[cc], "diag_b"))
        dstream = self.wstream(dloads, dsl, ["dsl0", "dsl1"], ["dd0", "dd1"])
        P.op("pool", lambda h: h.memset(ubuf[0][:, :, 0:30], 0.0), writes=["ubuf0"])
        P.op("pool", lambda h: h.memset(xl[0][:, :, 0:3], 0.0), writes=["xl0"])
        P.op("pool", lambda h: h.memset(hcar[:], 0.0), writes=["hcar"])
        bankrot = 0

        def tile_body(ti, s0, W):
            nonlocal bankrot
            ub, ubn = ubuf[ti % 2], ubuf[(ti + 1) % 2]
            ubk, ubnk = f"ubuf{ti % 2}", f"ubuf{(ti + 1) % 2}"
            xc, xcn = xl[ti % 2], xl[(ti + 1) % 2]
            xck, xcnk = f"xl{ti % 2}", f"xl{(ti + 1) % 2}"
            self.rmsnorm(s0, W, 0, xn, "xn", sqv, "Ta", rstd, "Te", ps[6], "ps6", gain)

            def fm_piece(dest_fn):
                nonlocal bankrot
                s = self.wacquire(ws)
                wvw = wsl[s][:, 0:2048].rearrange("p (o k m) -> p o k m", o=2, k=8)
                for i in range(2):
                    bank = bankrot % 4
                    bankrot += 1
                    pt = ps[bank]

                    def mm(h, i=i, pt=pt, wvw=wvw):
                        ins = None
                        for kc in range(8):
                            ins = h.matmul(pt[:, 0:W], lhsT=wvw[:, i, kc, :], rhs=xn[:, kc, 0:W],
                                           start=(kc == 0), stop=(kc == 7))
                        return ins
                    P.op("pe", mm, reads=[f"wsl{s}", "xn"], writes=[f"ps{bank}"])
                    dest_fn(i, pt[:, 0:W], f"ps{bank}")

            for pc in range(2):
                def db(i, pap, pk, pc=pc):
                    cc = 2 * pc + i
                    P.op("act", lambda h: h.activation(out=Ta[:, cc, 0:W], in_=pap, func=AF.Sigmoid), reads=[pk], writes=["Ta"])
                fm_piece(db)
            for pc in range(2):
                def da(i, pap, pk, pc=pc):
                    cc = 2 * pc + i
                    P.op("dve", lambda h: h.tensor_tensor(out=ub[:, cc, 30:30 + W], in0=Ta[:, cc, 0:W], in1=pap, op=ALU.mult),
                         reads=[pk, "Ta"], writes=[ubk])
                fm_piece(da)
            P.op("pool", lambda h: h.tensor_copy(out=ubn[:, :, 0:30], in_=ub[:, :, W:W + 30]), reads=[ubk], writes=[ubnk])
            for pc in range(2):
                def dx(i, pap, pk, pc=pc):
                    cc = 2 * pc + i
                    P.op("act", lambda h: h.activation(out=xc[:, cc, 3:3 + W], in_=pap, func=AF.Copy), reads=[pk], writes=[xck])
                fm_piece(dx)
            P.op("pool", lambda h: h.tensor_copy(out=xcn[:, :, 0:3], in_=xc[:, :, W:W + 3]), reads=[xck], writes=[xcnk])
            for pc in range(2):
                def dgf(i, pap, pk, pc=pc):
                    cc = 2 * pc + i
                    P.op("act", lambda h: h.activation(out=Tc[:, cc, 0:W], in_=pap, func=AF.Gelu), reads=[pk], writes=["Tc"])
                fm_piece(dgf)
            for cc in range(4):
                ds = self.wacquire(dstream)
                bank = 4 + (cc % 2)

                def mmc(h, cc=cc, ds=ds, bank=bank):
                    ins = None
                    for j in range(31):
                        ins = h.matmul(ps[bank][:, 0:W], lhsT=dsl[ds][:, j, :], rhs=ub[:, cc, j:j + W],
                                       start=(j == 0), stop=(j == 30))
                    return ins
                P.op("pe", mmc, reads=[f"dsl{ds}", ubk], writes=[f"ps{bank}"])
                P.op("act", lambda h, cc=cc, bank=bank: h.activation(out=Tb[:, cc, 0:W], in_=ps[bank][:, 0:W], func=AF.Identity,
                                                                     bias=cb[:, cc:cc + 1]),
                     reads=[f"ps{bank}", "small"], writes=["Tb"])
            for cc in range(4):
                bank = 6 + (cc % 2)
                P.op("pe", lambda h, cc=cc, bank=bank: h.matmul(ps[bank][:, 0:W], lhsT=self.onesf[:], rhs=Tb[:, cc, 0:W],
                                                                start=True, stop=True),
                     reads=["onesf", "Tb"], writes=[f"ps{bank}"])
                P.op("dve", lambda h, cc=cc, bank=bank: h.tensor_tensor(out=Tb[:, cc, 0:W], in0=Tb[:, cc, 0:W], in1=ps[bank][:, 0:W],
                                                                        op=ALU.subtract),
                     reads=["Tb", f"ps{bank}"], writes=["Tb"])
                P.op("act", lambda h, cc=cc: h.activation(out=Ta[:, cc, 0:W], in_=Tb[:, cc, 0:W], func=AF.Square),
                     reads=["Tb"], writes=["Ta"])
                P.op("pe", lambda h, cc=cc, bank=bank: h.matmul(ps[bank][:, 0:W], lhsT=self.onesf[:], rhs=Ta[:, cc, 0:W],
                                                                start=True, stop=True),
                     reads=["onesf", "Ta"], writes=[f"ps{bank}"])
                P.op("act", lambda h, cc=cc, bank=bank: h.activation(out=Ta[:, cc, 0:W], in_=ps[bank][:, 0:W], func=AF.Ln,
                                                                     bias=self.eps_t[:]),
                     reads=[f"ps{bank}", "eps_t"], writes=["Ta"])
            P.op("act", lambda h: h.activation(out=Ta[:, :, 0:W], in_=Ta[:, :, 0:W], func=AF.Exp, scale=-0.5), reads=["Ta"], writes=["Ta"])
            P.op("dve", lambda h: h.tensor_tensor(out=Tb[:, :, 0:W], in0=Tb[:, :, 0:W], in1=Ta[:, :, 0:W], op=ALU.mult),
                 reads=["Ta", "Tb"], writes=["Tb"])
            for cc in range(4):
                P.op("dve", lambda h, cc=cc: h.tensor_scalar(out=Tb[:, cc, 0:W], in0=Tb[:, cc, 0:W], scalar1=lng[:, cc:cc + 1],
                                                            scalar2=lnb[:, cc:cc + 1], op0=ALU.mult, op1=ALU.add),
                     reads=["Tb", "small"], writes=["Tb"])
            P.op("act", lambda h: h.activation(out=ycat[:, 0:4, 0:W], in_=Tb[:, :, 0:W], func=AF.Silu), reads=["Tb"], writes=["ycat"])
            for cc in range(4):
                P.op("dve", lambda h, cc=cc: h.tensor_scalar(out=Td[:, cc, 0:W], in0=xc[:, cc, 3:3 + W], scalar1=lw[:, 12 + cc:13 + cc],
                                                            scalar2=lbias[:, cc:cc + 1], op0=ALU.mult, op1=ALU.add),
                     reads=[xck, "small"], writes=["Td"])
                for j in range(3):
                    P.op("dve", lambda h, cc=cc, j=j: h.scalar_tensor_tensor(out=Td[:, cc, 0:W], in0=xc[:, cc, j:j + W],
                                                                            scalar=lw[:, 4 * j + cc:4 * j + cc + 1], in1=Td[:, cc, 0:W],
                                                                            op0=ALU.mult, op1=ALU.add),
                         reads=[xck, "small", "Td"], writes=["Td"])
            P.op("pool", lambda h: h.tensor_copy(out=ulb[:, :, 0:W], in_=Td[:, :, 0:W]), reads=["Td"], writes=["ulb"])
            for cc in range(4):
                bank = cc % 2
                P.op("pe", lambda h, cc=cc, bank=bank: h.matmul(ps[bank][:, 0:W], lhsT=self.wa_bd[:, cc, :], rhs=ulb[:, cc, 0:W],
                                                                start=True, stop=True),
                     reads=["lru_wa_bd", "ulb"], writes=[f"ps{bank}"])
                P.op("act", lambda h, cc=cc, bank=bank: h.activation(out=Ta[:, cc, 0:W], in_=ps[bank][:, 0:W], func=AF.Sigmoid,
                                                                     bias=ba[:, cc:cc + 1]),
                     reads=[f"ps{bank}", "small"], writes=["Ta"])
                P.op("pe", lambda h, cc=cc, bank=bank: h.matmul(ps[2 + bank][:, 0:W], lhsT=self.wx_bd[:, cc, :], rhs=ulb[:, cc, 0:W],
                                                                start=True, stop=True),
                     reads=["lru_wx_bd", "ulb"], writes=[f"ps{2 + bank}"])
                P.op("act", lambda h, cc=cc, bank=bank: h.activation(out=Tb[:, cc, 0:W], in_=ps[2 + bank][:, 0:W], func=AF.Sigmoid,
                                                                     bias=bx[:, cc:cc + 1]),
                     reads=[f"ps{2 + bank}", "small"], writes=["Tb"])
            for cc in range(4):
                P.op("act", lambda h, cc=cc: h.activation(out=Te[:, cc, 0:W], in_=Ta[:, cc, 0:W], func=AF.Exp, scale=c1[:, cc:cc + 1]),
                     reads=["Ta", "clru"], writes=["Te"])
                P.op("act", lambda h, cc=cc: h.activation(out=Ta[:, cc, 0:W], in_=Ta[:, cc, 0:W], func=AF.Exp, scale=c2[:, cc:cc + 1]),
                     reads=["Ta", "clru"], writes=["Ta"])
            P.op("act", lambda h: h.activation(out=Ta[:, :, 0:W], in_=Ta[:, :, 0:W], func=AF.Ln, bias=self.one_t[:], scale=-1.0),
                 reads=["Ta", "one_t"], writes=["Ta"])
            P.op("act", lambda h: h.activation(out=Ta[:, :, 0:W], in_=Ta[:, :, 0:W], func=AF.Exp, scale=0.5), reads=["Ta"], writes=["Ta"])
            if ti == 0:
                P.op("dve", lambda h: h.memset(Ta[:, :, 0:1], 1.0), writes=["Ta"])
            P.op("dve", lambda h: h.tensor_tensor(out=Tb[:, :, 0:W], in0=Tb[:, :, 0:W], in1=Td[:, :, 0:W], op=ALU.mult),
                 reads=["Tb", "Td"], writes=["Tb"])
            P.op("dve", lambda h: h.tensor_tensor(out=Tb[:, :, 0:W], in0=Tb[:, :, 0:W], in1=Ta[:, :, 0:W], op=ALU.mult),
                 reads=["Tb", "Ta"], writes=["Tb"])
            for cc in range(4):
                P.op("dve", lambda h, cc=cc: h.tensor_tensor_scan(out=Td[:, cc, 0:W], data0=Te[:, cc, 0:W], data1=Tb[:, cc, 0:W],
                                                                 initial=hcar[:, cc:cc + 1], op0=ALU.mult, op1=ALU.add),
                     reads=["Te", "Tb", "hcar"], writes=["Td"])
            P.op("dve", lambda h: h.tensor_copy(out=hcar[:], in_=Td[:, :, W - 1]), reads=["Td"], writes=["hcar"])
            P.op("dve", lambda h: h.tensor_tensor(out=ycat[:, 4:8, 0:W], in0=Tc[:, :, 0:W], in1=Td[:, :, 0:W], op=ALU.mult),
                 reads=["Tc", "Td"], writes=["ycat"])
            self.out_proj(ws, wsl, ycat, s0, W)

        for ti, (s0, W) in enumerate(tiles):
            tile_body(ti, s0, W)
        P.barrier()
        ar.release(m)

    @staticmethod
    def interleave(ga, gb):
        done_a = done_b = False
        while not (done_a and done_b):
            if not done_a:
                try:
                    next(ga)
                except StopIteration:
                    done_a = True
            if not done_b and gb is not None:
                try:
                    next(gb)
                except StopIteration:
                    done_b = True
            if gb is None:
                done_b = True

    def mix_odd(self):
        P = self.P
        ar = self.ar
        H = self.H
        ps = self.ps
        m = ar.mark()
        WT = 256
        xn = ar.alloc("xn", [128, 8, WT], BF16)
        wsl = [ar.alloc(f"wsl{i}", [128, 2048], BF16) for i in range(2)]
        wso = [ar.alloc(f"wso{i}", [128, 1024], BF16) for i in range(2)]
        dsl = [ar.alloc(f"dsl{i}", [128, 31, 128], BF16) for i in range(2)]
        ubuf = [ar.alloc(f"ubuf{i}", [128, 4, 30 + WT], BF16) for i in range(2)]
        xl = [ar.alloc(f"xl{i}", [128, 4, 3 + WT], F32) for i in range(2)]
        xt = ar.alloc("xt", [128, 2, WT], F32)
        rstdx = xt[:, 0, :]
        gel = [ar.alloc(f"gel{i}", [128, 4, WT], BF16) for i in range(2)]
        ulb = ar.alloc("ulb", [128, 4, WT], BF16)
        Ta = ar.alloc("Ta", [128, 4, WT], F32)
        Tb = ar.alloc("Tb", [128, 4, WT], F32)
        Td = ar.alloc("Td", [128, 4, WT], F32)
        Te = ar.alloc("Te", [128, 4, WT], F32)
        ycat = ar.alloc("ycat", [128, 8, WT], BF16)
        hcar = ar.alloc("hcar", [128, 4], F32)
        gain = self.sm("mix_norm")[:, 8:16]
        cw = self.sm("conv_w")
        cb = self.sm("conv_b")
        lng = self.sm("conv_ln_g")
        lnb = self.sm("conv_ln_b")
        lw = self.sm("lru_conv_w")
        lbias = self.sm("lru_conv_b")
        ba = self.sm("lru_ba")
        bx = self.sm("lru_bx")
        c1 = self.clru[:, 0, :]
        c2 = self.clru[:, 1, :]
        win = self.dram["win_o_b"]
        wo = self.dram["wout_o_b"]
        diag = self.dram["diag_b"]
        self.sem("d_diag")
        for cc in range(4 if not getattr(self, "diag_prebuilt", False) else 0):
            for j in range(31):
                P.op("dve", lambda h, cc=cc, j=j: h.tensor_scalar(out=dsl[0][:, j, :], in0=self.ident_bf[:],
                                                                 scalar1=cw[:, j * 4 + cc:j * 4 + cc + 1], scalar2=None, op0=ALU.mult),
                     reads=["ident_bf", "small"], writes=["dsl0"])
            self.dma("sp", diag[cc], dsl[0][:].rearrange("p j m -> p (j m)"), reads=["dsl0"], writes=["diag_b"], sem="d_diag")
        tiles = [(0, 16)] + [(16 + 256 * i, 256) for i in range(16)]
        tiles = tiles[:self.cfg.get("mix_tiles", len(tiles))]
        loads = []
        for _ in tiles:
            for oc0 in (8, 10, 4, 0, 6, 2, 12, 14):
                loads.append(lambda slot, oc0=oc0: (slot[:, 0:2048].rearrange("p (o x) -> p o x", o=2),
                                                    win[oc0:oc0 + 2].rearrange("o p x -> p o x"), "win_o_b"))
        ws = self.wstream(loads, wsl, ["wsl0", "wsl1"], ["dw0", "dw1"])
        oloads = []
        for _ in tiles:
            for e in range(8):
                oloads.append(lambda slot, e=e: (slot[:, 0:1024].rearrange("p (c m) -> p c m", c=8),
                                                 wo[e // 2].rearrange("p (c m) -> p c m", c=8)[:, :, 128 * (e % 2):128 * (e % 2) + 128],
                                                 "wout_o_b"))
        wsO = self.wstream(oloads, wso, ["wso0", "wso1"], ["dwo0", "dwo1"])
        dloads = []
        for _ in tiles:
            for cc in range(4):
                dloads.append(lambda slot, cc=cc: (slot[:].rearrange("p j m -> p (j m)"), diag[cc], "diag_b"))
        dstream = self.wstream(dloads, dsl, ["dsl0", "dsl1"], ["dd0", "dd1"])
        P.op("pool", lambda h: h.memset(ubuf[0][:, :, 0:30], 0.0), writes=["ubuf0"])
        P.op("pool", lambda h: h.memset(xl[0][:, :, 0:3], 0.0), writes=["xl0"])
        P.op("pool", lambda h: h.memset(hcar[:], 0.0), writes=["hcar"])
        ISQ2 = 0.7071067811865476
        xbank = [0]

        def X(ti):
            s0, W = tiles[ti]
            ub, ubn = ubuf[ti % 2], ubuf[(ti + 1) % 2]
            ubk, ubnk = f"ubuf{ti % 2}", f"ubuf{(ti + 1) % 2}"
            xc, xcn = xl[ti % 2], xl[(ti + 1) % 2]
            xck, xcnk = f"xl{ti % 2}", f"xl{(ti + 1) % 2}"
            gl, glk = gel[ti % 2], f"gel{ti % 2}"
            self.rmsnorm(s0, W, 0, xn, "xn", ycat, "ycat", rstdx, "xt", ps[1], "ps1", gain)
            yield

            def fm_piece(dest_fn):
                s = self.wacquire(ws)
                wvw = wsl[s][:, 0:2048].rearrange("p (o k m) -> p o k m", o=2, k=8)
                bank = xbank[0] % 2
                xbank[0] += 1
                for i in range(2):
                    pap = ps[bank][:, i * 256:i * 256 + W]

                    def mm(h, i=i, pap=pap, wvw=wvw):
                        ins = None
                        for kc in range(8):
                            ins = h.matmul(pap, lhsT=wvw[:, i, kc, :], rhs=xn[:, kc, 0:W], start=(kc == 0), stop=(kc == 7))
                        return ins
                    P.op("pe", mm, reads=[f"wsl{s}", "xn"], writes=[f"ps{bank}"])
                    dest_fn(i, pap, f"ps{bank}")

            for pc in range(2):
                def dx(i, pap, pk, pc=pc):
                    cc = 2 * pc + i
                    P.op("act", lambda h: h.activation(out=xc[:, cc, 3:3 + W], in_=pap, func=AF.Copy), reads=[pk], writes=[xck])
                fm_piece(dx)
                yield
            P.op("pool", lambda h: h.tensor_copy(out=xcn[:, :, 0:3], in_=xc[:, :, W:W + 3]), reads=[xck], writes=[xcnk])
            for pc in range(2):
                def db(i, pap, pk):
                    P.op("act", lambda h: h.activation(out=xt[:, i, 0:W], in_=pap, func=AF.Sigmoid), reads=[pk], writes=["xt"])
                fm_piece(db)
                yield

                def da(i, pap, pk, pc=pc):
                    cc = 2 * pc + i
                    P.op("dve", lambda h: h.tensor_tensor(out=ub[:, cc, 30:30 + W], in0=xt[:, i, 0:W], in1=pap, op=ALU.mult),
                         reads=[pk, "xt"], writes=[ubk])
                fm_piece(da)
                yield
            P.op("pool", lambda h: h.tensor_copy(out=ubn[:, :, 0:30], in_=ub[:, :, W:W + 30]), reads=[ubk], writes=[ubnk])
            for pc in range(2):
                def dgf(i, pap, pk, pc=pc):
                    cc = 2 * pc + i
                    P.op("act", lambda h: h.activation(out=gl[:, cc, 0:W], in_=pap, func=AF.Gelu), reads=[pk], writes=[glk])
                fm_piece(dgf)
                yield

        def Y(ti):
            s0, W = tiles[ti]
            ub, ubk = ubuf[ti % 2], f"ubuf{ti % 2}"
            xc, xck = xl[ti % 2], f"xl{ti % 2}"
            gl, glk = gel[ti % 2], f"gel{ti % 2}"
            for cc in range(4):
                ds = self.wacquire(dstream)
                bank = 2 + (cc % 2)

                def mmc(h, cc=cc, ds=ds, bank=bank):
                    ins = None
                    for j in range(31):
                        ins = h.matmul(ps[bank][:, 0:W], lhsT=dsl[ds][:, j, :], rhs=ub[:, cc, j:j + W],
                                       start=(j == 0), stop=(j == 30))
                    return ins
                P.op("pe", mmc, reads=[f"dsl{ds}", ubk], writes=[f"ps{bank}"])
                P.op("act", lambda h, cc=cc, bank=bank: h.activation(out=Tb[:, cc, 0:W], in_=ps[bank][:, 0:W], func=AF.Identity,
                                                                     bias=cb[:, cc:cc + 1]),
                     reads=[f"ps{bank}", "small"], writes=["Tb"])
                P.op("dve", lambda h, cc=cc: h.tensor_scalar(out=Td[:, cc, 0:W], in0=xc[:, cc, 3:3 + W], scalar1=lw[:, 12 + cc:13 + cc],
                                                            scalar2=lbias[:, cc:cc + 1], op0=ALU.mult, op1=ALU.add),
                     reads=[xck, "small"], writes=["Td"])
                for j in range(3):
                    P.op("dve", lambda h, cc=cc, j=j: h.scalar_tensor_tensor(out=Td[:, cc, 0:W], in0=xc[:, cc, j:j + W],
                                                                            scalar=lw[:, 4 * j + cc:4 * j + cc + 1], in1=Td[:, cc, 0:W],
                                                                            op0=ALU.mult, op1=ALU.add),
                         reads=[xck, "small", "Td"], writes=["Td"])
                if cc % 2 == 1:
                    yield
            P.op("act", lambda h: h.activation(out=ulb[:, :, 0:W], in_=Td[:, :, 0:W], func=AF.Copy), reads=["Td"], writes=["ulb"])
            lnb_ = [4, 5, 2, 3]
            for cc in range(4):
                bank = lnb_[cc]
                P.op("pe", lambda h, cc=cc, bank=bank: h.matmul(ps[bank][:, 0:W], lhsT=self.onesf[:], rhs=Tb[:, cc, 0:W],
                                                                start=True, stop=True),
                     reads=["onesf", "Tb"], writes=[f"ps{bank}"])
            for cc in range(4):
                bank = lnb_[cc]
                P.op("dve", lambda h, cc=cc, bank=bank: h.tensor_tensor(out=Tb[:, cc, 0:W], in0=Tb[:, cc, 0:W], in1=ps[bank][:, 0:W],
                                                                        op=ALU.subtract),
                     reads=["Tb", f"ps{bank}"], writes=[f"Tbc{cc}"])
            yield
            for cc in range(4):
                P.op("act", lambda h, cc=cc: h.activation(out=Ta[:, cc, 0:W], in_=Tb[:, cc, 0:W], func=AF.Square),
                     reads=[f"Tbc{cc}"], writes=[f"Tac{cc}"] + (["Ta"] if cc == 0 else []))
            for cc in range(4):
                bank = lnb_[cc]
                P.op("pe", lambda h, cc=cc, bank=bank: h.matmul(ps[bank][:, 0:W], lhsT=self.onesf[:], rhs=Ta[:, cc, 0:W],
                                                                start=True, stop=True),
                     reads=["onesf", f"Tac{cc}"], writes=[f"ps{bank}"])
            for cc in range(4):
                bank = lnb_[cc]
                P.op("act", lambda h, cc=cc, bank=bank: h.activation(out=Ta[:, cc, 0:W], in_=ps[bank][:, 0:W], func=AF.Ln,
                                                                     bias=self.eps_t[:]),
                     reads=[f"ps{bank}", "eps_t", f"Tac{cc}"], writes=["Ta"])
            P.op("act", lambda h: h.activation(out=Ta[:, :, 0:W], in_=Ta[:, :, 0:W], func=AF.Exp, scale=-0.5), reads=["Ta"], writes=["Ta"])
            yield
            P.op("dve", lambda h: h.tensor_tensor(out=Tb[:, :, 0:W], in0=Tb[:, :, 0:W], in1=Ta[:, :, 0:W], op=ALU.mult),
                 reads=["Ta", "Tb", "Tbc0", "Tbc1", "Tbc2", "Tbc3"], writes=["Tb", "Tbc0", "Tbc1", "Tbc2", "Tbc3"])
            for cc in range(4):
                P.op("dve", lambda h, cc=cc: h.tensor_scalar(out=Tb[:, cc, 0:W], in0=Tb[:, cc, 0:W], scalar1=lng[:, cc:cc + 1],
                                                            scalar2=lnb[:, cc:cc + 1], op0=ALU.mult, op1=ALU.add),
                     reads=["Tb", "small"], writes=["Tb"])
            P.op("act", lambda h: h.activation(out=Ta[:, :, 0:W], in_=Tb[:, :, 0:W], func=AF.Sigmoid), reads=["Tb"], writes=["Ta"])
            P.op("dve", lambda h: h.tensor_tensor(out=ycat[:, 0:4, 0:W], in0=Tb[:, :, 0:W], in1=Ta[:, :, 0:W], op=ALU.mult),
                 reads=["Ta", "Tb"], writes=["ycat"])
            yield
            for cc in range(4):
                off = (cc % 2) * 256
                P.op("pe", lambda h, cc=cc, off=off: h.matmul(ps[4][:, off:off + W], lhsT=self.wa_bd[:, cc, :], rhs=ulb[:, cc, 0:W],
                                                              start=True, stop=True),
                     reads=["lru_wa_bd", "ulb"], writes=["ps4"])
                P.op("act", lambda h, cc=cc, off=off: h.activation(out=Ta[:, cc, 0:W], in_=ps[4][:, off:off + W], func=AF.Sigmoid,
                                                                   bias=ba[:, cc:cc + 1]),
                     reads=["ps4", "small"], writes=["Ta"])
                P.op("pe", lambda h, cc=cc, off=off: h.matmul(ps[5][:, off:off + W], lhsT=self.wx_bd[:, cc, :], rhs=ulb[:, cc, 0:W],
                                                              start=True, stop=True),
                     reads=["lru_wx_bd", "ulb"], writes=["ps5"])
                P.op("act", lambda h, cc=cc, off=off: h.activation(out=Tb[:, cc, 0:W], in_=ps[5][:, off:off + W], func=AF.Sigmoid,
                                                                   bias=bx[:, cc:cc + 1]),
                     reads=["ps5", "small"], writes=["Tb"])
            yield
            for cc in range(4):
                P.op("act", lambda h, cc=cc: h.activation(out=Te[:, cc, 0:W], in_=Ta[:, cc, 0:W], func=AF.Exp, scale=c1[:, cc:cc + 1]),
                     reads=["Ta", "clru"], writes=["Te"])
                P.op("act", lambda h, cc=cc: h.activation(out=Ta[:, cc, 0:W], in_=Ta[:, cc, 0:W], func=AF.Exp, scale=c2[:, cc:cc + 1]),
                     reads=["Ta", "clru"], writes=["Ta"])
            P.op("act", lambda h: h.activation(out=Ta[:, :, 0:W], in_=Ta[:, :, 0:W], func=AF.Ln, bias=self.one_t[:], scale=-1.0),
                 reads=["Ta", "one_t"], writes=["Ta"])
            P.op("act", lambda h: h.activation(out=Ta[:, :, 0:W], in_=Ta[:, :, 0:W], func=AF.Exp, scale=0.5), reads=["Ta"], writes=["Ta"])
            if ti == 0:
                P.op("dve", lambda h: h.memset(Ta[:, :, 0:1], 1.0), writes=["Ta"])
            P.op("dve", lambda h: h.tensor_tensor(out=Tb[:, :, 0:W], in0=Tb[:, :, 0:W], in1=Td[:, :, 0:W], op=ALU.mult),
                 reads=["Tb", "Td"], writes=["Tb"])
            P.op("dve", lambda h: h.tensor_tensor(out=Tb[:, :, 0:W], in0=Tb[:, :, 0:W], in1=Ta[:, :, 0:W], op=ALU.mult),
                 reads=["Tb", "Ta"], writes=["Tb"])
            yield
            for cc in range(4):
                P.op("dve", lambda h, cc=cc: h.tensor_tensor_scan(out=Td[:, cc, 0:W], data0=Te[:, cc, 0:W], data1=Tb[:, cc, 0:W],
                                                                 initial=hcar[:, cc:cc + 1], op0=ALU.mult, op1=ALU.add),
                     reads=["Te", "Tb", "hcar"], writes=["Td"])
            P.op("dve", lambda h: h.tensor_copy(out=hcar[:], in_=Td[:, :, W - 1]), reads=["Td"], writes=["hcar"])
            P.op("dve", lambda h: h.tensor_tensor(out=ycat[:, 4:8, 0:W], in0=gl[:, :, 0:W], in1=Td[:, :, 0:W], op=ALU.mult),
                 reads=[glk, "Td"], writes=["ycat"])
            yield
            hk = hkeys(s0, W)
            for dc in range(8):
                s = self.wacquire(wsO)
                wq = wso[s][:, 0:1024].rearrange("p (c m) -> p c m", c=8)
                bank = 6 + (dc % 2)

                def mm(h, bank=bank, wq=wq):
                    ins = None
                    for cc in range(8):
                        ins = h.matmul(ps[bank][:, 0:W], lhsT=wq[:, cc, :], rhs=ycat[:, cc, 0:W], start=(cc == 0), stop=(cc == 7))
                    return ins
                P.op("pe", mm, reads=[f"wso{s}", "ycat"], writes=[f"ps{bank}"])
                P.op("dve", lambda h, dc=dc, bank=bank: h.tensor_tensor(out=H[:, dc, s0:s0 + W], in0=ps[bank][:, 0:W],
                                                                        in1=H[:, dc, s0:s0 + W], op=ALU.add),
                     reads=[f"ps{bank}"] + hk, writes=hk)
                if dc % 2 == 1:
                    yield

        for _ in X(0):
            pass
        for ti in range(len(tiles)):
            self.interleave(Y(ti), X(ti + 1) if ti + 1 < len(tiles) else None)
        P.barrier()
        ar.release(m)


    def final_tile(self, subs, sq, sq_key, rstd, psn):
        for _ in self.final_tile_gen(subs, sq, sq_key, rstd, psn):
            pass

    def final_tile_gen(self, subs, sq, sq_key, rstd, psn):
        P = self.P
        H = self.H
        gain = self.sm("final_norm")
        for (s0, n) in subs:
            hk = hkeys(s0, n)
            P.op("pool", lambda h, s0=s0, n=n: h.tensor_tensor(out=sq[:, :, 0:n], in0=H[:, :, s0:s0 + n], in1=H[:, :, s0:s0 + n], op=ALU.mult),
                 reads=hk, writes=[sq_key])

            def mm(h, n=n):
                ins = None
                for kc in range(8):
                    ins = h.matmul(psn[:, 0:n], lhsT=self.ones_bf[:], rhs=sq[:, kc, 0:n],
                                   start=(kc == 0), stop=(kc == 7))
                return ins
            P.op("pe", mm, reads=[sq_key, "ones_bf"], writes=["ps6"])
            P.op("act", lambda h, n=n: h.activation(out=rstd[:, 0:n], in_=psn[:, 0:n], func=AF.Ln,
                                                    bias=self.eps_t[:], scale=1.0 / D),
                 reads=["ps6", "eps_t"], writes=["rstd"])
            P.op("act", lambda h, n=n: h.activation(out=rstd[:, 0:n], in_=rstd[:, 0:n], func=AF.Exp, scale=-0.5),
                 reads=["rstd"], writes=["rstd"])
            yield
            for kc in range(8):
                P.op("dve", lambda h, kc=kc, s0=s0, n=n: h.scalar_tensor_tensor(
                    out=H[:, kc, s0:s0 + n], in0=H[:, kc, s0:s0 + n], scalar=gain[:, kc:kc + 1],
                    in1=rstd[:, 0:n], op0=ALU.mult, op1=ALU.mult),
                    reads=hk + ["rstd", "small"], writes=hk)
                yield
            if s0 >= NM:
                self.out_events.append(self.dma("sp", self.dram["out"][:, :, s0 - NM:s0 - NM + n], H[:, :, s0:s0 + n],
                                                reads=hk, writes=["out"], sem="d_out"))
            yield

    def dump_H(self):
        self.issue_x(8)
        for c in range(8):
            self.out_events.append(self.dma("sp", self.dram["out"][:, c, :], self.H[:, c, NM:T],
                                            reads=[f"H{i}" for i in range(16)], writes=["out"], sem="d_out"))

    def build(self):
        cfg = self.cfg
        self.declare()
        self.alloc_fixed()
        self.out_events = []
        phases = cfg["phases"]
        order = []
        for ph in phases:
            if ph.startswith("ffn"):
                order += [f"ffa{ph[3]}{ph[4]}", f"ffb{ph[3]}{ph[4]}"]
            elif ph == "mix0":
                order += ["win_e", "wv_e", "wout_e"]
            elif ph == "mix1":
                order += ["win_o", "wout_o"]
        def packs(ph):
            if ph.startswith("ffn"):
                return [f"ffa{ph[3]}{ph[4]}", f"ffb{ph[3]}{ph[4]}"]
            return ["win_e", "wv_e", "wout_e"] if ph == "mix0" else ["win_o", "wout_o"]

        plans = {}
        for i, ph in enumerate(phases):
            for name in packs(ph):
                plans[name] = self.cast_plan(name, fine_first=(i == 0))
        self.pending_casts = []
        for name in packs(phases[0]) if phases else []:
            self.pending_casts += plans.pop(name)
        self.issue_casts(len(self.pending_casts))
        self.setup()
        mixer_setup_done = False
        for i, ph in enumerate(phases):
            last = (i == len(phases) - 1) and cfg.get("final", True)
            if not mixer_setup_done and (i >= 1 or not ph.startswith("ffn")):
                if "mix0" in phases or "mix1" in phases:
                    self.setup_even()
                if "mix1" in phases:
                    self.setup_odd()
                mixer_setup_done = True
            for name in packs(ph):
                if name in plans:
                    self.pending_casts += plans.pop(name)
            self.issue_casts(len(self.pending_casts))
            if ph.startswith("ffn") and mixer_setup_done and i + 1 < len(phases) and phases[i + 1] == "mix1":
                self.plan_diag()
            if ph.startswith("ffn"):
                for j in range(i + 1, len(phases)):
                    for name in packs(phases[j]):
                        if name in plans:
                            self.pending_casts += plans.pop(name)
                    if phases[j].startswith("ffn"):
                        break
                self.ffn(int(ph[3]), int(ph[4]), final=last)
                self.issue_casts(len(self.pending_casts))
            elif ph == "mix0":
                self.issue_x(8)
                self.mix_even()
            elif ph == "mix1":
                self.issue_x(8)
                self.mix_odd()
        if not cfg.get("final", True):
            self.dump_H()
        block = self.nc.Block()
        with block as blk:
            self.P.emit(blk, self.sems, final_waits=[self.out_events[-1]])
        return self.nc


FULL_PHASES = ["ffn01", "mix0", "ffn02", "ffn11", "mix1", "ffn12"]


def run(inputs, cfg):
    shared = _host_pack(inputs)
    x = np.asarray(inputs["x"], np.float32)
    bld = Builder(cfg)
    nc = bld.build()
    in_maps = []
    ncores = cfg.get("ncores", NCORES)
    for b in range(ncores):
        m = dict(shared)
        m["xin"] = np.ascontiguousarray(x[b].T.reshape(8, 128, SEQ).transpose(1, 0, 2))
        in_maps.append(m)
    if cfg.get("trace"):
        res = run_bass_kernel_spmd(nc, in_maps, core_ids=list(range(ncores)), trace=True)
        print("EXEC_TIME_NS", res.exec_time_ns)
    else:
        res = run_bass_kernel_spmd(nc, in_maps, core_ids=list(range(ncores)))
    outs = []
    for b in range(ncores):
        o = res.results[b]["out"]
        outs.append(o.transpose(2, 1, 0).reshape(SEQ, D))
    return np.ascontiguousarray(np.stack(outs, axis=0), dtype=np.float32)


def kernel(**inputs):
    return run(inputs, dict(phases=FULL_PHASES, final=True))
```
